# Optimizing a Trainium2 kernel written in Bass

```python
import jax, jax.numpy as jnp
from jax import lax
import numpy as np

D_MODEL = 1024
BATCH = 4
SEQ = 4096
DEPTH = 4

GRID_W = 64
CTX_LEN = 256
D_MIX = D_MODEL
D_A = D_MIX // 2
A_GROUPS = 4
A_GROUP_DIM = D_A // A_GROUPS
CHUNK = 128
D_B = D_MIX - D_A
POOL_WINDOWS = (2, 4, 8, 16)
B_GROUPS = len(POOL_WINDOWS)
B_GROUP_DIM = D_B // B_GROUPS
N_HEADS_NA = 16
HEAD_DIM = D_MODEL // N_HEADS_NA
NA_ROWS = 8
NA_COLS = 16
D_FF = ((8 * D_MODEL + 3 * 256 - 1) // (3 * 256)) * 256
N_EVEN = (DEPTH + 1) // 2
N_ODD = DEPTH // 2
EPS = 1e-6
NEG_INF = -1e30
MOD_INIT = 0.5

kernel_name = "hybrid_dit_gmlp_pool_natten"


def rms_norm(x, g):
    xf = x.astype(jnp.float32)
    y = xf * lax.rsqrt(jnp.mean(xf * xf, axis=-1, keepdims=True) + EPS)
    return (y * g.astype(jnp.float32)).astype(x.dtype)


def layer_norm(x, g):
    xf = x.astype(jnp.float32)
    xc = xf - jnp.mean(xf, axis=-1, keepdims=True)
    y = xc * lax.rsqrt(jnp.mean(xc * xc, axis=-1, keepdims=True) + EPS)
    return (y * g.astype(jnp.float32)).astype(x.dtype)


def adaln(cond, w_mod, b_mod):
    m = jax.nn.silu(cond) @ w_mod + b_mod
    return jnp.split(m[..., None, :], 6, axis=-1)


def modulate(h, shift, scale):
    return h * (1 + scale) + shift


def chunk_gating(u, v, w_s, b_s):
    b, l, _ = v.shape
    vg = v.reshape(b, l // CHUNK, CHUNK, A_GROUPS, A_GROUP_DIM)
    mixed = jnp.einsum('gij,bnjgc->bnigc', w_s, vg) + b_s.T[:, :, None]
    return u * mixed.reshape(b, l, D_A)


def multiscale_pool(p, w_pool, pool_scale):
    b, l, _ = p.shape
    pf = p.astype(jnp.float32)
    cs = jnp.concatenate([jnp.zeros((b, 1, D_B), jnp.float32), jnp.cumsum(pf, axis=1)], axis=1)
    t = jnp.arange(l)
    pooled = []
    for g, w in enumerate(POOL_WINDOWS):
        lo = jnp.clip(t - w // 2, 0, l)
        hi = jnp.clip(t + w // 2, 0, l)
        cs_g = cs[..., g * B_GROUP_DIM:(g + 1) * B_GROUP_DIM]
        seg = jnp.take(cs_g, hi, axis=1) - jnp.take(cs_g, lo, axis=1)
        pooled.append(seg / (hi - lo).astype(jnp.float32)[None, :, None])
    pooled = jnp.stack(pooled, axis=2)
    diff = (pooled - pf.reshape(b, l, B_GROUPS, B_GROUP_DIM)).astype(p.dtype)
    y = jnp.einsum('blgc,gcd->blgd', diff, w_pool).reshape(b, l, D_B)
    return y * pool_scale


def ab_mixer(h, w_in, ln_v, w_s, b_s, w_pool, pool_scale, w_out):
    z = h @ w_in
    za = jax.nn.gelu(z[..., :2 * D_A])
    u, v = za[..., :D_A], layer_norm(za[..., D_A:], ln_v)
    ya = chunk_gating(u, v, w_s, b_s)
    yb = multiscale_pool(z[..., 2 * D_A:], w_pool, pool_scale)
    return jnp.concatenate([ya, yb], axis=-1) @ w_out


def na_mixer(h, hc, w_qkv, rpb, w_out, ctx_out):
    b, l, _ = h.shape
    rows = l // GRID_W
    kr = min(NA_ROWS, rows)
    scale = HEAD_DIM ** -0.5
    q, k, v = jnp.split(h @ w_qkv, 3, axis=-1)
    q = q.reshape(b, rows, GRID_W, N_HEADS_NA, HEAD_DIM)
    k = k.reshape(b, rows, GRID_W, N_HEADS_NA, HEAD_DIM)
    v = v.reshape(b, rows, GRID_W, N_HEADS_NA, HEAD_DIM)
    cl = hc.shape[1]
    qc, kc, vc = jnp.split(hc @ w_qkv, 3, axis=-1)
    kc = kc.reshape(b, cl, N_HEADS_NA, HEAD_DIM)
    vc = vc.reshape(b, cl, N_HEADS_NA, HEAD_DIM)
    r = jnp.arange(rows)
    row_start = jnp.clip(r - kr // 2, 0, rows - kr)
    row_idx = row_start[:, None] + jnp.arange(kr)
    k_rows = k[:, row_idx]
    v_rows = v[:, row_idx]
    s_nb = jnp.einsum('brqhd,brkwhd->bhrqkw', q, k_rows, preferred_element_type=jnp.float32) * scale
    col = jnp.arange(GRID_W)
    col_start = jnp.clip(col - NA_COLS // 2, 0, GRID_W - NA_COLS)
    col_mask = (col[None, :] >= col_start[:, None]) & (col[None, :] < col_start[:, None] + NA_COLS)
    dr = row_idx - r[:, None] + (NA_ROWS - 1)
    dc = jnp.clip(col[None, :] - col[:, None], -(NA_COLS - 1), NA_COLS - 1) + (NA_COLS - 1)
    bias = rpb[:, dr[:, None, :, None], dc[None, :, None, :]].astype(jnp.float32)
    s_nb = jnp.where(col_mask[None, None, None, :, None, :], s_nb + bias[None], NEG_INF)
    n_nb = kr * GRID_W
    s_nb = s_nb.reshape(b, N_HEADS_NA, rows, GRID_W, n_nb)
    s_cx = jnp.einsum('brqhd,bkhd->bhrqk', q, kc, preferred_element_type=jnp.float32) * scale
    p = jax.nn.softmax(jnp.concatenate([s_nb, s_cx], axis=-1), axis=-1)
    p_nb = p[..., :n_nb].reshape(b, N_HEADS_NA, rows, GRID_W, kr, GRID_W).astype(v.dtype)
    p_cx = p[..., n_nb:].astype(v.dtype)
    o = (jnp.einsum('bhrqkw,brkwhd->brqhd', p_nb, v_rows)
         + jnp.einsum('bhrqk,bkhd->brqhd', p_cx, vc))
    y = o.reshape(b, l, D_MODEL) @ w_out
    if ctx_out:
        qc = qc.reshape(b, cl, N_HEADS_NA, HEAD_DIM)
        sc = jnp.einsum('bqhd,bkhd->bhqk', qc, kc, preferred_element_type=jnp.float32) * scale
        pc = jax.nn.softmax(sc, axis=-1).astype(vc.dtype)
        oc = jnp.einsum('bhqk,bkhd->bqhd', pc, vc)
        yc = oc.reshape(b, cl, D_MODEL) @ w_out
    else:
        yc = None
    return y, yc


def swiglu(h, w_in, w_out):
    a, g = jnp.split(h @ w_in, 2, axis=-1)
    return (jax.nn.silu(a) * g) @ w_out


def setup_inputs(seed: int = 0) -> dict:
    key = jax.random.key(seed)
    ks = jax.random.split(key, 24)
    nrm = jax.random.normal
    f32 = jnp.float32
    return {
        "x": nrm(ks[0], (BATCH, SEQ, D_MODEL), f32),
        "c": nrm(ks[1], (BATCH, D_MODEL), f32),
        "ctx": nrm(ks[2], (BATCH, CTX_LEN, D_MODEL), f32),
        "c_ctx": nrm(ks[3], (D_MODEL,), f32),
        "w_mod": nrm(ks[4], (DEPTH, D_MODEL, 6 * D_MODEL), f32) * (MOD_INIT * D_MODEL ** -0.5),
        "b_mod": nrm(ks[5], (DEPTH, 6 * D_MODEL), f32) * 0.02,
        "norm_mix": 1.0 + 0.05 * nrm(ks[6], (DEPTH, D_MODEL), f32),
        "norm_ffn": 1.0 + 0.05 * nrm(ks[7], (DEPTH, D_MODEL), f32),
        "w_in_ab": nrm(ks[8], (N_EVEN, D_MODEL, 2 * D_A + D_B), f32) * D_MODEL ** -0.5,
        "ln_v": 1.0 + 0.05 * nrm(ks[9], (N_EVEN, D_A), f32),
        "w_spatial": nrm(ks[10], (N_EVEN, A_GROUPS, CHUNK, CHUNK), f32) * CHUNK ** -0.5,
        "b_spatial": 1.0 + 0.05 * nrm(ks[11], (N_EVEN, A_GROUPS, CHUNK), f32),
        "w_pool": nrm(ks[12], (N_EVEN, B_GROUPS, B_GROUP_DIM, B_GROUP_DIM), f32) * B_GROUP_DIM ** -0.5,
        "pool_scale": 1.0 + 0.05 * nrm(ks[13], (N_EVEN, D_B), f32),
        "w_out_ab": nrm(ks[14], (N_EVEN, D_MIX, D_MODEL), f32) * D_MIX ** -0.5,
        "w_qkv": nrm(ks[15], (N_ODD, D_MODEL, 3 * D_MODEL), f32) * D_MODEL ** -0.5,
        "rpb": nrm(ks[16], (N_ODD, N_HEADS_NA, 2 * NA_ROWS - 1, 2 * NA_COLS - 1), f32) * 0.1,
        "w_out_na": nrm(ks[17], (N_ODD, D_MODEL, D_MODEL), f32) * D_MODEL ** -0.5,
        "w_ffn_in": nrm(ks[18], (DEPTH, D_MODEL, 2 * D_FF), f32) * D_MODEL ** -0.5,
        "w_ffn_out": nrm(ks[19], (DEPTH, D_FF, D_MODEL), f32) * D_FF ** -0.5,
        "norm_final": 1.0 + 0.05 * nrm(ks[20], (D_MODEL,), f32),
    }


def reference(x, c, ctx, c_ctx, w_mod, b_mod, norm_mix, norm_ffn, w_in_ab, ln_v, w_spatial,
              b_spatial, w_pool, pool_scale, w_out_ab, w_qkv, rpb, w_out_na, w_ffn_in,
              w_ffn_out, norm_final):
    xc = ctx
    for i in range(DEPTH):
        last = i == DEPTH - 1
        odd = i % 2 == 1
        j = i // 2
        sh1, sc1, g1, sh2, sc2, g2 = adaln(c, w_mod[i], b_mod[i])
        h = modulate(rms_norm(x, norm_mix[i]), sh1, sc1)
        need_ctx = (not last) or odd
        if need_ctx:
            csh1, csc1, cg1, csh2, csc2, cg2 = adaln(c_ctx, w_mod[i], b_mod[i])
            hc = modulate(rms_norm(xc, norm_mix[i]), csh1, csc1)
        if odd:
            y, yc = na_mixer(h, hc, w_qkv[j], rpb[j], w_out_na[j], not last)
            x = x + g1 * y
            if not last:
                xc = xc + cg1 * yc
        else:
            ab = (w_in_ab[j], ln_v[j], w_spatial[j], b_spatial[j], w_pool[j], pool_scale[j], w_out_ab[j])
            x = x + g1 * ab_mixer(h, *ab)
            if not last:
                xc = xc + cg1 * ab_mixer(hc, *ab)
        x = x + g2 * swiglu(modulate(rms_norm(x, norm_ffn[i]), sh2, sc2), w_ffn_in[i], w_ffn_out[i])
        if not last:
            xc = xc + cg2 * swiglu(modulate(rms_norm(xc, norm_ffn[i]), csh2, csc2), w_ffn_in[i], w_ffn_out[i])
    return rms_norm(x, norm_final)
```

```python
import contextlib
import numpy as np
import concourse.bass as bass
import concourse.mybir as mybir

F32 = mybir.dt.float32
BF16 = mybir.dt.bfloat16
AF = mybir.ActivationFunctionType
ALU = mybir.AluOpType
AX = mybir.AxisListType

NLANES = 24
NHW = 10
GRAN = 64
_DTSIZE = {F32: 4, BF16: 2}


class Sched:
    def __init__(self, nc, sb_bytes, ps_bytes=16384):
        self.nc = nc
        self.es = contextlib.ExitStack()
        self.eng = {"pe": nc.tensor, "act": nc.scalar, "dve": nc.vector,
                    "pool": nc.gpsimd, "sp": nc.sync}
        names = ["pe", "act", "dve", "pool"] + [f"ln{i}" for i in range(NLANES)]
        self.names = names
        self.idx = {n: i for i, n in enumerate(names)}
        self.NE = len(names)
        self.sems = [self.es.enter_context(nc.semaphore("s_" + n)) for n in names]
        self.cnt = np.zeros(self.NE, np.int64)
        self.seen = {e: np.zeros(self.NE, np.int64) for e in self.eng}
        self.snap = {}
        self.seen["pe"][self.idx["pe"]] = 1 << 60
        self.lane_rr = 0
        self.lane_rr_sw = 0
        self.sb = self.es.enter_context(nc.sbuf_tensor("arena_sb", [128, sb_bytes // 4], F32))
        self.ps = self.es.enter_context(nc.psum_tensor("arena_ps", [128, ps_bytes // 4], F32))
        self.trk = {
            "arena_sb": (np.zeros((sb_bytes // GRAN + 1, self.NE), np.int64),
                         np.zeros((sb_bytes // GRAN + 1, self.NE), np.int64)),
            "arena_ps": (np.zeros((ps_bytes // GRAN + 1, self.NE), np.int64),
                         np.zeros((ps_bytes // GRAN + 1, self.NE), np.int64)),
        }
        self.sb_top = 0
        self.sb_bytes = sb_bytes
        self.blk_cache = {}
        self.n_wait = 0
        self.n_ins = 0

    def alloc(self, nbytes, align=64):
        off = (self.sb_top + align - 1) // align * align
        assert off + nbytes <= self.sb_bytes, f"SBUF overflow {off + nbytes} > {self.sb_bytes}"
        self.sb_top = off + nbytes
        return off

    def view(self, off, shape, dt):
        n = int(np.prod(shape))
        nb = n * _DTSIZE[dt]
        assert off % 4 == 0 and nb % 4 == 0
        v = self.sb[:, off // 4:(off + nb) // 4]
        if dt != F32:
            v = v.bitcast(dt)
        if len(shape) == 2:
            v = v.rearrange("p (a b) -> p a b", a=shape[0])
        elif len(shape) == 3:
            v = v.rearrange("p (a b c) -> p a b c", a=shape[0], b=shape[1])
        return v

    def new(self, shape, dt):
        n = int(np.prod(shape)) * _DTSIZE[dt]
        n = (n + 3) // 4 * 4
        off = self.alloc(n)
        return self.view(off, shape, dt)

    def psum(self, bank, cols=512, dt=F32, col0=0):
        v = self.ps[:, bank * 512 + col0: bank * 512 + col0 + cols]
        return v

    def _blocks(self, ap):
        name = ap.tensor.name
        if name not in self.trk:
            return None, None
        key = (name, ap.offset, ap.ap, ap.dtype)
        r = self.blk_cache.get(key)
        if r is None:
            es = _DTSIZE[ap.dtype]
            dims = ap.ap
            pstride = dims[0][0]
            off = (ap.offset % pstride) * es
            inner = [(s * es, c) for s, c in dims[1:]]
            starts = np.array([off], np.int64)
            run = es
            if inner:
                s_last, c_last = inner[-1]
                if s_last == es:
                    run = es * c_last
                    inner = inner[:-1]
                for s, c in inner:
                    starts = (starts[:, None] + (np.arange(c, dtype=np.int64) * s)[None, :]).ravel()
            lo = starts // GRAN
            hi = (starts + run - 1) // GRAN
            if len(starts) == 1:
                r = np.arange(lo[0], hi[0] + 1)
            else:
                r = np.unique(np.concatenate([np.arange(a, b + 1) for a, b in zip(lo, hi)]))
            self.blk_cache[key] = r
        return self.trk[name], r

    def _need(self, reads, writes):
        need = np.zeros(self.NE, np.int64)
        for ap in reads:
            t, b = self._blocks(ap)
            if t is not None:
                np.maximum(need, t[0][b].max(0), out=need)
        for ap in writes:
            t, b = self._blocks(ap)
            if t is not None:
                np.maximum(need, t[0][b].max(0), out=need)
                np.maximum(need, t[1][b].max(0), out=need)
        return need

    def _do_waits(self, e, need):
        seen = self.seen[e]
        eng = self.eng[e]
        for j in np.argsort(-(need - seen)):
            if need[j] > seen[j]:
                eng.wait_ge(self.sems[j], int(need[j]))
                self.n_wait += 1
                seen[j] = need[j]
                sn = self.snap.get((int(j), int(need[j])))
                if sn is not None:
                    np.maximum(seen, sn, out=seen)

    def _record(self, ei, val, reads, writes):
        for ap in reads:
            t, b = self._blocks(ap)
            if t is not None:
                t[1][b, ei] = val
        for ap in writes:
            t, b = self._blocks(ap)
            if t is not None:
                t[0][b, ei] = val

    def op(self, e, fn, reads, writes, inc=True):
        ei = self.idx[e]
        need = self._need(reads, writes)
        self._do_waits(e, need)
        ins = fn()
        self.n_ins += 1
        val = int(self.cnt[ei]) + 1
        self._record(ei, val, reads, writes)
        if inc:
            ins.then_inc(self.sems[ei], 1)
            self.cnt[ei] = val
            sn = self.seen[e].copy()
            sn[ei] = val
            self.snap[(ei, val)] = sn
        return ins

    def dma(self, q, out, in_):
        if q == "pool":
            lane = 4 + NHW + self.lane_rr_sw
            self.lane_rr_sw = (self.lane_rr_sw + 1) % (NLANES - NHW)
        else:
            lane = 4 + self.lane_rr
            self.lane_rr = (self.lane_rr + 1) % NHW
        need = self._need([in_], [out])
        need[lane] = max(need[lane], self.cnt[lane])
        self._do_waits(q, need)
        val = int(self.cnt[lane]) + 16
        ins = self.eng[q].dma_start(out=out, in_=in_)
        ins.then_inc(self.sems[lane], 16)
        self.n_ins += 1
        self.cnt[lane] = val
        self._record(lane, val, [in_], [out])
        self.snap[(lane, val)] = self.seen[q].copy()
        return (lane, val)

    def wait_all_dma(self, e="sp"):
        need = np.zeros(self.NE, np.int64)
        need[4:] = self.cnt[4:]
        self._do_waits(e, need)

    def mm(self, out, lhsT, rhs, start, stop, inc=None):
        return self.op("pe", lambda: self.nc.tensor.matmul(out, lhsT, rhs, start=start, stop=stop),
                       [lhsT, rhs], [out], inc=(stop if inc is None else inc))

    def transpose(self, out, in_, ident):
        return self.op("pe", lambda: self.nc.tensor.transpose(out, in_, ident), [in_, ident], [out])

    def act(self, out, in_, func, bias=None, scale=None, accum_out=None, e="act"):
        kw = {}
        reads = [in_]
        writes = [out]
        if bias is not None:
            kw["bias"] = bias
            if not isinstance(bias, (int, float)):
                reads.append(bias)
        if scale is not None:
            kw["scale"] = scale
            if not isinstance(scale, (int, float)):
                reads.append(scale)
        if accum_out is not None:
            kw["accum_out"] = accum_out
            writes.append(accum_out)
        return self.op("act", lambda: self.nc.scalar.activation(out, in_, func, **kw), reads, writes)

    def tt(self, e, out, in0, in1, op):
        return self.op(e, lambda: self.eng[e].tensor_tensor(out, in0, in1, op), [in0, in1], [out])

    def ts(self, e, out, in0, s1, s2, op0, op1=None, accum_out=None):
        reads = [in0] + [s for s in (s1, s2) if s is not None and not isinstance(s, (int, float))]
        writes = [out] + ([accum_out] if accum_out is not None else [])
        kw = {}
        if op1 is not None:
            kw["op1"] = op1
        if accum_out is not None:
            kw["accum_out"] = accum_out
        return self.op(e, lambda: self.eng[e].tensor_scalar(out, in0, s1, s2, op0, **kw), reads, writes)

    def stt(self, e, out, in0, scalar, in1, op0, op1):
        reads = [in0, in1] + ([] if isinstance(scalar, (int, float)) else [scalar])
        return self.op(e, lambda: self.eng[e].scalar_tensor_tensor(out, in0, scalar, in1, op0, op1),
                       reads, [out])

    def copy(self, e, out, in_):
        if e == "act":
            return self.op(e, lambda: self.nc.scalar.copy(out, in_), [in_], [out])
        return self.op(e, lambda: self.eng[e].tensor_copy(out, in_), [in_], [out])

    def memset(self, e, out, val):
        return self.op(e, lambda: self.eng[e].memset(out, val), [], [out])

    def recip(self, out, in_):
        return self.op("dve", lambda: self.nc.vector.reciprocal(out, in_), [in_], [out])


import os
from concourse.bass_utils import run_bass_kernel_spmd

D = 1024
NTOK = 2816
NOUT = 2048
NCTX = 256
EPS = 1e-6
DFF = 2816
SB_BYTES = 212000
NRING = 3
DEBUG_STOP = os.environ.get("MK_STOP", "")

AB_OUT = {0: 2688, 2: 2304}
AB_IN = {0: 2816, 2: 2432}
NA_PAIRS = {1: 19, 3: 16}
NA_NCH = {1: 21, 3: 18}
FFN_N = {0: 2688, 1: 2432, 2: 2304, 3: 2048}


def _split(n, parts):
    base, rem = divmod(n, parts)
    out = []
    s = 0
    for i in range(parts):
        e = s + base + (1 if i < rem else 0)
        out.append((s, e))
        s = e
    return out


def _tiles(n, w=512):
    return [(t, min(w, n - t)) for t in range(0, n, w)]


class Prog:
    def __init__(self):
        nc = bass.Bass("TRN2", target_bir_lowering=False)
        self.nc = nc
        dt = lambda name, shape, kind="ExternalInput": nc.dram_tensor(name, list(shape), F32, kind=kind).ap()
        self.d_x = dt("xT", [8, 128, NTOK])
        self.d_c = dt("cT", [8, 128, NCTX])
        self.d_cc = dt("cc", [128, 8, 2])
        self.d_wmod = dt("wmod", [4, 12, 128, 8, 512])
        self.d_bmod = dt("bmod", [128, 4, 48, 2])
        self.d_nmix = dt("nmix", [128, 4, 8, 2])
        self.d_nffn = dt("nffn", [128, 4, 8, 2])
        self.d_nfin = dt("nfin", [128, 8])
        self.d_ident = dt("ident", [128, 128])
        self.d_winab = dt("winab", [2, 3, 128, 8, 512])
        self.d_woutab = dt("woutab", [2, 2, 128, 8, 512])
        self.d_wsT = dt("wsT", [2, 128, 4, 128])
        self.d_bsB = dt("bsB", [2, 128, 4, 128])
        self.d_lnvB = dt("lnvB", [2, 128, 512])
        self.d_wpool = dt("wpool", [2, 128, 4, 128])
        self.d_pscale = dt("pscale", [128, 2, 4])
        self.d_invc = dt("invc", [128, 3, 4, 8])
        self.d_alpha = dt("alpha", [128, 1])
        self.d_wqkv = dt("wqkv", [2, 8, 128, 8, 384])
        self.d_wona = dt("wona", [2, 2, 128, 8, 512])
        self.d_btab = dt("btab", [2, 16, 128, 15, 128])
        self.d_mtab = dt("mtab", [128, 15, 128])
        self.d_wfin = dt("wfin", [4, 22, 128, 8, 256])
        self.d_wfout = dt("wfout", [4, 22, 128, 1024])
        self.d_out = dt("out", [8, 128, NOUT], kind="ExternalOutput")

        S = Sched(nc, SB_BYTES)
        self.S = S
        self.pbc = 0
        self.ringc = 0
        self.bg = []
        self.X = S.new([8, NTOK], F32)
        self.XC = S.new([8, NCTX], F32)
        self.RING = [S.new([4096], BF16) for _ in range(NRING)]
        self.MOD = S.new([4, 48, 2], F32)
        self.A1 = S.new([4, 8, 2], F32)
        self.A2 = S.new([4, 8, 2], F32)
        self.ONES = S.new([128], BF16)
        self.IDENT = S.new([128], BF16)
        self.NFIN = S.new([8], F32)
        self.ALPHA = S.new([1], F32)
        self.INVC = S.new([3, 4, 8], F32)
        self.PSC = S.new([2, 4], F32)
        self.MTAB = S.new([15, 128], BF16)
        self.WST = S.new([4, 128], BF16)
        self.BSB = S.new([512], F32)
        self.LNVB = S.new([512], F32)
        self.WPOOL = S.new([4, 128], BF16)
        self.HPREV = S.new([8, 256], BF16)
        self.SCR0 = S.sb_top

    def pb(self, cols=512):
        b = self.pbc % 7
        self.pbc += 1
        return self.S.ps[:, b * 512: b * 512 + cols]

    def pb2(self, cols):
        while (self.pbc % 7) not in (0, 2, 4):
            self.pbc += 1
        b = self.pbc % 7
        self.pbc += 2
        return self.S.ps[:, b * 512: b * 512 + cols]

    def bg_step(self, n=1):
        for _ in range(n):
            if self.bg:
                self.bg[0]()
                self.bg.pop(0)

    def bg_flush(self):
        while self.bg:
            self.bg_step()

    def ring(self, bg=False):
        if bg:
            return self.RING[NRING - 1]
        nr = NRING - 1 if self.bg else NRING
        r = self.RING[self.ringc % nr]
        self.ringc += 1
        return r

    def load_w(self, src, k, n, bg=False):
        slot = self.ring(bg)[:, 0:k * n].rearrange("p (k n) -> p k n", k=k)
        self.S.dma("pool", slot, src)
        return slot

    def norm_tmp(self):
        S = self.S
        self.SQ = [S.new([512], BF16) for _ in range(2)]
        self.RS = S.new([512], F32)
        self.TMPN = [S.new([512], F32) for _ in range(2)]

    def norm_h(self, A, B, j, src, dst, n):
        S = self.S
        for (t0, w) in _tiles(n):
            ps = self.pb()[:, :w]
            for k in range(8):
                sq = self.SQ[k % 2][:, :w]
                S.act(sq, src[:, k, t0:t0 + w], AF.Square)
                S.mm(ps, self.ONES, sq, k == 0, k == 7, inc=True)
            r = self.RS[:, :w]
            S.ts("dve", r, ps, 1024.0 * EPS, None, ALU.add)
            S.act(r, r, AF.Sqrt)
            S.recip(r, r)
            for k in range(8):
                tmp = self.TMPN[k % 2][:, :w]
                S.tt("dve", tmp, src[:, k, t0:t0 + w], r, ALU.mult)
                if B is None:
                    S.act(dst[:, k, t0:t0 + w], tmp, AF.Identity, scale=A[:, k, j:j + 1])
                else:
                    S.act(dst[:, k, t0:t0 + w], tmp, AF.Identity, bias=B[:, k, j:j + 1], scale=A[:, k, j:j + 1])

    def prologue(self):
        S = self.S
        nc = self.nc
        S.sb_top = self.SCR0
        for k in range(8):
            S.dma("sp", self.X[:, k, :], self.d_x[k])
            S.dma("sp", self.XC[:, k, :], self.d_c[k])
        ccs = S.new([8, 2], F32)
        csil = S.new([8, 2], BF16)
        bmod = S.new([4, 48, 2], F32)
        nmix = S.new([4, 8, 2], F32)
        nffn = S.new([4, 8, 2], F32)
        onesf = S.new([128], F32)
        S.dma("sp", ccs, self.d_cc)
        S.dma("sp", bmod, self.d_bmod)
        S.dma("sp", nmix, self.d_nmix)
        S.dma("sp", nffn, self.d_nffn)
        S.dma("sp", self.NFIN, self.d_nfin)
        S.dma("sp", self.ALPHA, self.d_alpha)
        S.dma("sp", self.INVC, self.d_invc)
        S.dma("sp", self.PSC, self.d_pscale)
        S.dma("pool", self.IDENT, self.d_ident)
        S.dma("pool", self.MTAB, self.d_mtab)
        S.memset("dve", onesf, 1.0)
        S.copy("dve", self.ONES, onesf)
        S.act(csil, ccs, AF.Silu)
        self.p_csil, self.p_bmod, self.p_nmix, self.p_nffn = csil, bmod, nmix, nffn
        self.SCR0 = S.sb_top
        for t in self.mod_tasks(0, bg=False):
            t()
        S.ts("dve", self.NFIN, self.NFIN, 32.0, None, ALU.mult)

    def mod_tasks(self, l, bg=True):
        S = self.S
        ps = S.ps[:, 7 * 512:7 * 512 + 96]
        csil, bmod = self.p_csil, self.p_bmod

        def piece(n):
            def f():
                slot = self.load_w(self.d_wmod[l, n], 8, 512, bg=bg)
                for mi in range(4):
                    m = n * 4 + mi
                    for k in range(8):
                        S.mm(ps[:, m * 2:(m + 1) * 2], slot[:, k, mi * 128:(mi + 1) * 128], csil[:, k, :], k == 0, k == 7)
            return f

        def fin():
            S.tt("dve", self.MOD[:, l].rearrange("p a b -> p (a b)"), ps, bmod[:, l].rearrange("p a b -> p (a b)"), ALU.add)
            for (Adst, gain, c0) in ((self.A1, self.p_nmix, 8), (self.A2, self.p_nffn, 32)):
                S.stt("dve", Adst[:, l], self.MOD[:, l, c0:c0 + 8, :], 1.0, gain[:, l], ALU.add, ALU.mult)
                S.ts("dve", Adst[:, l], Adst[:, l], 32.0, None, ALU.mult)
        return [piece(n) for n in range(12)] + [fin]

    def ffn(self, l, segs):
        S = self.S
        S.sb_top = self.SCR0
        self.norm_tmp()
        ntot = sum(s[1] for s in segs)
        H = S.new([8, ntot], BF16)
        HID = S.new([8, ntot], BF16)
        SA = [S.new([512], F32) for _ in range(2)]
        B2 = self.MOD[:, l, 24:32, :]
        G2 = self.MOD[:, l, 40:48, :]
        off = 0
        tl = []
        for (res, n, j) in segs:
            self.norm_h(self.A2[:, l], B2, j, res, H[:, :, off:off + n], n)
            for (t0, w) in _tiles(n):
                tl.append((off + t0, w, res, t0, j))
            off += n
        it = 0
        for (j0, j1) in ((0, 8), (8, 15), (15, 22)):
            for jh in range(j0, j1):
                slot = self.load_w(self.d_wfin[l, jh], 8, 256)
                jl = jh - j0
                for (c0, w, res, t0, j) in tl:
                    pa = self.pb()[:, :w]
                    pg = self.pb()[:, :w]
                    for k in range(8):
                        S.mm(pa, slot[:, k, 0:128], H[:, k, c0:c0 + w], k == 0, k == 7)
                    for k in range(8):
                        S.mm(pg, slot[:, k, 128:256], H[:, k, c0:c0 + w], k == 0, k == 7)
                    sa = SA[it % 2][:, :w]
                    it += 1
                    S.act(sa, pa, AF.Silu)
                    S.tt("dve", HID[:, jl, c0:c0 + w], sa, pg, ALU.mult)
            nb = j1 - j0
            wo = []
            for q0 in range(0, nb, 4):
                qn = min(4, nb - q0)
                slot = self.ring()[:, 0:qn * 1024].rearrange("p (k n) -> p k n", k=qn)
                S.dma("pool", slot, self.d_wfout[l, j0 + q0:j0 + q0 + qn].rearrange("j p n -> p j n"))
                for qi in range(qn):
                    wo.append(slot[:, qi, :])
            for m in range(8):
                for (c0, w, res, t0, j) in tl:
                    py = self.pb()[:, :w]
                    for jl in range(nb):
                        S.mm(py, wo[jl][:, m * 128:(m + 1) * 128], HID[:, jl, c0:c0 + w], jl == 0, jl == nb - 1)
                    rr = res[:, m, t0:t0 + w]
                    S.stt("dve", rr, py, G2[:, m, j:j + 1], rr, ALU.mult, ALU.add)

    def ab_tables(self, e):
        S = self.S
        S.dma("pool", self.WST, self.d_wsT[e])
        S.dma("sp", self.BSB.rearrange("p (a b) -> p a b", a=4), self.d_bsB[e])
        S.dma("sp", self.LNVB, self.d_lnvB[e])
        S.dma("pool", self.WPOOL, self.d_wpool[e])

    def ab_mixer(self, l, e, Xb, c0, c1, hs, he, j, fix_head, fix_tail):
        S = self.S
        nc = self.nc
        S.sb_top = self.SCR0
        self.norm_tmp()
        nh = he - hs
        n = c1 - c0
        o = c0 - hs
        H = S.new([8, nh], BF16)
        YA = S.new([4, n], BF16)
        YB = S.new([4, n], BF16)
        PB = [S.new([nh + 16], F32) for _ in range(2)]
        Aa = S.new([nh + 16], F32)
        Ab = S.new([nh + 16], F32)
        DF = S.new([n], BF16)
        VT = [S.new([512], F32) for _ in range(3)]
        CEN = [S.new([512], F32) for _ in range(3)]
        VH = [S.new([512], BF16) for _ in range(3)]
        SQJ = S.new([512], F32)
        SM = [S.new([4], F32) for _ in range(3)]
        T8 = S.new([8], F32)
        B1 = self.MOD[:, l, 0:8, :]
        G1 = self.MOD[:, l, 16:24, :]
        if o > 0:
            S.copy("act", H[:, :, 0:o], self.HPREV[:, :, 256 - o:256])
        self.norm_h(self.A1[:, l], B1, j, Xb[:, :, c0:he], H[:, :, o:nh], nh - o)
        if j == 0:
            S.copy("act", self.HPREV[:, :, 128:256], H[:, :, o + n - 128:o + n])
        slot_u = self.load_w(self.d_winab[e, 0], 8, 512)
        slot_p = self.load_w(self.d_winab[e, 2], 8, 512)

        def u_proj(m):
            for (t0, w) in _tiles(n):
                ps = self.pb()[:, :w]
                for k in range(8):
                    S.mm(ps, slot_u[:, k, m * 128:(m + 1) * 128], H[:, k, o + t0:o + t0 + w], k == 0, k == 7)
                S.act(YA[:, m, t0:t0 + w], ps, AF.Gelu_apprx_tanh)

        def p_proj(g):
            P = PB[g % 2]
            S.memset("dve", P[:, 0:8], 0.0)
            S.memset("dve", P[:, 8 + nh:16 + nh], 0.0)
            for (t0, w) in _tiles(nh):
                ps = self.pb()[:, :w]
                for k in range(8):
                    S.mm(ps, slot_p[:, k, g * 128:(g + 1) * 128], H[:, k, t0:t0 + w], k == 0, k == 7)
                S.copy("act", P[:, 8 + t0:8 + t0 + w], ps)

        def pooling(g):
            P = PB[g % 2]
            wd = 2 ** (g + 1)
            half = wd // 2
            cur = P
            length = nh + 16
            step = 1
            bufs = [Aa, Ab]
            bi = 0
            while step < wd:
                nxt = bufs[bi]
                bi ^= 1
                L2 = length - step
                S.tt("dve", nxt[:, 0:L2], cur[:, 0:L2], cur[:, step:step + L2], ALU.add)
                cur = nxt
                length = L2
                step *= 2
            i0 = o + 8
            Dd = bufs[bi][:, 0:n]
            fw_ = cur[:, i0 - half:i0 - half + n]
            rv_ = cur[:, i0 - half + 1:i0 - half + 1 + n]
            S.tt("dve", Dd, fw_, rv_, ALU.subtract)
            S.stt("dve", Dd, Dd, self.ALPHA[:, 0:1], rv_, ALU.mult, ALU.add)
            S.stt("dve", DF, Dd, 1.0 / wd, P[:, i0:i0 + n], ALU.mult, ALU.subtract)
            if fix_head is not None:
                S.tt("dve", T8, Dd[:, 0:8], self.INVC[:, fix_head, g, :], ALU.mult)
                S.tt("dve", DF[:, 0:8], T8, P[:, i0:i0 + 8], ALU.subtract)
            if fix_tail is not None:
                S.tt("dve", T8, Dd[:, n - 8:n], self.INVC[:, fix_tail, g, :], ALU.mult)
                S.tt("dve", DF[:, n - 8:n], T8, P[:, i0 + n - 8:i0 + n], ALU.subtract)

        def yb_proj(g):
            for (t0, w) in _tiles(n):
                ps = self.pb()[:, :w]
                S.mm(ps, self.WPOOL[:, g, :], DF[:, t0:t0 + w], True, True)
                S.act(YB[:, g, t0:t0 + w], ps, AF.Identity, scale=self.PSC[:, e, g:g + 1])

        p_proj(0)
        for g in range(4):
            if j == 0:
                self.bg_step()
            u_proj(g)
            if g < 3:
                p_proj(g + 1)
            pooling(g)
            yb_proj(g)

        slot_v = self.load_w(self.d_winab[e, 1], 8, 512)
        nchunk = n // 128

        def v_a(ci):
            tc = o + ci * 128
            ps = self.pb()
            for k in range(8):
                S.mm(ps, H[:, k, tc:tc + 128], slot_v[:, k, :], k == 0, k == 7)
            S.act(VT[ci % 3], ps, AF.Gelu_apprx_tanh)

        def v_b1(ci):
            vt = VT[ci % 3]
            cen = CEN[ci % 3]
            vh = VH[ci % 3]
            sm = SM[ci % 3]
            S.op("dve", lambda: nc.vector.reduce_sum(sm[:, 0:1], vt, AX.X), [vt], [sm[:, 0:1]])
            S.ts("dve", sm[:, 1:2], sm[:, 0:1], -1.0 / 512.0, None, ALU.mult)
            S.ts("dve", cen, vt, sm[:, 1:2], None, ALU.add)
            S.act(SQJ, cen, AF.Square)
            S.op("dve", lambda: nc.vector.reduce_sum(sm[:, 2:3], SQJ, AX.X), [SQJ], [sm[:, 2:3]])
            S.ts("dve", sm[:, 3:4], sm[:, 2:3], 1.0 / 512.0, EPS, ALU.mult, ALU.add)
            S.act(sm[:, 3:4], sm[:, 3:4], AF.Sqrt)
            S.recip(sm[:, 3:4], sm[:, 3:4])
            S.stt("dve", vh, cen, sm[:, 3:4], self.LNVB, ALU.mult, ALU.mult)

        def v_b2(ci):
            cen = CEN[ci % 3]
            vh = VH[ci % 3]
            psg = self.pb()
            for g in range(4):
                S.mm(psg[:, g * 128:(g + 1) * 128], vh[:, g * 128:(g + 1) * 128], self.WST[:, g, :], True, True)
            S.tt("dve", cen, psg, self.BSB, ALU.add)
            ya = YA[:, :, ci * 128:(ci + 1) * 128]
            S.tt("dve", ya, cen.rearrange("p (a b) -> p a b", a=4), ya, ALU.mult)

        for i in range(nchunk + 2):
            if i < nchunk:
                v_a(i)
            if 1 <= i <= nchunk:
                v_b1(i - 1)
            if i >= 2:
                v_b2(i - 2)
        for hf in range(2):
            slot = self.load_w(self.d_woutab[e, hf], 8, 512)
            for mi in range(4):
                m = hf * 4 + mi
                for (t0, w) in _tiles(n):
                    ps = self.pb()[:, :w]
                    for k in range(8):
                        rhs = (YA if k < 4 else YB)[:, k % 4, t0:t0 + w]
                        S.mm(ps, slot[:, k, mi * 128:(mi + 1) * 128], rhs, k == 0, k == 7)
                    rr = Xb[:, m, c0 + t0:c0 + t0 + w]
                    S.stt("dve", rr, ps, G1[:, m, j:j + 1], rr, ALU.mult, ALU.add)

    def na_mixer(self, l, o_, a0, a1, do_ctx_q):
        S = self.S
        nc = self.nc
        S.sb_top = self.SCR0
        self.norm_tmp()
        NCH = NA_NCH[l]
        cs = lambda a: min(max(a - 2, 0), NCH - 5)
        k0 = cs(a0)
        k1 = cs(a1 - 1) + 5
        nkc = k1 - k0
        nk = nkc * 128
        npair = a1 - a0
        nq = npair * 128
        qoff = (a0 - k0) * 128
        H = S.new([8, nk], BF16)
        HC = S.new([8, NCTX], BF16)
        OT = S.new([8, nq], BF16)
        OTC = S.new([8, NCTX], BF16) if do_ctx_q else None
        QT = S.new([nq], BF16)
        KT = S.new([nk], BF16)
        V1 = S.new([nkc, 2, 65], BF16)
        QC = S.new([NCTX], BF16)
        KC = S.new([NCTX], BF16)
        VC = S.new([2, 2, 65], BF16)
        OTOK = S.new([npair, 128], BF16)
        OTOKC = S.new([2, 128], BF16)
        PT = [S.new([7, 128], BF16) for _ in range(3)]
        E = [S.new([15, 128], BF16) for _ in range(2)]
        RINV = [S.new([1], F32) for _ in range(3)]
        B1 = self.MOD[:, l, 0:8, :]
        G1 = self.MOD[:, l, 16:24, :]
        lh = (a0 - k0) * 128
        if lh > 0:
            S.copy("act", H[:, :, 0:lh], self.HPREV[:, :, 256 - lh:256])
        self.norm_h(self.A1[:, l], B1, 0, self.X[:, :, a0 * 128:k1 * 128], H[:, :, lh:nk], nk - lh)
        S.copy("act", self.HPREV, H[:, :, (a1 - 2 - k0) * 128:(a1 - k0) * 128])
        self.norm_h(self.A1[:, l], B1, 1, self.XC, HC, NCTX)
        for c in range(nkc):
            S.memset("dve", V1[:, c, :, 64:65], 1.0)
        for c in range(2):
            S.memset("dve", VC[:, c, :, 64:65], 1.0)
        it = 0
        for hp in range(8):
            self.bg_step()
            slot = self.load_w(self.d_wqkv[o_, hp], 8, 384)
            for (t0, w) in _tiles(nq):
                ps = self.pb()[:, :w]
                for k in range(8):
                    S.mm(ps, slot[:, k, 0:128], H[:, k, qoff + t0:qoff + t0 + w], k == 0, k == 7)
                S.copy("act", QT[:, t0:t0 + w], ps)
            for (t0, w) in _tiles(nk):
                ps = self.pb()[:, :w]
                for k in range(8):
                    S.mm(ps, slot[:, k, 128:256], H[:, k, t0:t0 + w], k == 0, k == 7)
                S.copy("act", KT[:, t0:t0 + w], ps)
            for c in range(nkc):
                ps = self.pb()[:, :128]
                for k in range(8):
                    S.mm(ps, H[:, k, c * 128:(c + 1) * 128], slot[:, k, 256:384], k == 0, k == 7)
                S.copy("dve", V1[:, c, :, 0:64], ps.rearrange("p (a b) -> p a b", a=2))
            ps = self.pb()[:, :NCTX]
            for k in range(8):
                S.mm(ps, slot[:, k, 128:256], HC[:, k, :], k == 0, k == 7)
            S.copy("act", KC, ps)
            for c in range(2):
                ps = self.pb()[:, :128]
                for k in range(8):
                    S.mm(ps, HC[:, k, c * 128:(c + 1) * 128], slot[:, k, 256:384], k == 0, k == 7)
                S.copy("dve", VC[:, c, :, 0:64], ps.rearrange("p (a b) -> p a b", a=2))
            if do_ctx_q:
                ps = self.pb()[:, :NCTX]
                for k in range(8):
                    S.mm(ps, slot[:, k, 0:128], HC[:, k, :], k == 0, k == 7)
                S.copy("act", QC, ps)
            for hh in range(2):
                S.dma("pool", E[hh], self.d_btab[o_, hp * 2 + hh])
                S.act(E[hh], E[hh], AF.Exp)
                S.tt("dve", E[hh], E[hh], self.MTAB, ALU.mult)
            blocks = [("x", a) for a in range(a0, a1)]
            if do_ctx_q:
                blocks += [("c", 0), ("c", 1)]
            its = [(kind, a, hh) for (kind, a) in blocks for hh in range(2)]
            LA = 2

            def stage1(i):
                kind, a, hh = its[i]
                hsl = slice(hh * 64, hh * 64 + 64)
                pt = PT[i % 3]
                ptf = pt.rearrange("p a b -> p (a b)")
                if kind == "x":
                    typ = min(a, 2)
                    c_s = cs(a) - k0
                    q = QT[hsl, (a - a0) * 128:(a - a0 + 1) * 128]
                    pss = self.pb2(896)
                    for jj in range(5):
                        S.mm(pss[:, jj * 128:(jj + 1) * 128], KT[hsl, (c_s + jj) * 128:(c_s + jj + 1) * 128], q, True, True)
                    for jc in range(2):
                        S.mm(pss[:, (5 + jc) * 128:(6 + jc) * 128], KC[hsl, jc * 128:(jc + 1) * 128], q, True, True)
                    S.act(ptf[:, 0:512], pss[:, 0:512], AF.Exp, scale=0.125)
                    S.act(ptf[:, 512:896], pss[:, 512:896], AF.Exp, scale=0.125)
                    S.tt("dve", pt[:, 0:5, :], pt[:, 0:5, :], E[hh][:, typ * 5:typ * 5 + 5, :], ALU.mult)
                else:
                    q = QC[hsl, a * 128:(a + 1) * 128]
                    pss = self.pb()[:, 0:256]
                    for jc in range(2):
                        S.mm(pss[:, jc * 128:(jc + 1) * 128], KC[hsl, jc * 128:(jc + 1) * 128], q, True, True)
                    S.act(ptf[:, 0:256], pss, AF.Exp, scale=0.125)

            def stage2(i):
                kind, a, hh = its[i]
                pt = PT[i % 3]
                rinv = RINV[i % 3]
                pso = self.pb()[:, 0:65]
                if kind == "x":
                    c_s = cs(a) - k0
                    for jj in range(5):
                        S.mm(pso, pt[:, jj, :], V1[:, c_s + jj, hh, :], jj == 0, False)
                    for jc in range(2):
                        S.mm(pso, pt[:, 5 + jc, :], VC[:, jc, hh, :], False, jc == 1)
                    dst = OTOK[:, a - a0, hh * 64:hh * 64 + 64]
                else:
                    for jc in range(2):
                        S.mm(pso, pt[:, jc, :], VC[:, jc, hh, :], jc == 0, jc == 1)
                    dst = OTOKC[:, a, hh * 64:hh * 64 + 64]
                S.recip(rinv, pso[:, 64:65])
                S.act(dst, pso[:, 0:64], AF.Identity, scale=rinv[:, 0:1])

            for i in range(len(its) + LA):
                if i < len(its):
                    stage1(i)
                if i >= LA:
                    stage2(i - LA)
            for (kind, a) in blocks:
                pst = self.pb()[:, 0:64].bitcast(BF16)
                if kind == "x":
                    S.transpose(pst, OTOK[:, a - a0, :], self.IDENT)
                    S.copy("dve", OT[:, hp, (a - a0) * 128:(a - a0 + 1) * 128], pst)
                else:
                    S.transpose(pst, OTOKC[:, a, :], self.IDENT)
                    S.copy("dve", OTC[:, hp, a * 128:(a + 1) * 128], pst)
        q0 = a0 * 128
        for hf in range(2):
            slot = self.load_w(self.d_wona[o_, hf], 8, 512)
            for mi in range(4):
                m = hf * 4 + mi
                for (t0, w) in _tiles(nq):
                    ps = self.pb()[:, :w]
                    for k in range(8):
                        S.mm(ps, slot[:, k, mi * 128:(mi + 1) * 128], OT[:, k, t0:t0 + w], k == 0, k == 7)
                    rr = self.X[:, m, q0 + t0:q0 + t0 + w]
                    S.stt("dve", rr, ps, G1[:, m, 0:1], rr, ALU.mult, ALU.add)
                if do_ctx_q:
                    ps = self.pb()[:, :NCTX]
                    for k in range(8):
                        S.mm(ps, slot[:, k, mi * 128:(mi + 1) * 128], OTC[:, k, :], k == 0, k == 7)
                    rr = self.XC[:, m, :]
                    S.stt("dve", rr, ps, G1[:, m, 1:2], rr, ALU.mult, ALU.add)

    def write_out(self, final):
        S = self.S
        S.sb_top = self.SCR0
        self.norm_tmp()
        if final:
            ST = [S.new([8, 512], F32) for _ in range(2)]
            for i, (t0, w) in enumerate(_tiles(NOUT)):
                st = ST[i % 2]
                self.norm_h_f32(self.X[:, :, t0:t0 + w], st, w)
                for k in range(8):
                    S.dma("sp", self.d_out[k][:, t0:t0 + w], st[:, k, :w])
        else:
            for k in range(8):
                S.dma("sp", self.d_out[k], self.X[:, k, 0:NOUT])
        S.wait_all_dma("sp")

    def norm_h_f32(self, src, dst, w):
        S = self.S
        ps = self.pb()[:, :w]
        for k in range(8):
            sq = self.SQ[k % 2][:, :w]
            S.act(sq, src[:, k, :], AF.Square)
            S.mm(ps, self.ONES, sq, k == 0, k == 7, inc=True)
        r = self.RS[:, :w]
        S.ts("dve", r, ps, 1024.0 * EPS, None, ALU.add)
        S.act(r, r, AF.Sqrt)
        S.recip(r, r)
        for k in range(8):
            tmp = self.TMPN[k % 2][:, :w]
            S.tt("dve", tmp, src[:, k, :], r, ALU.mult)
            S.act(dst[:, k, :w], tmp, AF.Identity, scale=self.NFIN[:, k:k + 1])

    def build(self, stop=""):
        self.prologue()
        for l in range(4):
            if stop == "p":
                break
            last = l == 3
            if not last:
                self.bg = self.mod_tasks(l + 1)
            if l % 2 == 0:
                e = l // 2
                self.ab_tables(e)
                nout = AB_OUT[l]
                nin = AB_IN[l]
                for (s, t) in _split(nout // 128, 4):
                    c0, c1 = s * 128, t * 128
                    hs = max(c0 - 128, 0)
                    he = min(c1 + 128, nin)
                    self.ab_mixer(l, e, self.X, c0, c1, hs, he, 0, 0 if c0 == 0 else None, None)
                self.ab_mixer(l, e, self.XC, 0, NCTX, 0, NCTX, 1, 1, 2)
            else:
                o_ = l // 2
                sp = _split(NA_PAIRS[l], 4)
                for i, (a0, a1) in enumerate(sp):
                    self.na_mixer(l, o_, a0, a1, (not last) and i == len(sp) - 1)
            self.bg_flush()
            if stop == "m%d" % l:
                break
            n = FFN_N[l]
            h1 = (n // 128 + 1) // 2 * 128
            self.ffn(l, [(self.X[:, :, 0:h1], h1, 0)])
            segs = [(self.X[:, :, h1:n], n - h1, 0)]
            if not last:
                segs.append((self.XC, NCTX, 1))
            self.ffn(l, segs)
            if stop == "f%d" % l:
                break
        self.write_out(final=(stop == ""))
        return self.nc


def _na_tables():
    out = {}
    for rev in (0, 1):
        dr_i = np.zeros((128, 15, 128), np.int64)
        dc_i = np.zeros((128, 15, 128), np.int64)
        msk = np.zeros((128, 15, 128), np.float32)
        kp = np.arange(128)
        qi = np.arange(128)
        kr2, kcl = kp // 64, kp % 64
        qr2, qcl = qi // 64, qi % 64
        for typ, a in ((0, 0), (1, 1), (2, 4)):
            cs = max(a - 2, 0)
            for jj in range(5):
                krl = 2 * (cs + jj) + kr2[:, None]
                qrl = 2 * a + qr2[None, :]
                kc_ = kcl[:, None] + 0 * qrl
                qc_ = qcl[None, :] + 0 * krl
                if rev:
                    r, c, kr, kc = 63 - qrl, 63 - qc_, 63 - krl, 63 - kc_
                else:
                    r, c, kr, kc = qrl, qc_, krl, kc_
                r = r + 0 * kr
                kr = kr + 0 * r
                rs = np.clip(r - 4, 0, 56)
                cst = np.clip(c - 8, 0, 48)
                ok = (kr >= rs) & (kr < rs + 8) & (kc >= cst) & (kc < cst + 16)
                dr = np.where(ok, kr - r + 7, 0)
                dc = np.clip(kc - c, -15, 15) + 15
                dr_i[:, typ * 5 + jj, :] = dr
                dc_i[:, typ * 5 + jj, :] = dc
                msk[:, typ * 5 + jj, :] = ok
        out[rev] = (dr_i, dc_i, msk)
    return out


def _invc(rev):
    t = np.zeros((128, 3, 4, 8), np.float32)
    for g, wd in enumerate((2, 4, 8, 16)):
        half = wd // 2
        for which, L, locs in ((0, 4096, range(8)), (1, 256, range(8)), (2, 256, range(248, 256))):
            for i, tl in enumerate(locs):
                tg = (L - 1 - tl) if rev else tl
                cnt = min(tg + half, L) - max(tg - half, 0)
                t[:, which, g, i] = np.float32(1.0) / np.float32(cnt)
    return t


_CACHE = {}


def kernel(x, c, ctx, c_ctx, w_mod, b_mod, norm_mix, norm_ffn, w_in_ab, ln_v, w_spatial,
           b_spatial, w_pool, pool_scale, w_out_ab, w_qkv, rpb, w_out_na, w_ffn_in,
           w_ffn_out, norm_final):
    f = lambda a: np.ascontiguousarray(np.asarray(a, dtype=np.float32))
    x, c, ctx, c_ctx = f(x), f(c), f(ctx), f(c_ctx)
    w_mod, b_mod, norm_mix, norm_ffn = f(w_mod), f(b_mod), f(norm_mix), f(norm_ffn)
    w_in_ab, ln_v, w_spatial, b_spatial = f(w_in_ab), f(ln_v), f(w_spatial), f(b_spatial)
    w_pool, pool_scale, w_out_ab, w_qkv = f(w_pool), f(pool_scale), f(w_out_ab), f(w_qkv)
    rpb, w_out_na, w_ffn_in, w_ffn_out, norm_final = f(rpb), f(w_out_na), f(w_ffn_in), f(w_ffn_out), f(norm_final)

    stop = DEBUG_STOP
    if "nc" not in _CACHE:
        _CACHE["nc"] = Prog().build(stop)
    nc = _CACHE["nc"]

    kc = lambda w: w.reshape(8, 128, -1)
    shared = {}
    shared["wmod"] = f(w_mod.reshape(4, 8, 128, 12, 512).transpose(0, 3, 2, 1, 4))
    shared["bmod"] = f(np.repeat(b_mod.reshape(4, 48, 128).transpose(2, 0, 1)[..., None], 2, axis=3))
    shared["nmix"] = f(np.repeat(norm_mix.reshape(4, 8, 128).transpose(2, 0, 1)[..., None], 2, axis=3))
    shared["nffn"] = f(np.repeat(norm_ffn.reshape(4, 8, 128).transpose(2, 0, 1)[..., None], 2, axis=3))
    shared["nfin"] = f(norm_final.reshape(8, 128).T)
    shared["ident"] = np.eye(128, dtype=np.float32)
    shared["winab"] = f(w_in_ab.reshape(2, 8, 128, 3, 512).transpose(0, 3, 2, 1, 4))
    shared["woutab"] = f(w_out_ab.reshape(2, 8, 128, 2, 512).transpose(0, 3, 2, 1, 4))
    shared["lnvB"] = f(np.broadcast_to(ln_v[:, None, :], (2, 128, 512)))
    shared["wpool"] = f(w_pool.transpose(0, 2, 1, 3))
    shared["pscale"] = f(pool_scale.reshape(2, 4, 128).transpose(2, 0, 1))
    shared["wqkv"] = f(w_qkv.reshape(2, 8, 128, 3, 8, 128).transpose(0, 4, 2, 1, 3, 5).reshape(2, 8, 128, 8, 384))
    shared["wona"] = f(w_out_na.reshape(2, 8, 128, 2, 512).transpose(0, 3, 2, 1, 4))
    shared["wfin"] = f(w_ffn_in.reshape(4, 8, 128, 2, 22, 128).transpose(0, 4, 2, 1, 3, 5).reshape(4, 22, 128, 8, 256))
    shared["wfout"] = f(w_ffn_out.reshape(4, 22, 128, 1024))
    nat = _na_tables()
    per_rev = {}
    for rev in (0, 1):
        d = {}
        ws = w_spatial[:, :, ::-1, ::-1] if rev else w_spatial
        d["wsT"] = f(ws.transpose(0, 3, 1, 2))
        bs = b_spatial[:, :, ::-1] if rev else b_spatial
        d["bsB"] = f(np.broadcast_to(bs[:, None, :, :], (2, 128, 4, 128)))
        d["invc"] = _invc(rev)
        d["alpha"] = np.full((128, 1), 0.0 if rev else 1.0, np.float32)
        dr_i, dc_i, msk = nat[rev]
        d["btab"] = f(rpb[:, :, dr_i, dc_i])
        d["mtab"] = msk
        per_rev[rev] = d
    in_maps = []
    for core in range(8):
        b, rev = core // 2, core % 2
        m = dict(shared)
        m.update(per_rev[rev])
        if rev:
            xs = x[b, ::-1][:NTOK]
            cs_ = ctx[b, ::-1]
        else:
            xs = x[b, :NTOK]
            cs_ = ctx[b]
        m["xT"] = f(xs.T.reshape(8, 128, NTOK))
        m["cT"] = f(cs_.T.reshape(8, 128, NCTX))
        m["cc"] = f(np.stack([c[b], c_ctx], axis=1).reshape(8, 128, 2).transpose(1, 0, 2))
        in_maps.append(m)
    res = run_bass_kernel_spmd(nc, in_maps, core_ids=list(range(8)))
    out = np.zeros((4, 4096, D), np.float32)
    for core in range(8):
        b, rev = core // 2, core % 2
        o = np.asarray(res.results[core]["out"], dtype=np.float32).reshape(D, NOUT).T
        if rev:
            out[b, 2048:] = o[::-1]
        else:
            out[b, :2048] = o
    return out
```

```python
import contextlib
import numpy as np
import concourse.bass as bass
import concourse.mybir as mybir

F32 = mybir.dt.float32
BF16 = mybir.dt.bfloat16
AF = mybir.ActivationFunctionType
ALU = mybir.AluOpType
AX = mybir.AxisListType

NLANES = 24
NHW = 10
GRAN = 64
_DTSIZE = {F32: 4, BF16: 2}


class Sched:
    def __init__(self, nc, sb_bytes, ps_bytes=16384):
        self.nc = nc
        self.es = contextlib.ExitStack()
        self.eng = {"pe": nc.tensor, "act": nc.scalar, "dve": nc.vector,
                    "pool": nc.gpsimd, "sp": nc.sync}
        names = ["pe", "act", "dve", "pool"] + [f"ln{i}" for i in range(NLANES)]
        self.names = names
        self.idx = {n: i for i, n in enumerate(names)}
        self.NE = len(names)
        self.sems = [self.es.enter_context(nc.semaphore("s_" + n)) for n in names]
        self.cnt = np.zeros(self.NE, np.int64)
        self.seen = {e: np.zeros(self.NE, np.int64) for e in self.eng}
        self.snap = {}
        self.seen["pe"][self.idx["pe"]] = 1 << 60
        self.lane_rr = 0
        self.lane_rr_sw = 0
        self.sb = self.es.enter_context(nc.sbuf_tensor("arena_sb", [128, sb_bytes // 4], F32))
        self.ps = self.es.enter_context(nc.psum_tensor("arena_ps", [128, ps_bytes // 4], F32))
        self.trk = {
            "arena_sb": (np.zeros((sb_bytes // GRAN + 1, self.NE), np.int64),
                         np.zeros((sb_bytes // GRAN + 1, self.NE), np.int64)),
            "arena_ps": (np.zeros((ps_bytes // GRAN + 1, self.NE), np.int64),
                         np.zeros((ps_bytes // GRAN + 1, self.NE), np.int64)),
        }
        self.sb_top = 0
        self.sb_bytes = sb_bytes
        self.blk_cache = {}
        self.n_wait = 0
        self.n_ins = 0

    def alloc(self, nbytes, align=64):
        off = (self.sb_top + align - 1) // align * align
        assert off + nbytes <= self.sb_bytes, f"SBUF overflow {off + nbytes} > {self.sb_bytes}"
        self.sb_top = off + nbytes
        self.sb_max = max(getattr(self, 'sb_max', 0), self.sb_top)
        return off

    def view(self, off, shape, dt):
        n = int(np.prod(shape))
        nb = n * _DTSIZE[dt]
        assert off % 4 == 0 and nb % 4 == 0
        v = self.sb[:, off // 4:(off + nb) // 4]
        if dt != F32:
            v = v.bitcast(dt)
        if len(shape) == 2:
            v = v.rearrange("p (a b) -> p a b", a=shape[0])
        elif len(shape) == 3:
            v = v.rearrange("p (a b c) -> p a b c", a=shape[0], b=shape[1])
        return v

    def new(self, shape, dt):
        n = int(np.prod(shape)) * _DTSIZE[dt]
        n = (n + 3) // 4 * 4
        off = self.alloc(n)
        return self.view(off, shape, dt)

    def psum(self, bank, cols=512, dt=F32, col0=0):
        v = self.ps[:, bank * 512 + col0: bank * 512 + col0 + cols]
        return v

    def _blocks(self, ap):
        name = ap.tensor.name
        if name not in self.trk:
            return None, None
        key = (name, ap.offset, ap.ap, ap.dtype)
        r = self.blk_cache.get(key)
        if r is None:
            es = _DTSIZE[ap.dtype]
            dims = ap.ap
            pstride = dims[0][0]
            off = (ap.offset % pstride) * es
            inner = [(s * es, c) for s, c in dims[1:]]
            starts = np.array([off], np.int64)
            run = es
            if inner:
                s_last, c_last = inner[-1]
                if s_last == es:
                    run = es * c_last
                    inner = inner[:-1]
                for s, c in inner:
                    starts = (starts[:, None] + (np.arange(c, dtype=np.int64) * s)[None, :]).ravel()
            lo = starts // GRAN
            hi = (starts + run - 1) // GRAN
            if len(starts) == 1:
                r = np.arange(lo[0], hi[0] + 1)
            else:
                r = np.unique(np.concatenate([np.arange(a, b + 1) for a, b in zip(lo, hi)]))
            self.blk_cache[key] = r
        return self.trk[name], r

    def _need(self, reads, writes):
        need = np.zeros(self.NE, np.int64)
        for ap in reads:
            t, b = self._blocks(ap)
            if t is not None:
                np.maximum(need, t[0][b].max(0), out=need)
        for ap in writes:
            t, b = self._blocks(ap)
            if t is not None:
                np.maximum(need, t[0][b].max(0), out=need)
                np.maximum(need, t[1][b].max(0), out=need)
        return need

    def _do_waits(self, e, need):
        seen = self.seen[e]
        eng = self.eng[e]
        for j in np.argsort(-(need - seen)):
            if need[j] > seen[j]:
                eng.wait_ge(self.sems[j], int(need[j]))
                self.n_wait += 1
                seen[j] = need[j]
                sn = self.snap.get((int(j), int(need[j])))
                if sn is not None:
                    np.maximum(seen, sn, out=seen)

    def _record(self, ei, val, reads, writes):
        for ap in reads:
            t, b = self._blocks(ap)
            if t is not None:
                t[1][b, ei] = val
        for ap in writes:
            t, b = self._blocks(ap)
            if t is not None:
                t[0][b, ei] = val

    def op(self, e, fn, reads, writes, inc=True):
        ei = self.idx[e]
        need = self._need(reads, writes)
        self._do_waits(e, need)
        ins = fn()
        self.n_ins += 1
        val = int(self.cnt[ei]) + 1
        self._record(ei, val, reads, writes)
        if inc:
            ins.then_inc(self.sems[ei], 1)
            self.cnt[ei] = val
            sn = self.seen[e].copy()
            sn[ei] = val
            self.snap[(ei, val)] = sn
        return ins

    def dma(self, q, out, in_):
        if q == "pool":
            lane = 4 + NHW + self.lane_rr_sw
            self.lane_rr_sw = (self.lane_rr_sw + 1) % (NLANES - NHW)
        else:
            lane = 4 + self.lane_rr
            self.lane_rr = (self.lane_rr + 1) % NHW
        need = self._need([in_], [out])
        need[lane] = max(need[lane], self.cnt[lane])
        self._do_waits(q, need)
        val = int(self.cnt[lane]) + 16
        ins = self.eng[q].dma_start(out=out, in_=in_)
        ins.then_inc(self.sems[lane], 16)
        self.n_ins += 1
        self.cnt[lane] = val
        self._record(lane, val, [in_], [out])
        self.snap[(lane, val)] = self.seen[q].copy()
        return (lane, val)

    def wait_all_dma(self, e="sp"):
        need = np.zeros(self.NE, np.int64)
        need[4:] = self.cnt[4:]
        self._do_waits(e, need)

    def mm(self, out, lhsT, rhs, start, stop, inc=None):
        return self.op("pe", lambda: self.nc.tensor.matmul(out, lhsT, rhs, start=start, stop=stop),
                       [lhsT, rhs], [out], inc=(stop if inc is None else inc))

    def transpose(self, out, in_, ident):
        return self.op("pe", lambda: self.nc.tensor.transpose(out, in_, ident), [in_, ident], [out])

    def act(self, out, in_, func, bias=None, scale=None, accum_out=None, e="act"):
        kw = {}
        reads = [in_]
        writes = [out]
        if bias is not None:
            kw["bias"] = bias
            if not isinstance(bias, (int, float)):
                reads.append(bias)
        if scale is not None:
            kw["scale"] = scale
            if not isinstance(scale, (int, float)):
                reads.append(scale)
        if accum_out is not None:
            kw["accum_out"] = accum_out
            writes.append(accum_out)
        return self.op("act", lambda: self.nc.scalar.activation(out, in_, func, **kw), reads, writes)

    def tt(self, e, out, in0, in1, op):
        return self.op(e, lambda: self.eng[e].tensor_tensor(out, in0, in1, op), [in0, in1], [out])

    def ts(self, e, out, in0, s1, s2, op0, op1=None, accum_out=None):
        reads = [in0] + [s for s in (s1, s2) if s is not None and not isinstance(s, (int, float))]
        writes = [out] + ([accum_out] if accum_out is not None else [])
        kw = {}
        if op1 is not None:
            kw["op1"] = op1
        if accum_out is not None:
            kw["accum_out"] = accum_out
        return self.op(e, lambda: self.eng[e].tensor_scalar(out, in0, s1, s2, op0, **kw), reads, writes)

    def stt(self, e, out, in0, scalar, in1, op0, op1):
        reads = [in0, in1] + ([] if isinstance(scalar, (int, float)) else [scalar])
        return self.op(e, lambda: self.eng[e].scalar_tensor_tensor(out, in0, scalar, in1, op0, op1),
                       reads, [out])

    def copy(self, e, out, in_):
        if e == "act":
            return self.op(e, lambda: self.nc.scalar.copy(out, in_), [in_], [out])
        return self.op(e, lambda: self.eng[e].tensor_copy(out, in_), [in_], [out])

    def memset(self, e, out, val):
        return self.op(e, lambda: self.eng[e].memset(out, val), [], [out])

    def recip(self, out, in_):
        return self.op("dve", lambda: self.nc.vector.reciprocal(out, in_), [in_], [out])


import os
from concourse.bass_utils import run_bass_kernel_spmd

D = 1024
NTOK = 2816
NOUT = 2048
NCTX = 256
EPS = 1e-6
DFF = 2816
SB_BYTES = 212800
NRING = 3
DEBUG_STOP = os.environ.get("MK_STOP", "")

AB_OUT = {0: 2688, 2: 2304}
AB_IN = {0: 2816, 2: 2432}
NA_PAIRS = {1: 19, 3: 16}
NA_NCH = {1: 21, 3: 18}
FFN_N = {0: 2688, 1: 2432, 2: 2304, 3: 2048}


def _split(n, parts):
    base, rem = divmod(n, parts)
    out = []
    s = 0
    for i in range(parts):
        e = s + base + (1 if i < rem else 0)
        out.append((s, e))
        s = e
    return out


def _tiles(n, w=512):
    return [(t, min(w, n - t)) for t in range(0, n, w)]


class Prog:
    def __init__(self):
        nc = bass.Bass("TRN2", target_bir_lowering=False)
        self.nc = nc
        dt = lambda name, shape, kind="ExternalInput": nc.dram_tensor(name, list(shape), F32, kind=kind).ap()
        self.d_x = dt("xT", [8, 128, NTOK])
        self.d_c = dt("cT", [8, 128, NCTX])
        self.d_cc = dt("cc", [128, 8, 2])
        self.d_wmod = dt("wmod", [4, 12, 128, 8, 512])
        self.d_bmod = dt("bmod", [128, 4, 48, 2])
        self.d_nmix = dt("nmix", [128, 4, 8, 2])
        self.d_nffn = dt("nffn", [128, 4, 8, 2])
        self.d_nfin = dt("nfin", [128, 8])
        self.d_ident = dt("ident", [128, 128])
        self.d_winab = dt("winab", [2, 3, 128, 8, 512])
        self.d_woutab = dt("woutab", [2, 2, 128, 8, 512])
        self.d_wsT = dt("wsT", [2, 128, 4, 128])
        self.d_bsB = dt("bsB", [2, 128, 4, 128])
        self.d_lnvB = dt("lnvB", [2, 128, 512])
        self.d_wpool = dt("wpool", [2, 128, 4, 128])
        self.d_pscale = dt("pscale", [128, 2, 4])
        self.d_invc = dt("invc", [128, 3, 4, 8])
        self.d_alpha = dt("alpha", [128, 1])
        self.d_wqkv = dt("wqkv", [2, 8, 128, 8, 384])
        self.d_wona = dt("wona", [2, 2, 128, 8, 512])
        self.d_btab = dt("btab", [2, 16, 128, 15, 128])
        self.d_mtab = dt("mtab", [128, 15, 128])
        self.d_wfin = dt("wfin", [4, 22, 128, 8, 256])
        self.d_wfout = dt("wfout", [4, 22, 128, 1024])
        self.d_out = dt("out", [8, 128, NOUT], kind="ExternalOutput")

        S = Sched(nc, SB_BYTES)
        self.S = S
        self.pbc = 0
        self.ringc = 0
        self.bg = []
        self.X = S.new([8, NTOK], F32)
        self.XC = S.new([8, NCTX], F32)
        self.RING = [S.new([4096], BF16) for _ in range(NRING)]
        self.MOD = S.new([4, 48, 2], F32)
        self.A1 = S.new([4, 8, 2], F32)
        self.A2 = S.new([4, 8, 2], F32)
        self.ONES = S.new([128], BF16)
        self.IDENT = S.new([128], BF16)
        self.NFIN = S.new([8], F32)
        self.ALPHA = S.new([1], F32)
        self.INVC = S.new([3, 4, 8], F32)
        self.PSC = S.new([2, 4], F32)
        self.MTAB = S.new([15, 128], BF16)
        self.WST = S.new([4, 128], BF16)
        self.BSB = S.new([512], F32)
        self.LNVB = S.new([512], F32)
        self.WPOOL = S.new([4, 128], BF16)
        self.HPREV = S.new([8, 256], BF16)
        self.SCR0 = S.sb_top

    def pb(self, cols=512):
        b = self.pbc % 7
        self.pbc += 1
        return self.S.ps[:, b * 512: b * 512 + cols]

    def pb2(self, cols):
        while (self.pbc % 7) not in (0, 2, 4):
            self.pbc += 1
        b = self.pbc % 7
        self.pbc += 2
        return self.S.ps[:, b * 512: b * 512 + cols]

    def bg_step(self, n=1):
        for _ in range(n):
            if self.bg:
                self.bg[0]()
                self.bg.pop(0)

    def bg_flush(self):
        while self.bg:
            self.bg_step()

    def ring(self, bg=False):
        if bg:
            return self.RING[NRING - 1]
        nr = NRING - 1 if self.bg else NRING
        r = self.RING[self.ringc % nr]
        self.ringc += 1
        return r

    def load_w(self, src, k, n, bg=False):
        slot = self.ring(bg)[:, 0:k * n].rearrange("p (k n) -> p k n", k=k)
        self.S.dma("pool", slot, src)
        return slot

    def norm_tmp(self):
        S = self.S
        self.SQ = [S.new([512], BF16) for _ in range(2)]
        self.RS = S.new([512], F32)
        self.TMPN = [S.new([512], F32) for _ in range(2)]

    def norm_h(self, A, B, j, src, dst, n):
        S = self.S
        for (t0, w) in _tiles(n):
            ps = self.pb()[:, :w]
            for k in range(8):
                sq = self.SQ[k % 2][:, :w]
                S.act(sq, src[:, k, t0:t0 + w], AF.Square)
                S.mm(ps, self.ONES, sq, k == 0, k == 7, inc=True)
            r = self.RS[:, :w]
            S.ts("dve", r, ps, 1024.0 * EPS, None, ALU.add)
            S.act(r, r, AF.Sqrt)
            S.recip(r, r)
            for k in range(8):
                tmp = self.TMPN[k % 2][:, :w]
                S.tt("dve", tmp, src[:, k, t0:t0 + w], r, ALU.mult)
                if B is None:
                    S.act(dst[:, k, t0:t0 + w], tmp, AF.Identity, scale=A[:, k, j:j + 1])
                else:
                    S.act(dst[:, k, t0:t0 + w], tmp, AF.Identity, bias=B[:, k, j:j + 1], scale=A[:, k, j:j + 1])

    def prologue(self):
        S = self.S
        nc = self.nc
        S.sb_top = self.SCR0
        for k in range(8):
            S.dma("sp", self.X[:, k, :], self.d_x[k])
            S.dma("sp", self.XC[:, k, :], self.d_c[k])
        ccs = S.new([8, 2], F32)
        csil = S.new([8, 2], BF16)
        bmod = S.new([4, 48, 2], F32)
        nmix = S.new([4, 8, 2], F32)
        nffn = S.new([4, 8, 2], F32)
        onesf = S.new([128], F32)
        S.dma("sp", ccs, self.d_cc)
        S.dma("sp", bmod, self.d_bmod)
        S.dma("sp", nmix, self.d_nmix)
        S.dma("sp", nffn, self.d_nffn)
        S.dma("sp", self.NFIN, self.d_nfin)
        S.dma("sp", self.ALPHA, self.d_alpha)
        S.dma("sp", self.INVC, self.d_invc)
        S.dma("sp", self.PSC, self.d_pscale)
        S.dma("pool", self.IDENT, self.d_ident)
        S.dma("pool", self.MTAB, self.d_mtab)
        S.memset("dve", onesf, 1.0)
        S.copy("dve", self.ONES, onesf)
        S.act(csil, ccs, AF.Silu)
        self.p_csil, self.p_bmod, self.p_nmix, self.p_nffn = csil, bmod, nmix, nffn
        self.SCR0 = S.sb_top
        for t in self.mod_tasks(0, bg=False):
            t()
        S.ts("dve", self.NFIN, self.NFIN, 32.0, None, ALU.mult)

    def mod_tasks(self, l, bg=True):
        S = self.S
        ps = S.ps[:, 7 * 512:7 * 512 + 96]
        csil, bmod = self.p_csil, self.p_bmod

        def piece(n):
            def f():
                slot = self.load_w(self.d_wmod[l, n], 8, 512, bg=bg)
                for mi in range(4):
                    m = n * 4 + mi
                    for k in range(8):
                        S.mm(ps[:, m * 2:(m + 1) * 2], slot[:, k, mi * 128:(mi + 1) * 128], csil[:, k, :], k == 0, k == 7)
            return f

        def fin():
            S.tt("dve", self.MOD[:, l].rearrange("p a b -> p (a b)"), ps, bmod[:, l].rearrange("p a b -> p (a b)"), ALU.add)
            for (Adst, gain, c0) in ((self.A1, self.p_nmix, 8), (self.A2, self.p_nffn, 32)):
                S.stt("dve", Adst[:, l], self.MOD[:, l, c0:c0 + 8, :], 1.0, gain[:, l], ALU.add, ALU.mult)
                S.ts("dve", Adst[:, l], Adst[:, l], 32.0, None, ALU.mult)
        return [piece(n) for n in range(12)] + [fin]

    def ffn(self, l, segs):
        S = self.S
        S.sb_top = self.SCR0
        self.norm_tmp()
        ntot = sum(s[1] for s in segs)
        H = S.new([8, ntot], BF16)
        HID = S.new([8, ntot], BF16)
        SA = [S.new([512], F32) for _ in range(2)]
        B2 = self.MOD[:, l, 24:32, :]
        G2 = self.MOD[:, l, 40:48, :]
        off = 0
        tl = []
        for (res, n, j) in segs:
            self.norm_h(self.A2[:, l], B2, j, res, H[:, :, off:off + n], n)
            for (t0, w) in _tiles(n):
                tl.append((off + t0, w, res, t0, j))
            off += n
        it = 0
        for (j0, j1) in ((0, 8), (8, 15), (15, 22)):
            for jh in range(j0, j1):
                slot = self.load_w(self.d_wfin[l, jh], 8, 256)
                jl = jh - j0
                for (c0, w, res, t0, j) in tl:
                    pa = self.pb()[:, :w]
                    pg = self.pb()[:, :w]
                    for k in range(8):
                        S.mm(pa, slot[:, k, 0:128], H[:, k, c0:c0 + w], k == 0, k == 7)
                    for k in range(8):
                        S.mm(pg, slot[:, k, 128:256], H[:, k, c0:c0 + w], k == 0, k == 7)
                    sa = SA[it % 2][:, :w]
                    it += 1
                    S.act(sa, pa, AF.Silu)
                    S.tt("dve", HID[:, jl, c0:c0 + w], sa, pg, ALU.mult)
            nb = j1 - j0
            wo = []
            for q0 in range(0, nb, 4):
                qn = min(4, nb - q0)
                slot = self.ring()[:, 0:qn * 1024].rearrange("p (k n) -> p k n", k=qn)
                S.dma("pool", slot, self.d_wfout[l, j0 + q0:j0 + q0 + qn].rearrange("j p n -> p j n"))
                for qi in range(qn):
                    wo.append(slot[:, qi, :])
            for m in range(8):
                for (c0, w, res, t0, j) in tl:
                    py = self.pb()[:, :w]
                    for jl in range(nb):
                        S.mm(py, wo[jl][:, m * 128:(m + 1) * 128], HID[:, jl, c0:c0 + w], jl == 0, jl == nb - 1)
                    rr = res[:, m, t0:t0 + w]
                    S.stt("dve", rr, py, G2[:, m, j:j + 1], rr, ALU.mult, ALU.add)

    def ab_tables(self, e):
        S = self.S
        S.dma("pool", self.WST, self.d_wsT[e])
        S.dma("sp", self.BSB.rearrange("p (a b) -> p a b", a=4), self.d_bsB[e])
        S.dma("sp", self.LNVB, self.d_lnvB[e])
        S.dma("pool", self.WPOOL, self.d_wpool[e])

    def ab_mixer(self, l, e, Xb, c0, c1, hs, he, j, fix_head, fix_tail):
        S = self.S
        nc = self.nc
        S.sb_top = self.SCR0
        self.norm_tmp()
        nh = he - hs
        n = c1 - c0
        o = c0 - hs
        H = S.new([8, nh], BF16)
        YA = S.new([4, n], BF16)
        YB = S.new([4, n], BF16)
        PB = [S.new([nh + 16], F32) for _ in range(2)]
        Aa = S.new([nh + 16], F32)
        Ab = S.new([nh + 16], F32)
        DF = S.new([n], BF16)
        VT = [S.new([512], F32) for _ in range(3)]
        CEN = [S.new([512], F32) for _ in range(3)]
        VH = [S.new([512], BF16) for _ in range(3)]
        SQJ = S.new([512], F32)
        SM = [S.new([4], F32) for _ in range(3)]
        T8 = S.new([8], F32)
        B1 = self.MOD[:, l, 0:8, :]
        G1 = self.MOD[:, l, 16:24, :]
        if o > 0:
            S.copy("act", H[:, :, 0:o], self.HPREV[:, :, 256 - o:256])
        self.norm_h(self.A1[:, l], B1, j, Xb[:, :, c0:he], H[:, :, o:nh], nh - o)
        if j == 0:
            S.copy("act", self.HPREV[:, :, 128:256], H[:, :, o + n - 128:o + n])
        slot_u = self.load_w(self.d_winab[e, 0], 8, 512)
        slot_p = self.load_w(self.d_winab[e, 2], 8, 512)

        def u_proj(m):
            for (t0, w) in _tiles(n):
                ps = self.pb()[:, :w]
                for k in range(8):
                    S.mm(ps, slot_u[:, k, m * 128:(m + 1) * 128], H[:, k, o + t0:o + t0 + w], k == 0, k == 7)
                S.act(YA[:, m, t0:t0 + w], ps, AF.Gelu_apprx_tanh)

        def p_proj(g):
            P = PB[g % 2]
            S.memset("dve", P[:, 0:8], 0.0)
            S.memset("dve", P[:, 8 + nh:16 + nh], 0.0)
            for (t0, w) in _tiles(nh):
                ps = self.pb()[:, :w]
                for k in range(8):
                    S.mm(ps, slot_p[:, k, g * 128:(g + 1) * 128], H[:, k, t0:t0 + w], k == 0, k == 7)
                S.copy("act", P[:, 8 + t0:8 + t0 + w], ps)

        def pooling(g):
            P = PB[g % 2]
            wd = 2 ** (g + 1)
            half = wd // 2
            cur = P
            length = nh + 16
            step = 1
            bufs = [Aa, Ab]
            bi = 0
            while step < wd:
                nxt = bufs[bi]
                bi ^= 1
                L2 = length - step
                S.tt("dve", nxt[:, 0:L2], cur[:, 0:L2], cur[:, step:step + L2], ALU.add)
                cur = nxt
                length = L2
                step *= 2
            i0 = o + 8
            Dd = bufs[bi][:, 0:n]
            fw_ = cur[:, i0 - half:i0 - half + n]
            rv_ = cur[:, i0 - half + 1:i0 - half + 1 + n]
            S.tt("dve", Dd, fw_, rv_, ALU.subtract)
            S.stt("dve", Dd, Dd, self.ALPHA[:, 0:1], rv_, ALU.mult, ALU.add)
            S.stt("dve", DF, Dd, 1.0 / wd, P[:, i0:i0 + n], ALU.mult, ALU.subtract)
            if fix_head is not None:
                S.tt("dve", T8, Dd[:, 0:8], self.INVC[:, fix_head, g, :], ALU.mult)
                S.tt("dve", DF[:, 0:8], T8, P[:, i0:i0 + 8], ALU.subtract)
            if fix_tail is not None:
                S.tt("dve", T8, Dd[:, n - 8:n], self.INVC[:, fix_tail, g, :], ALU.mult)
                S.tt("dve", DF[:, n - 8:n], T8, P[:, i0 + n - 8:i0 + n], ALU.subtract)

        def yb_proj(g):
            for (t0, w) in _tiles(n):
                ps = self.pb()[:, :w]
                S.mm(ps, self.WPOOL[:, g, :], DF[:, t0:t0 + w], True, True)
                S.act(YB[:, g, t0:t0 + w], ps, AF.Identity, scale=self.PSC[:, e, g:g + 1])

        p_proj(0)
        for g in range(4):
            if j == 0:
                self.bg_step()
            u_proj(g)
            if g < 3:
                p_proj(g + 1)
            pooling(g)
            yb_proj(g)

        slot_v = self.load_w(self.d_winab[e, 1], 8, 512)
        nchunk = n // 128

        def v_a(ci):
            tc = o + ci * 128
            ps = self.pb()
            for k in range(8):
                S.mm(ps, H[:, k, tc:tc + 128], slot_v[:, k, :], k == 0, k == 7)
            S.act(VT[ci % 3], ps, AF.Gelu_apprx_tanh)

        def v_b1(ci):
            vt = VT[ci % 3]
            cen = CEN[ci % 3]
            vh = VH[ci % 3]
            sm = SM[ci % 3]
            S.op("dve", lambda: nc.vector.reduce_sum(sm[:, 0:1], vt, AX.X), [vt], [sm[:, 0:1]])
            S.ts("dve", sm[:, 1:2], sm[:, 0:1], -1.0 / 512.0, None, ALU.mult)
            S.ts("dve", cen, vt, sm[:, 1:2], None, ALU.add)
            S.act(SQJ, cen, AF.Square)
            S.op("dve", lambda: nc.vector.reduce_sum(sm[:, 2:3], SQJ, AX.X), [SQJ], [sm[:, 2:3]])
            S.ts("dve", sm[:, 3:4], sm[:, 2:3], 1.0 / 512.0, EPS, ALU.mult, ALU.add)
            S.act(sm[:, 3:4], sm[:, 3:4], AF.Sqrt)
            S.recip(sm[:, 3:4], sm[:, 3:4])
            S.stt("dve", vh, cen, sm[:, 3:4], self.LNVB, ALU.mult, ALU.mult)

        def v_b2(ci):
            cen = CEN[ci % 3]
            vh = VH[ci % 3]
            psg = self.pb()
            for g in range(4):
                S.mm(psg[:, g * 128:(g + 1) * 128], vh[:, g * 128:(g + 1) * 128], self.WST[:, g, :], True, True)
            S.tt("dve", cen, psg, self.BSB, ALU.add)
            ya = YA[:, :, ci * 128:(ci + 1) * 128]
            S.tt("dve", ya, cen.rearrange("p (a b) -> p a b", a=4), ya, ALU.mult)

        for i in range(nchunk + 2):
            if i < nchunk:
                v_a(i)
            if 1 <= i <= nchunk:
                v_b1(i - 1)
            if i >= 2:
                v_b2(i - 2)
        for hf in range(2):
            slot = self.load_w(self.d_woutab[e, hf], 8, 512)
            for mi in range(4):
                m = hf * 4 + mi
                for (t0, w) in _tiles(n):
                    ps = self.pb()[:, :w]
                    for k in range(8):
                        rhs = (YA if k < 4 else YB)[:, k % 4, t0:t0 + w]
                        S.mm(ps, slot[:, k, mi * 128:(mi + 1) * 128], rhs, k == 0, k == 7)
                    rr = Xb[:, m, c0 + t0:c0 + t0 + w]
                    S.stt("dve", rr, ps, G1[:, m, j:j + 1], rr, ALU.mult, ALU.add)

    def na_mixer(self, l, o_, a0, a1, do_ctx_q):
        S = self.S
        nc = self.nc
        S.sb_top = self.SCR0
        NCH = NA_NCH[l]
        cs = lambda a: min(max(a - 2, 0), NCH - 5)
        k0 = cs(a0)
        k1 = cs(a1 - 1) + 5
        nkc = k1 - k0
        nk = nkc * 128
        npair = a1 - a0
        nq = npair * 128
        qoff = (a0 - k0) * 128
        H = S.new([8, nk], BF16)
        HC = S.new([8, NCTX], BF16)
        OT = S.new([8, nq], BF16)
        OTC = S.new([8, NCTX], BF16) if do_ctx_q else None
        QT = S.new([nq], BF16)
        KT = S.new([nk], BF16)
        V1 = S.new([nkc, 2, 65], BF16)
        QC = S.new([NCTX], BF16)
        KC = S.new([NCTX], BF16)
        VC = S.new([2, 2, 65], BF16)
        B1 = self.MOD[:, l, 0:8, :]
        G1 = self.MOD[:, l, 16:24, :]
        mark = S.sb_top
        self.norm_tmp()
        lh = (a0 - k0) * 128
        if lh > 0:
            S.copy("act", H[:, :, 0:lh], self.HPREV[:, :, 256 - lh:256])
        self.norm_h(self.A1[:, l], B1, 0, self.X[:, :, a0 * 128:k1 * 128], H[:, :, lh:nk], nk - lh)
        S.copy("act", self.HPREV, H[:, :, (a1 - 2 - k0) * 128:(a1 - k0) * 128])
        self.norm_h(self.A1[:, l], B1, 1, self.XC, HC, NCTX)
        S.sb_top = mark
        OTOK = S.new([npair, 128], BF16)
        OTOKC = S.new([2, 128], BF16)
        PT = [S.new([7, 128], BF16) for _ in range(3)]
        tbase = 0 if a0 < 2 else 2
        ntyp = 3 - tbase
        E = [S.new([5 * ntyp, 128], BF16) for _ in range(2)]
        VTF = S.new([nk], BF16)
        VCF = S.new([NCTX], BF16)
        RINV = [S.new([1], F32) for _ in range(3)]
        for c in range(nkc):
            S.memset("dve", V1[:, c, :, 64:65], 1.0)
        for c in range(2):
            S.memset("dve", VC[:, c, :, 64:65], 1.0)
        it = 0
        for hp in range(8):
            self.bg_step()
            slot = self.load_w(self.d_wqkv[o_, hp], 8, 384)
            for (t0, w) in _tiles(nq):
                ps = self.pb()[:, :w]
                for k in range(8):
                    S.mm(ps, slot[:, k, 0:128], H[:, k, qoff + t0:qoff + t0 + w], k == 0, k == 7)
                S.copy("act", QT[:, t0:t0 + w], ps)
            for (t0, w) in _tiles(nk):
                ps = self.pb()[:, :w]
                for k in range(8):
                    S.mm(ps, slot[:, k, 128:256], H[:, k, t0:t0 + w], k == 0, k == 7)
                S.copy("act", KT[:, t0:t0 + w], ps)
            for (t0, w) in _tiles(nk):
                ps = self.pb()[:, :w]
                for k in range(8):
                    S.mm(ps, slot[:, k, 256:384], H[:, k, t0:t0 + w], k == 0, k == 7)
                S.copy("act", VTF[:, t0:t0 + w], ps)
            for c in range(nkc):
                pst = self.pb()[:, 0:64].bitcast(BF16)
                S.transpose(pst, VTF[:, c * 128:(c + 1) * 128], self.IDENT)
                S.copy("dve", V1[:, c, :, 0:64], pst.rearrange("p (a b) -> p a b", a=2))
            ps = self.pb()[:, :NCTX]
            for k in range(8):
                S.mm(ps, slot[:, k, 128:256], HC[:, k, :], k == 0, k == 7)
            S.copy("act", KC, ps)
            ps = self.pb()[:, :NCTX]
            for k in range(8):
                S.mm(ps, slot[:, k, 256:384], HC[:, k, :], k == 0, k == 7)
            S.copy("act", VCF, ps)
            for c in range(2):
                pst = self.pb()[:, 0:64].bitcast(BF16)
                S.transpose(pst, VCF[:, c * 128:(c + 1) * 128], self.IDENT)
                S.copy("dve", VC[:, c, :, 0:64], pst.rearrange("p (a b) -> p a b", a=2))
            if do_ctx_q:
                ps = self.pb()[:, :NCTX]
                for k in range(8):
                    S.mm(ps, slot[:, k, 0:128], HC[:, k, :], k == 0, k == 7)
                S.copy("act", QC, ps)
            for hh in range(2):
                S.dma("pool", E[hh], self.d_btab[o_, hp * 2 + hh][:, tbase * 5:15, :])
                S.act(E[hh], E[hh], AF.Exp)
                S.tt("dve", E[hh], E[hh], self.MTAB[:, tbase * 5:15, :], ALU.mult)
            blocks = [("x", a) for a in range(a0, a1)]
            if do_ctx_q:
                blocks += [("c", 0), ("c", 1)]
            its = [(kind, a, hh) for (kind, a) in blocks for hh in range(2)]
            LA = 2

            def stage1(i):
                kind, a, hh = its[i]
                hsl = slice(hh * 64, hh * 64 + 64)
                pt = PT[i % 3]
                ptf = pt.rearrange("p a b -> p (a b)")
                if kind == "x":
                    typ = min(a, 2)
                    c_s = cs(a) - k0
                    q = QT[hsl, (a - a0) * 128:(a - a0 + 1) * 128]
                    pss = self.pb2(896)
                    for jj in range(5):
                        S.mm(pss[:, jj * 128:(jj + 1) * 128], KT[hsl, (c_s + jj) * 128:(c_s + jj + 1) * 128], q, True, True)
                    for jc in range(2):
                        S.mm(pss[:, (5 + jc) * 128:(6 + jc) * 128], KC[hsl, jc * 128:(jc + 1) * 128], q, True, True)
                    S.act(ptf[:, 0:512], pss[:, 0:512], AF.Exp, scale=0.125)
                    S.act(ptf[:, 512:896], pss[:, 512:896], AF.Exp, scale=0.125)
                    S.tt("dve", pt[:, 0:5, :], pt[:, 0:5, :], E[hh][:, (typ - tbase) * 5:(typ - tbase) * 5 + 5, :], ALU.mult)
                else:
                    q = QC[hsl, a * 128:(a + 1) * 128]
                    pss = self.pb()[:, 0:256]
                    for jc in range(2):
                        S.mm(pss[:, jc * 128:(jc + 1) * 128], KC[hsl, jc * 128:(jc + 1) * 128], q, True, True)
                    S.act(ptf[:, 0:256], pss, AF.Exp, scale=0.125)

            def stage2(i):
                kind, a, hh = its[i]
                pt = PT[i % 3]
                rinv = RINV[i % 3]
                pso = self.pb()[:, 0:65]
                if kind == "x":
                    c_s = cs(a) - k0
                    for jj in range(5):
                        S.mm(pso, pt[:, jj, :], V1[:, c_s + jj, hh, :], jj == 0, False)
                    for jc in range(2):
                        S.mm(pso, pt[:, 5 + jc, :], VC[:, jc, hh, :], False, jc == 1)
                    dst = OTOK[:, a - a0, hh * 64:hh * 64 + 64]
                else:
                    for jc in range(2):
                        S.mm(pso, pt[:, jc, :], VC[:, jc, hh, :], jc == 0, jc == 1)
                    dst = OTOKC[:, a, hh * 64:hh * 64 + 64]
                S.recip(rinv, pso[:, 64:65])
                S.ts("dve", dst, pso[:, 0:64], rinv[:, 0:1], None, ALU.mult)

            for i in range(len(its) + LA):
                if i < len(its):
                    stage1(i)
                if i >= LA:
                    stage2(i - LA)
            for (kind, a) in blocks:
                pst = self.pb()[:, 0:64].bitcast(BF16)
                if kind == "x":
                    S.transpose(pst, OTOK[:, a - a0, :], self.IDENT)
                    S.copy("dve", OT[:, hp, (a - a0) * 128:(a - a0 + 1) * 128], pst)
                else:
                    S.transpose(pst, OTOKC[:, a, :], self.IDENT)
                    S.copy("dve", OTC[:, hp, a * 128:(a + 1) * 128], pst)
        q0 = a0 * 128
        for hf in range(2):
            slot = self.load_w(self.d_wona[o_, hf], 8, 512)
            for mi in range(4):
                m = hf * 4 + mi
                for (t0, w) in _tiles(nq):
                    ps = self.pb()[:, :w]
                    for k in range(8):
                        S.mm(ps, slot[:, k, mi * 128:(mi + 1) * 128], OT[:, k, t0:t0 + w], k == 0, k == 7)
                    rr = self.X[:, m, q0 + t0:q0 + t0 + w]
                    S.stt("dve", rr, ps, G1[:, m, 0:1], rr, ALU.mult, ALU.add)
                if do_ctx_q:
                    ps = self.pb()[:, :NCTX]
                    for k in range(8):
                        S.mm(ps, slot[:, k, mi * 128:(mi + 1) * 128], OTC[:, k, :], k == 0, k == 7)
                    rr = self.XC[:, m, :]
                    S.stt("dve", rr, ps, G1[:, m, 1:2], rr, ALU.mult, ALU.add)

    def write_out(self, final):
        S = self.S
        S.sb_top = self.SCR0
        self.norm_tmp()
        if final:
            ST = [S.new([8, 512], F32) for _ in range(2)]
            for i, (t0, w) in enumerate(_tiles(NOUT)):
                st = ST[i % 2]
                self.norm_h_f32(self.X[:, :, t0:t0 + w], st, w)
                for k in range(8):
                    S.dma("sp", self.d_out[k][:, t0:t0 + w], st[:, k, :w])
        else:
            for k in range(8):
                S.dma("sp", self.d_out[k], self.X[:, k, 0:NOUT])
        S.wait_all_dma("sp")

    def norm_h_f32(self, src, dst, w):
        S = self.S
        ps = self.pb()[:, :w]
        for k in range(8):
            sq = self.SQ[k % 2][:, :w]
            S.act(sq, src[:, k, :], AF.Square)
            S.mm(ps, self.ONES, sq, k == 0, k == 7, inc=True)
        r = self.RS[:, :w]
        S.ts("dve", r, ps, 1024.0 * EPS, None, ALU.add)
        S.act(r, r, AF.Sqrt)
        S.recip(r, r)
        for k in range(8):
            tmp = self.TMPN[k % 2][:, :w]
            S.tt("dve", tmp, src[:, k, :], r, ALU.mult)
            S.act(dst[:, k, :w], tmp, AF.Identity, scale=self.NFIN[:, k:k + 1])

    def build(self, stop=""):
        self.prologue()
        for l in range(4):
            if stop == "p":
                break
            last = l == 3
            if not last:
                self.bg = self.mod_tasks(l + 1)
            if l % 2 == 0:
                e = l // 2
                self.ab_tables(e)
                nout = AB_OUT[l]
                nin = AB_IN[l]
                for (s, t) in _split(nout // 128, 4):
                    c0, c1 = s * 128, t * 128
                    hs = max(c0 - 128, 0)
                    he = min(c1 + 128, nin)
                    self.ab_mixer(l, e, self.X, c0, c1, hs, he, 0, 0 if c0 == 0 else None, None)
                self.ab_mixer(l, e, self.XC, 0, NCTX, 0, NCTX, 1, 1, 2)
            else:
                o_ = l // 2
                sp = _split(NA_PAIRS[l], 3)
                for i, (a0, a1) in enumerate(sp):
                    self.na_mixer(l, o_, a0, a1, (not last) and i == len(sp) - 1)
            self.bg_flush()
            if stop == "m%d" % l:
                break
            n = FFN_N[l]
            h1 = (n // 128 + 1) // 2 * 128
            self.ffn(l, [(self.X[:, :, 0:h1], h1, 0)])
            segs = [(self.X[:, :, h1:n], n - h1, 0)]
            if not last:
                segs.append((self.XC, NCTX, 1))
            self.ffn(l, segs)
            if stop == "f%d" % l:
                break
        self.write_out(final=(stop == ""))
        return self.nc


def _na_tables():
    out = {}
    for rev in (0, 1):
        dr_i = np.zeros((128, 15, 128), np.int64)
        dc_i = np.zeros((128, 15, 128), np.int64)
        msk = np.zeros((128, 15, 128), np.float32)
        kp = np.arange(128)
        qi = np.arange(128)
        kr2, kcl = kp // 64, kp % 64
        qr2, qcl = qi // 64, qi % 64
        for typ, a in ((0, 0), (1, 1), (2, 4)):
            cs = max(a - 2, 0)
            for jj in range(5):
                krl = 2 * (cs + jj) + kr2[:, None]
                qrl = 2 * a + qr2[None, :]
                kc_ = kcl[:, None] + 0 * qrl
                qc_ = qcl[None, :] + 0 * krl
                if rev:
                    r, c, kr, kc = 63 - qrl, 63 - qc_, 63 - krl, 63 - kc_
                else:
                    r, c, kr, kc = qrl, qc_, krl, kc_
                r = r + 0 * kr
                kr = kr + 0 * r
                rs = np.clip(r - 4, 0, 56)
                cst = np.clip(c - 8, 0, 48)
                ok = (kr >= rs) & (kr < rs + 8) & (kc >= cst) & (kc < cst + 16)
                dr = np.where(ok, kr - r + 7, 0)
                dc = np.clip(kc - c, -15, 15) + 15
                dr_i[:, typ * 5 + jj, :] = dr
                dc_i[:, typ * 5 + jj, :] = dc
                msk[:, typ * 5 + jj, :] = ok
        out[rev] = (dr_i, dc_i, msk)
    return out


def _invc(rev):
    t = np.zeros((128, 3, 4, 8), np.float32)
    for g, wd in enumerate((2, 4, 8, 16)):
        half = wd // 2
        for which, L, locs in ((0, 4096, range(8)), (1, 256, range(8)), (2, 256, range(248, 256))):
            for i, tl in enumerate(locs):
                tg = (L - 1 - tl) if rev else tl
                cnt = min(tg + half, L) - max(tg - half, 0)
                t[:, which, g, i] = np.float32(1.0) / np.float32(cnt)
    return t


_CACHE = {}


def kernel(x, c, ctx, c_ctx, w_mod, b_mod, norm_mix, norm_ffn, w_in_ab, ln_v, w_spatial,
           b_spatial, w_pool, pool_scale, w_out_ab, w_qkv, rpb, w_out_na, w_ffn_in,
           w_ffn_out, norm_final):
    f = lambda a: np.ascontiguousarray(np.asarray(a, dtype=np.float32))
    x, c, ctx, c_ctx = f(x), f(c), f(ctx), f(c_ctx)
    w_mod, b_mod, norm_mix, norm_ffn = f(w_mod), f(b_mod), f(norm_mix), f(norm_ffn)
    w_in_ab, ln_v, w_spatial, b_spatial = f(w_in_ab), f(ln_v), f(w_spatial), f(b_spatial)
    w_pool, pool_scale, w_out_ab, w_qkv = f(w_pool), f(pool_scale), f(w_out_ab), f(w_qkv)
    rpb, w_out_na, w_ffn_in, w_ffn_out, norm_final = f(rpb), f(w_out_na), f(w_ffn_in), f(w_ffn_out), f(norm_final)

    stop = DEBUG_STOP
    if "nc" not in _CACHE:
        _CACHE["nc"] = Prog().build(stop)
    nc = _CACHE["nc"]

    kc = lambda w: w.reshape(8, 128, -1)
    shared = {}
    shared["wmod"] = f(w_mod.reshape(4, 8, 128, 12, 512).transpose(0, 3, 2, 1, 4))
    shared["bmod"] = f(np.repeat(b_mod.reshape(4, 48, 128).transpose(2, 0, 1)[..., None], 2, axis=3))
    shared["nmix"] = f(np.repeat(norm_mix.reshape(4, 8, 128).transpose(2, 0, 1)[..., None], 2, axis=3))
    shared["nffn"] = f(np.repeat(norm_ffn.reshape(4, 8, 128).transpose(2, 0, 1)[..., None], 2, axis=3))
    shared["nfin"] = f(norm_final.reshape(8, 128).T)
    shared["ident"] = np.eye(128, dtype=np.float32)
    shared["winab"] = f(w_in_ab.reshape(2, 8, 128, 3, 512).transpose(0, 3, 2, 1, 4))
    shared["woutab"] = f(w_out_ab.reshape(2, 8, 128, 2, 512).transpose(0, 3, 2, 1, 4))
    shared["lnvB"] = f(np.broadcast_to(ln_v[:, None, :], (2, 128, 512)))
    shared["wpool"] = f(w_pool.transpose(0, 2, 1, 3))
    shared["pscale"] = f(pool_scale.reshape(2, 4, 128).transpose(2, 0, 1))
    shared["wqkv"] = f(w_qkv.reshape(2, 8, 128, 3, 8, 128).transpose(0, 4, 2, 1, 3, 5).reshape(2, 8, 128, 8, 384))
    shared["wona"] = f(w_out_na.reshape(2, 8, 128, 2, 512).transpose(0, 3, 2, 1, 4))
    shared["wfin"] = f(w_ffn_in.reshape(4, 8, 128, 2, 22, 128).transpose(0, 4, 2, 1, 3, 5).reshape(4, 22, 128, 8, 256))
    shared["wfout"] = f(w_ffn_out.reshape(4, 22, 128, 1024))
    nat = _na_tables()
    per_rev = {}
    for rev in (0, 1):
        d = {}
        ws = w_spatial[:, :, ::-1, ::-1] if rev else w_spatial
        d["wsT"] = f(ws.transpose(0, 3, 1, 2))
        bs = b_spatial[:, :, ::-1] if rev else b_spatial
        d["bsB"] = f(np.broadcast_to(bs[:, None, :, :], (2, 128, 4, 128)))
        d["invc"] = _invc(rev)
        d["alpha"] = np.full((128, 1), 0.0 if rev else 1.0, np.float32)
        dr_i, dc_i, msk = nat[rev]
        d["btab"] = f(rpb[:, :, dr_i, dc_i])
        d["mtab"] = msk
        per_rev[rev] = d
    in_maps = []
    for core in range(8):
        b, rev = core // 2, core % 2
        m = dict(shared)
        m.update(per_rev[rev])
        if rev:
            xs = x[b, ::-1][:NTOK]
            cs_ = ctx[b, ::-1]
        else:
            xs = x[b, :NTOK]
            cs_ = ctx[b]
        m["xT"] = f(xs.T.reshape(8, 128, NTOK))
        m["cT"] = f(cs_.T.reshape(8, 128, NCTX))
        m["cc"] = f(np.stack([c[b], c_ctx], axis=1).reshape(8, 128, 2).transpose(1, 0, 2))
        in_maps.append(m)
    res = run_bass_kernel_spmd(nc, in_maps, core_ids=list(range(8)))
    out = np.zeros((4, 4096, D), np.float32)
    for core in range(8):
        b, rev = core // 2, core % 2
        o = np.asarray(res.results[core]["out"], dtype=np.float32).reshape(D, NOUT).T
        if rev:
            out[b, 2048:] = o[::-1]
        else:
            out[b, :2048] = o
    return out
```

```python
import contextlib
import numpy as np
import concourse.bass as bass
import concourse.mybir as mybir

F32 = mybir.dt.float32
BF16 = mybir.dt.bfloat16
AF = mybir.ActivationFunctionType
ALU = mybir.AluOpType
AX = mybir.AxisListType

NLANES = 24
NHW = 10
GRAN = 64
_DTSIZE = {F32: 4, BF16: 2}


class Sched:
    def __init__(self, nc, sb_bytes, ps_bytes=16384):
        self.nc = nc
        self.es = contextlib.ExitStack()
        self.eng = {"pe": nc.tensor, "act": nc.scalar, "dve": nc.vector,
                    "pool": nc.gpsimd, "sp": nc.sync}
        names = ["pe", "act", "dve", "pool"] + [f"ln{i}" for i in range(NLANES)]
        self.names = names
        self.idx = {n: i for i, n in enumerate(names)}
        self.NE = len(names)
        self.sems = [self.es.enter_context(nc.semaphore("s_" + n)) for n in names]
        self.cnt = np.zeros(self.NE, np.int64)
        self.seen = {e: np.zeros(self.NE, np.int64) for e in self.eng}
        self.snap = {}
        self.seen["pe"][self.idx["pe"]] = 1 << 60
        self.lane_rr = 0
        self.lane_rr_sw = 0
        self.sb = self.es.enter_context(nc.sbuf_tensor("arena_sb", [128, sb_bytes // 4], F32))
        self.ps = self.es.enter_context(nc.psum_tensor("arena_ps", [128, ps_bytes // 4], F32))
        self.trk = {
            "arena_sb": (np.zeros((sb_bytes // GRAN + 1, self.NE), np.int64),
                         np.zeros((sb_bytes // GRAN + 1, self.NE), np.int64)),
            "arena_ps": (np.zeros((ps_bytes // GRAN + 1, self.NE), np.int64),
                         np.zeros((ps_bytes // GRAN + 1, self.NE), np.int64)),
        }
        self.sb_top = 0
        self.sb_bytes = sb_bytes
        self.blk_cache = {}
        self.n_wait = 0
        self.n_ins = 0

    def alloc(self, nbytes, align=64):
        off = (self.sb_top + align - 1) // align * align
        assert off + nbytes <= self.sb_bytes, f"SBUF overflow {off + nbytes} > {self.sb_bytes}"
        self.sb_top = off + nbytes
        self.sb_max = max(getattr(self, 'sb_max', 0), self.sb_top)
        return off

    def view(self, off, shape, dt):
        n = int(np.prod(shape))
        nb = n * _DTSIZE[dt]
        assert off % 4 == 0 and nb % 4 == 0
        v = self.sb[:, off // 4:(off + nb) // 4]
        if dt != F32:
            v = v.bitcast(dt)
        if len(shape) == 2:
            v = v.rearrange("p (a b) -> p a b", a=shape[0])
        elif len(shape) == 3:
            v = v.rearrange("p (a b c) -> p a b c", a=shape[0], b=shape[1])
        return v

    def new(self, shape, dt):
        n = int(np.prod(shape)) * _DTSIZE[dt]
        n = (n + 3) // 4 * 4
        off = self.alloc(n)
        return self.view(off, shape, dt)

    def psum(self, bank, cols=512, dt=F32, col0=0):
        v = self.ps[:, bank * 512 + col0: bank * 512 + col0 + cols]
        return v

    def _blocks(self, ap):
        name = ap.tensor.name
        if name not in self.trk:
            return None, None
        key = (name, ap.offset, ap.ap, ap.dtype)
        r = self.blk_cache.get(key)
        if r is None:
            es = _DTSIZE[ap.dtype]
            dims = ap.ap
            pstride = dims[0][0]
            off = (ap.offset % pstride) * es
            inner = [(s * es, c) for s, c in dims[1:]]
            starts = np.array([off], np.int64)
            run = es
            if inner:
                s_last, c_last = inner[-1]
                if s_last == es:
                    run = es * c_last
                    inner = inner[:-1]
                for s, c in inner:
                    starts = (starts[:, None] + (np.arange(c, dtype=np.int64) * s)[None, :]).ravel()
            lo = starts // GRAN
            hi = (starts + run - 1) // GRAN
            if len(starts) == 1:
                r = np.arange(lo[0], hi[0] + 1)
            else:
                r = np.unique(np.concatenate([np.arange(a, b + 1) for a, b in zip(lo, hi)]))
            self.blk_cache[key] = r
        return self.trk[name], r

    def _need(self, reads, writes):
        need = np.zeros(self.NE, np.int64)
        for ap in reads:
            t, b = self._blocks(ap)
            if t is not None:
                np.maximum(need, t[0][b].max(0), out=need)
        for ap in writes:
            t, b = self._blocks(ap)
            if t is not None:
                np.maximum(need, t[0][b].max(0), out=need)
                np.maximum(need, t[1][b].max(0), out=need)
        return need

    def _do_waits(self, e, need):
        seen = self.seen[e]
        eng = self.eng[e]
        for j in np.argsort(-(need - seen)):
            if need[j] > seen[j]:
                eng.wait_ge(self.sems[j], int(need[j]))
                self.n_wait += 1
                seen[j] = need[j]
                sn = self.snap.get((int(j), int(need[j])))
                if sn is not None:
                    np.maximum(seen, sn, out=seen)

    def _record(self, ei, val, reads, writes):
        for ap in reads:
            t, b = self._blocks(ap)
            if t is not None:
                t[1][b, ei] = val
        for ap in writes:
            t, b = self._blocks(ap)
            if t is not None:
                t[0][b, ei] = val

    def op(self, e, fn, reads, writes, inc=True):
        ei = self.idx[e]
        need = self._need(reads, writes)
        self._do_waits(e, need)
        ins = fn()
        self.n_ins += 1
        val = int(self.cnt[ei]) + 1
        self._record(ei, val, reads, writes)
        if inc:
            ins.then_inc(self.sems[ei], 1)
            self.cnt[ei] = val
            sn = self.seen[e].copy()
            sn[ei] = val
            self.snap[(ei, val)] = sn
        return ins

    def dma(self, q, out, in_):
        if q == "pool":
            lane = 4 + NHW + self.lane_rr_sw
            self.lane_rr_sw = (self.lane_rr_sw + 1) % (NLANES - NHW)
        else:
            lane = 4 + self.lane_rr
            self.lane_rr = (self.lane_rr + 1) % NHW
        need = self._need([in_], [out])
        need[lane] = max(need[lane], self.cnt[lane])
        self._do_waits(q, need)
        val = int(self.cnt[lane]) + 16
        ins = self.eng[q].dma_start(out=out, in_=in_)
        ins.then_inc(self.sems[lane], 16)
        self.n_ins += 1
        self.cnt[lane] = val
        self._record(lane, val, [in_], [out])
        self.snap[(lane, val)] = self.seen[q].copy()
        return (lane, val)

    def wait_all_dma(self, e="sp"):
        need = np.zeros(self.NE, np.int64)
        need[4:] = self.cnt[4:]
        self._do_waits(e, need)

    def mm(self, out, lhsT, rhs, start, stop, inc=None):
        return self.op("pe", lambda: self.nc.tensor.matmul(out, lhsT, rhs, start=start, stop=stop),
                       [lhsT, rhs], [out], inc=(stop if inc is None else inc))

    def transpose(self, out, in_, ident):
        return self.op("pe", lambda: self.nc.tensor.transpose(out, in_, ident), [in_, ident], [out])

    def act(self, out, in_, func, bias=None, scale=None, accum_out=None, e="act"):
        kw = {}
        reads = [in_]
        writes = [out]
        if bias is not None:
            kw["bias"] = bias
            if not isinstance(bias, (int, float)):
                reads.append(bias)
        if scale is not None:
            kw["scale"] = scale
            if not isinstance(scale, (int, float)):
                reads.append(scale)
        if accum_out is not None:
            kw["accum_out"] = accum_out
            writes.append(accum_out)
        return self.op("act", lambda: self.nc.scalar.activation(out, in_, func, **kw), reads, writes)

    def tt(self, e, out, in0, in1, op):
        return self.op(e, lambda: self.eng[e].tensor_tensor(out, in0, in1, op), [in0, in1], [out])

    def ts(self, e, out, in0, s1, s2, op0, op1=None, accum_out=None):
        reads = [in0] + [s for s in (s1, s2) if s is not None and not isinstance(s, (int, float))]
        writes = [out] + ([accum_out] if accum_out is not None else [])
        kw = {}
        if op1 is not None:
            kw["op1"] = op1
        if accum_out is not None:
            kw["accum_out"] = accum_out
        return self.op(e, lambda: self.eng[e].tensor_scalar(out, in0, s1, s2, op0, **kw), reads, writes)

    def stt(self, e, out, in0, scalar, in1, op0, op1):
        reads = [in0, in1] + ([] if isinstance(scalar, (int, float)) else [scalar])
        return self.op(e, lambda: self.eng[e].scalar_tensor_tensor(out, in0, scalar, in1, op0, op1),
                       reads, [out])

    def copy(self, e, out, in_):
        if e == "act":
            return self.op(e, lambda: self.nc.scalar.copy(out, in_), [in_], [out])
        return self.op(e, lambda: self.eng[e].tensor_copy(out, in_), [in_], [out])

    def memset(self, e, out, val):
        return self.op(e, lambda: self.eng[e].memset(out, val), [], [out])

    def recip(self, out, in_):
        return self.op("dve", lambda: self.nc.vector.reciprocal(out, in_), [in_], [out])


import os
from concourse.bass_utils import run_bass_kernel_spmd

D = 1024
NTOK = 2816
NOUT = 2048
NCTX = 256
EPS = 1e-6
DFF = 2816
SB_BYTES = 212800
NRING = 3
DEBUG_STOP = os.environ.get("MK_STOP", "")

AB_OUT = {0: 2688, 2: 2304}
AB_IN = {0: 2816, 2: 2432}
NA_PAIRS = {1: 19, 3: 16}
NA_NCH = {1: 21, 3: 18}
FFN_N = {0: 2688, 1: 2432, 2: 2304, 3: 2048}


def _split(n, parts):
    base, rem = divmod(n, parts)
    out = []
    s = 0
    for i in range(parts):
        e = s + base + (1 if i < rem else 0)
        out.append((s, e))
        s = e
    return out


def _tiles(n, w=512):
    return [(t, min(w, n - t)) for t in range(0, n, w)]


class Prog:
    def __init__(self):
        nc = bass.Bass("TRN2", target_bir_lowering=False)
        self.nc = nc
        dt = lambda name, shape, kind="ExternalInput": nc.dram_tensor(name, list(shape), F32, kind=kind).ap()
        self.d_x = dt("xT", [8, 128, NTOK])
        self.d_c = dt("cT", [8, 128, NCTX])
        self.d_cc = dt("cc", [128, 8, 2])
        self.d_wmod = dt("wmod", [4, 12, 128, 8, 512])
        self.d_bmod = dt("bmod", [128, 4, 48, 2])
        self.d_nmix = dt("nmix", [128, 4, 8, 2])
        self.d_nffn = dt("nffn", [128, 4, 8, 2])
        self.d_nfin = dt("nfin", [128, 8])
        self.d_ident = dt("ident", [128, 128])
        self.d_winab = dt("winab", [2, 3, 128, 8, 512])
        self.d_woutab = dt("woutab", [2, 2, 128, 8, 512])
        self.d_wsT = dt("wsT", [2, 128, 4, 128])
        self.d_bsB = dt("bsB", [2, 128, 4, 128])
        self.d_lnvB = dt("lnvB", [2, 128, 512])
        self.d_wpool = dt("wpool", [2, 128, 4, 128])
        self.d_pscale = dt("pscale", [128, 2, 4])
        self.d_invc = dt("invc", [128, 3, 4, 8])
        self.d_alpha = dt("alpha", [128, 1])
        self.d_wqkv = dt("wqkv", [2, 8, 128, 8, 384])
        self.d_wona = dt("wona", [2, 2, 128, 8, 512])
        self.d_btab = dt("btab", [2, 16, 128, 15, 128])
        self.d_mtab = dt("mtab", [128, 15, 128])
        self.d_wfin = dt("wfin", [4, 22, 128, 8, 256])
        self.d_wfout = dt("wfout", [4, 22, 128, 1024])
        self.d_out = dt("out", [8, 128, NOUT], kind="ExternalOutput")

        S = Sched(nc, SB_BYTES)
        self.S = S
        self.pbc = 0
        self.ringc = 0
        self.bg = []
        self.X = S.new([8, NTOK], F32)
        self.XC = S.new([8, NCTX], F32)
        self.RING = [S.new([4096], BF16) for _ in range(NRING)]
        self.MOD = S.new([4, 48, 2], F32)
        self.A1 = S.new([4, 8, 2], F32)
        self.A2 = S.new([4, 8, 2], F32)
        self.ONES = S.new([128], BF16)
        self.IDENT = S.new([128], BF16)
        self.NFIN = S.new([8], F32)
        self.ALPHA = S.new([1], F32)
        self.INVC = S.new([3, 4, 8], F32)
        self.PSC = S.new([2, 4], F32)
        self.MTAB = S.new([15, 128], BF16)
        self.WST = S.new([4, 128], BF16)
        self.BSB = S.new([512], F32)
        self.LNVB = S.new([512], F32)
        self.WPOOL = S.new([4, 128], BF16)
        self.HPREV = S.new([8, 256], BF16)
        self.SCR0 = S.sb_top

    def pb(self, cols=512):
        b = self.pbc % 7
        self.pbc += 1
        return self.S.ps[:, b * 512: b * 512 + cols]

    def pb2(self, cols):
        while (self.pbc % 7) not in (0, 2, 4):
            self.pbc += 1
        b = self.pbc % 7
        self.pbc += 2
        return self.S.ps[:, b * 512: b * 512 + cols]

    def bg_step(self, n=1):
        for _ in range(n):
            if self.bg:
                self.bg[0]()
                self.bg.pop(0)

    def bg_flush(self):
        while self.bg:
            self.bg_step()

    def ring(self, bg=False):
        if bg:
            return self.RING[NRING - 1]
        nr = NRING - 1 if self.bg else NRING
        r = self.RING[self.ringc % nr]
        self.ringc += 1
        return r

    def load_w(self, src, k, n, bg=False):
        slot = self.ring(bg)[:, 0:k * n].rearrange("p (k n) -> p k n", k=k)
        self.S.dma("pool", slot, src)
        return slot

    def norm_tmp(self):
        S = self.S
        self.SQ = [S.new([512], BF16) for _ in range(2)]
        self.RS = S.new([512], F32)
        self.TMPN = [S.new([512], F32) for _ in range(2)]

    def norm_h(self, A, B, j, src, dst, n):
        S = self.S
        for (t0, w) in _tiles(n):
            ps = self.pb()[:, :w]
            for k in range(8):
                sq = self.SQ[k % 2][:, :w]
                S.act(sq, src[:, k, t0:t0 + w], AF.Square)
                S.mm(ps, self.ONES, sq, k == 0, k == 7, inc=True)
            r = self.RS[:, :w]
            S.ts("dve", r, ps, 1024.0 * EPS, None, ALU.add)
            S.act(r, r, AF.Sqrt)
            S.recip(r, r)
            for k in range(8):
                tmp = self.TMPN[k % 2][:, :w]
                S.tt("dve", tmp, src[:, k, t0:t0 + w], r, ALU.mult)
                if B is None:
                    S.act(dst[:, k, t0:t0 + w], tmp, AF.Identity, scale=A[:, k, j:j + 1])
                else:
                    S.act(dst[:, k, t0:t0 + w], tmp, AF.Identity, bias=B[:, k, j:j + 1], scale=A[:, k, j:j + 1])

    def prologue(self):
        S = self.S
        nc = self.nc
        S.sb_top = self.SCR0
        for k in range(8):
            S.dma("sp", self.X[:, k, :], self.d_x[k])
            S.dma("sp", self.XC[:, k, :], self.d_c[k])
        ccs = S.new([8, 2], F32)
        csil = S.new([8, 2], BF16)
        bmod = S.new([4, 48, 2], F32)
        nmix = S.new([4, 8, 2], F32)
        nffn = S.new([4, 8, 2], F32)
        onesf = S.new([128], F32)
        S.dma("sp", ccs, self.d_cc)
        S.dma("sp", bmod, self.d_bmod)
        S.dma("sp", nmix, self.d_nmix)
        S.dma("sp", nffn, self.d_nffn)
        S.dma("sp", self.NFIN, self.d_nfin)
        S.dma("sp", self.ALPHA, self.d_alpha)
        S.dma("sp", self.INVC, self.d_invc)
        S.dma("sp", self.PSC, self.d_pscale)
        S.dma("pool", self.IDENT, self.d_ident)
        S.dma("pool", self.MTAB, self.d_mtab)
        S.memset("dve", onesf, 1.0)
        S.copy("dve", self.ONES, onesf)
        S.act(csil, ccs, AF.Silu)
        self.p_csil, self.p_bmod, self.p_nmix, self.p_nffn = csil, bmod, nmix, nffn
        self.SCR0 = S.sb_top
        for t in self.mod_tasks(0, bg=False):
            t()
        S.ts("dve", self.NFIN, self.NFIN, 32.0, None, ALU.mult)

    def mod_tasks(self, l, bg=True):
        S = self.S
        ps = S.ps[:, 7 * 512:7 * 512 + 96]
        csil, bmod = self.p_csil, self.p_bmod

        def piece(n):
            def f():
                slot = self.load_w(self.d_wmod[l, n], 8, 512, bg=bg)
                for mi in range(4):
                    m = n * 4 + mi
                    for k in range(8):
                        S.mm(ps[:, m * 2:(m + 1) * 2], slot[:, k, mi * 128:(mi + 1) * 128], csil[:, k, :], k == 0, k == 7)
            return f

        def fin():
            S.tt("dve", self.MOD[:, l].rearrange("p a b -> p (a b)"), ps, bmod[:, l].rearrange("p a b -> p (a b)"), ALU.add)
            for (Adst, gain, c0) in ((self.A1, self.p_nmix, 8), (self.A2, self.p_nffn, 32)):
                S.stt("dve", Adst[:, l], self.MOD[:, l, c0:c0 + 8, :], 1.0, gain[:, l], ALU.add, ALU.mult)
                S.ts("dve", Adst[:, l], Adst[:, l], 32.0, None, ALU.mult)
        return [piece(n) for n in range(12)] + [fin]

    def ffn(self, l, seglists):
        S = self.S
        S.sb_top = self.SCR0
        self.norm_tmp()
        nmax = max(sum(sg[1] for sg in segs) for segs in seglists)
        H = S.new([8, nmax], BF16)
        HID = S.new([8, nmax], BF16)
        SA = [S.new([512], F32) for _ in range(2)]
        B2 = self.MOD[:, l, 24:32, :]
        G2 = self.MOD[:, l, 40:48, :]

        def prep(segs):
            off = 0
            tl = []
            for (res, n, j) in segs:
                self.norm_h(self.A2[:, l], B2, j, res, H[:, :, off:off + n], n)
                for (t0, w) in _tiles(n):
                    tl.append((off + t0, w, res, t0, j))
                off += n
            return tl

        it = 0
        blocks = ((0, 8), (8, 15), (15, 22))
        tl_next = prep(seglists[0])
        for si in range(len(seglists)):
            tl = tl_next
            for bi, (j0, j1) in enumerate(blocks):
                for jh in range(j0, j1):
                    slot = self.load_w(self.d_wfin[l, jh], 8, 256)
                    jl = jh - j0
                    for (c0, w, res, t0, j) in tl:
                        pa = self.pb()[:, :w]
                        pg = self.pb()[:, :w]
                        for k in range(8):
                            S.mm(pa, slot[:, k, 0:128], H[:, k, c0:c0 + w], k == 0, k == 7)
                        for k in range(8):
                            S.mm(pg, slot[:, k, 128:256], H[:, k, c0:c0 + w], k == 0, k == 7)
                        sa = SA[it % 2][:, :w]
                        it += 1
                        S.act(sa, pa, AF.Silu)
                        S.tt("dve", HID[:, jl, c0:c0 + w], sa, pg, ALU.mult)
                if bi == len(blocks) - 1 and si + 1 < len(seglists):
                    tl_next = prep(seglists[si + 1])
                nb = j1 - j0
                wo = []
                for q0 in range(0, nb, 4):
                    qn = min(4, nb - q0)
                    slot = self.ring()[:, 0:qn * 1024].rearrange("p (k n) -> p k n", k=qn)
                    S.dma("pool", slot, self.d_wfout[l, j0 + q0:j0 + q0 + qn].rearrange("j p n -> p j n"))
                    for qi in range(qn):
                        wo.append(slot[:, qi, :])
                for m in range(8):
                    for (c0, w, res, t0, j) in tl:
                        py = self.pb()[:, :w]
                        for jl in range(nb):
                            S.mm(py, wo[jl][:, m * 128:(m + 1) * 128], HID[:, jl, c0:c0 + w], jl == 0, jl == nb - 1)
                        rr = res[:, m, t0:t0 + w]
                        S.stt("dve", rr, py, G2[:, m, j:j + 1], rr, ALU.mult, ALU.add)

    def ab_tables(self, e):
        S = self.S
        S.dma("pool", self.WST, self.d_wsT[e])
        S.dma("sp", self.BSB.rearrange("p (a b) -> p a b", a=4), self.d_bsB[e])
        S.dma("sp", self.LNVB, self.d_lnvB[e])
        S.dma("pool", self.WPOOL, self.d_wpool[e])

    def ab_mixer(self, l, e, Xb, c0, c1, hs, he, j, fix_head, fix_tail):
        S = self.S
        nc = self.nc
        S.sb_top = self.SCR0
        self.norm_tmp()
        nh = he - hs
        n = c1 - c0
        o = c0 - hs
        H = S.new([8, nh], BF16)
        YA = S.new([4, n], BF16)
        YB = S.new([4, n], BF16)
        PB = [S.new([nh + 16], F32) for _ in range(2)]
        Aa = S.new([nh + 16], F32)
        Ab = S.new([nh + 16], F32)
        DF = S.new([n], BF16)
        VT = [S.new([512], F32) for _ in range(3)]
        CEN = [S.new([512], F32) for _ in range(3)]
        VH = [S.new([512], BF16) for _ in range(3)]
        SQJ = S.new([512], F32)
        SM = [S.new([4], F32) for _ in range(3)]
        T8 = S.new([8], F32)
        B1 = self.MOD[:, l, 0:8, :]
        G1 = self.MOD[:, l, 16:24, :]
        if o > 0:
            S.copy("act", H[:, :, 0:o], self.HPREV[:, :, 256 - o:256])
        self.norm_h(self.A1[:, l], B1, j, Xb[:, :, c0:he], H[:, :, o:nh], nh - o)
        if j == 0:
            S.copy("act", self.HPREV[:, :, 128:256], H[:, :, o + n - 128:o + n])
        slot_p = self.load_w(self.d_winab[e, 2], 8, 512)
        slot_u = self.load_w(self.d_winab[e, 0], 8, 512)

        def u_proj(m):
            for (t0, w) in _tiles(n):
                ps = self.pb()[:, :w]
                for k in range(8):
                    S.mm(ps, slot_u[:, k, m * 128:(m + 1) * 128], H[:, k, o + t0:o + t0 + w], k == 0, k == 7)
                S.act(YA[:, m, t0:t0 + w], ps, AF.Gelu_apprx_tanh)

        def p_proj(g):
            P = PB[g % 2]
            S.memset("dve", P[:, 0:8], 0.0)
            S.memset("dve", P[:, 8 + nh:16 + nh], 0.0)
            for (t0, w) in _tiles(nh):
                ps = self.pb()[:, :w]
                for k in range(8):
                    S.mm(ps, slot_p[:, k, g * 128:(g + 1) * 128], H[:, k, t0:t0 + w], k == 0, k == 7)
                S.copy("act", P[:, 8 + t0:8 + t0 + w], ps)

        def pooling(g):
            P = PB[g % 2]
            wd = 2 ** (g + 1)
            half = wd // 2
            cur = P
            length = nh + 16
            step = 1
            bufs = [Aa, Ab]
            bi = 0
            while step < wd:
                nxt = bufs[bi]
                bi ^= 1
                L2 = length - step
                S.tt("dve", nxt[:, 0:L2], cur[:, 0:L2], cur[:, step:step + L2], ALU.add)
                cur = nxt
                length = L2
                step *= 2
            i0 = o + 8
            Dd = bufs[bi][:, 0:n]
            fw_ = cur[:, i0 - half:i0 - half + n]
            rv_ = cur[:, i0 - half + 1:i0 - half + 1 + n]
            S.tt("dve", Dd, fw_, rv_, ALU.subtract)
            S.stt("dve", Dd, Dd, self.ALPHA[:, 0:1], rv_, ALU.mult, ALU.add)
            S.stt("dve", DF, Dd, 1.0 / wd, P[:, i0:i0 + n], ALU.mult, ALU.subtract)
            if fix_head is not None:
                S.tt("dve", T8, Dd[:, 0:8], self.INVC[:, fix_head, g, :], ALU.mult)
                S.tt("dve", DF[:, 0:8], T8, P[:, i0:i0 + 8], ALU.subtract)
            if fix_tail is not None:
                S.tt("dve", T8, Dd[:, n - 8:n], self.INVC[:, fix_tail, g, :], ALU.mult)
                S.tt("dve", DF[:, n - 8:n], T8, P[:, i0 + n - 8:i0 + n], ALU.subtract)

        def yb_proj(g):
            for (t0, w) in _tiles(n):
                ps = self.pb()[:, :w]
                S.mm(ps, self.WPOOL[:, g, :], DF[:, t0:t0 + w], True, True)
                S.act(YB[:, g, t0:t0 + w], ps, AF.Identity, scale=self.PSC[:, e, g:g + 1])

        p_proj(0)
        for g in range(4):
            if j == 0:
                self.bg_step()
            u_proj(g)
            if g < 3:
                p_proj(g + 1)
            pooling(g)
            yb_proj(g)

        slot_v = self.load_w(self.d_winab[e, 1], 8, 512)
        nchunk = n // 128

        def v_a(ci):
            tc = o + ci * 128
            ps = self.pb()
            for k in range(8):
                S.mm(ps, H[:, k, tc:tc + 128], slot_v[:, k, :], k == 0, k == 7)
            S.act(VT[ci % 3], ps, AF.Gelu_apprx_tanh)

        def v_b1(ci):
            vt = VT[ci % 3]
            cen = CEN[ci % 3]
            vh = VH[ci % 3]
            sm = SM[ci % 3]
            S.op("dve", lambda: nc.vector.reduce_sum(sm[:, 0:1], vt, AX.X), [vt], [sm[:, 0:1]])
            S.ts("dve", sm[:, 1:2], sm[:, 0:1], -1.0 / 512.0, None, ALU.mult)
            S.ts("dve", cen, vt, sm[:, 1:2], None, ALU.add)
            S.act(SQJ, cen, AF.Square)
            S.op("dve", lambda: nc.vector.reduce_sum(sm[:, 2:3], SQJ, AX.X), [SQJ], [sm[:, 2:3]])
            S.ts("dve", sm[:, 3:4], sm[:, 2:3], 1.0 / 512.0, EPS, ALU.mult, ALU.add)
            S.act(sm[:, 3:4], sm[:, 3:4], AF.Sqrt)
            S.recip(sm[:, 3:4], sm[:, 3:4])
            S.stt("dve", vh, cen, sm[:, 3:4], self.LNVB, ALU.mult, ALU.mult)

        def v_b2(ci):
            cen = CEN[ci % 3]
            vh = VH[ci % 3]
            psg = self.pb()
            for g in range(4):
                S.mm(psg[:, g * 128:(g + 1) * 128], vh[:, g * 128:(g + 1) * 128], self.WST[:, g, :], True, True)
            S.tt("dve", cen, psg, self.BSB, ALU.add)
            ya = YA[:, :, ci * 128:(ci + 1) * 128]
            S.tt("dve", ya, cen.rearrange("p (a b) -> p a b", a=4), ya, ALU.mult)

        for i in range(nchunk + 2):
            if i < nchunk:
                v_a(i)
            if 1 <= i <= nchunk:
                v_b1(i - 1)
            if i >= 2:
                v_b2(i - 2)
        for hf in range(2):
            slot = self.load_w(self.d_woutab[e, hf], 8, 512)
            for mi in range(4):
                m = hf * 4 + mi
                for (t0, w) in _tiles(n):
                    ps = self.pb()[:, :w]
                    for k in range(8):
                        rhs = (YA if k < 4 else YB)[:, k % 4, t0:t0 + w]
                        S.mm(ps, slot[:, k, mi * 128:(mi + 1) * 128], rhs, k == 0, k == 7)
                    rr = Xb[:, m, c0 + t0:c0 + t0 + w]
                    S.stt("dve", rr, ps, G1[:, m, j:j + 1], rr, ALU.mult, ALU.add)

    def na_mixer(self, l, o_, a0, a1, do_ctx_q):
        S = self.S
        nc = self.nc
        S.sb_top = self.SCR0
        NCH = NA_NCH[l]
        cs = lambda a: min(max(a - 2, 0), NCH - 5)
        k0 = cs(a0)
        k1 = cs(a1 - 1) + 5
        nkc = k1 - k0
        nk = nkc * 128
        npair = a1 - a0
        nq = npair * 128
        qoff = (a0 - k0) * 128
        H = S.new([8, nk], BF16)
        HC = S.new([8, NCTX], BF16)
        OT = S.new([8, nq], BF16)
        OTC = S.new([8, NCTX], BF16) if do_ctx_q else None
        QT = S.new([nq], BF16)
        KT = S.new([nk], BF16)
        V1 = S.new([nkc, 2, 65], BF16)
        QC = S.new([NCTX], BF16)
        KC = S.new([NCTX], BF16)
        VC = S.new([2, 2, 65], BF16)
        B1 = self.MOD[:, l, 0:8, :]
        G1 = self.MOD[:, l, 16:24, :]
        mark = S.sb_top
        self.norm_tmp()
        lh = (a0 - k0) * 128
        if lh > 0:
            S.copy("act", H[:, :, 0:lh], self.HPREV[:, :, 256 - lh:256])
        self.norm_h(self.A1[:, l], B1, 0, self.X[:, :, a0 * 128:k1 * 128], H[:, :, lh:nk], nk - lh)
        S.copy("act", self.HPREV, H[:, :, (a1 - 2 - k0) * 128:(a1 - k0) * 128])
        self.norm_h(self.A1[:, l], B1, 1, self.XC, HC, NCTX)
        S.sb_top = mark
        OTOK = S.new([npair, 128], BF16)
        OTOKC = S.new([2, 128], BF16)
        PT = [S.new([7, 128], BF16) for _ in range(3)]
        tbase = 0 if a0 < 2 else 2
        ntyp = 3 - tbase
        E = [S.new([5 * ntyp, 128], BF16) for _ in range(2)]
        VTF = S.new([nk], BF16)
        VCF = S.new([NCTX], BF16)
        RINV = [S.new([1], F32) for _ in range(3)]
        for c in range(nkc):
            S.memset("dve", V1[:, c, :, 64:65], 1.0)
        for c in range(2):
            S.memset("dve", VC[:, c, :, 64:65], 1.0)
        it = 0
        for hp in range(8):
            self.bg_step()
            slot = self.load_w(self.d_wqkv[o_, hp], 8, 384)
            for (t0, w) in _tiles(nq):
                ps = self.pb()[:, :w]
                for k in range(8):
                    S.mm(ps, slot[:, k, 0:128], H[:, k, qoff + t0:qoff + t0 + w], k == 0, k == 7)
                S.copy("act", QT[:, t0:t0 + w], ps)
            for (t0, w) in _tiles(nk):
                ps = self.pb()[:, :w]
                for k in range(8):
                    S.mm(ps, slot[:, k, 128:256], H[:, k, t0:t0 + w], k == 0, k == 7)
                S.copy("act", KT[:, t0:t0 + w], ps)
            for (t0, w) in _tiles(nk):
                ps = self.pb()[:, :w]
                for k in range(8):
                    S.mm(ps, slot[:, k, 256:384], H[:, k, t0:t0 + w], k == 0, k == 7)
                S.copy("act", VTF[:, t0:t0 + w], ps)
            for c in range(nkc):
                pst = self.pb()[:, 0:64].bitcast(BF16)
                S.transpose(pst, VTF[:, c * 128:(c + 1) * 128], self.IDENT)
                S.copy("dve", V1[:, c, :, 0:64], pst.rearrange("p (a b) -> p a b", a=2))
            ps = self.pb()[:, :NCTX]
            for k in range(8):
                S.mm(ps, slot[:, k, 128:256], HC[:, k, :], k == 0, k == 7)
            S.copy("act", KC, ps)
            ps = self.pb()[:, :NCTX]
            for k in range(8):
                S.mm(ps, slot[:, k, 256:384], HC[:, k, :], k == 0, k == 7)
            S.copy("act", VCF, ps)
            for c in range(2):
                pst = self.pb()[:, 0:64].bitcast(BF16)
                S.transpose(pst, VCF[:, c * 128:(c + 1) * 128], self.IDENT)
                S.copy("dve", VC[:, c, :, 0:64], pst.rearrange("p (a b) -> p a b", a=2))
            if do_ctx_q:
                ps = self.pb()[:, :NCTX]
                for k in range(8):
                    S.mm(ps, slot[:, k, 0:128], HC[:, k, :], k == 0, k == 7)
                S.copy("act", QC, ps)
            for hh in range(2):
                S.dma("pool", E[hh], self.d_btab[o_, hp * 2 + hh][:, tbase * 5:15, :])
                S.act(E[hh], E[hh], AF.Exp)
                S.tt("dve", E[hh], E[hh], self.MTAB[:, tbase * 5:15, :], ALU.mult)
            blocks = [("x", a) for a in range(a0, a1)]
            if do_ctx_q:
                blocks += [("c", 0), ("c", 1)]
            its = [(kind, a, hh) for (kind, a) in blocks for hh in range(2)]
            LA = 2

            def stage1(i):
                kind, a, hh = its[i]
                hsl = slice(hh * 64, hh * 64 + 64)
                pt = PT[i % 3]
                ptf = pt.rearrange("p a b -> p (a b)")
                if kind == "x":
                    typ = min(a, 2)
                    c_s = cs(a) - k0
                    q = QT[hsl, (a - a0) * 128:(a - a0 + 1) * 128]
                    pss = self.pb2(896)
                    for jj in range(5):
                        S.mm(pss[:, jj * 128:(jj + 1) * 128], KT[hsl, (c_s + jj) * 128:(c_s + jj + 1) * 128], q, True, True)
                    for jc in range(2):
                        S.mm(pss[:, (5 + jc) * 128:(6 + jc) * 128], KC[hsl, jc * 128:(jc + 1) * 128], q, True, True)
                    S.act(ptf[:, 0:512], pss[:, 0:512], AF.Exp, scale=0.125)
                    S.act(ptf[:, 512:896], pss[:, 512:896], AF.Exp, scale=0.125)
                    S.tt("dve", pt[:, 0:5, :], pt[:, 0:5, :], E[hh][:, (typ - tbase) * 5:(typ - tbase) * 5 + 5, :], ALU.mult)
                else:
                    q = QC[hsl, a * 128:(a + 1) * 128]
                    pss = self.pb()[:, 0:256]
                    for jc in range(2):
                        S.mm(pss[:, jc * 128:(jc + 1) * 128], KC[hsl, jc * 128:(jc + 1) * 128], q, True, True)
                    S.act(ptf[:, 0:256], pss, AF.Exp, scale=0.125)

            def stage2(i):
                kind, a, hh = its[i]
                pt = PT[i % 3]
                rinv = RINV[i % 3]
                pso = self.pb()[:, 0:65]
                if kind == "x":
                    c_s = cs(a) - k0
                    for jj in range(5):
                        S.mm(pso, pt[:, jj, :], V1[:, c_s + jj, hh, :], jj == 0, False)
                    for jc in range(2):
                        S.mm(pso, pt[:, 5 + jc, :], VC[:, jc, hh, :], False, jc == 1)
                    dst = OTOK[:, a - a0, hh * 64:hh * 64 + 64]
                else:
                    for jc in range(2):
                        S.mm(pso, pt[:, jc, :], VC[:, jc, hh, :], jc == 0, jc == 1)
                    dst = OTOKC[:, a, hh * 64:hh * 64 + 64]
                S.recip(rinv, pso[:, 64:65])
                S.ts("dve", dst, pso[:, 0:64], rinv[:, 0:1], None, ALU.mult)

            for i in range(len(its) + LA):
                if i < len(its):
                    stage1(i)
                if i >= LA:
                    stage2(i - LA)
            for (kind, a) in blocks:
                pst = self.pb()[:, 0:64].bitcast(BF16)
                if kind == "x":
                    S.transpose(pst, OTOK[:, a - a0, :], self.IDENT)
                    S.copy("dve", OT[:, hp, (a - a0) * 128:(a - a0 + 1) * 128], pst)
                else:
                    S.transpose(pst, OTOKC[:, a, :], self.IDENT)
                    S.copy("dve", OTC[:, hp, a * 128:(a + 1) * 128], pst)
        q0 = a0 * 128
        for hf in range(2):
            slot = self.load_w(self.d_wona[o_, hf], 8, 512)
            for mi in range(4):
                m = hf * 4 + mi
                for (t0, w) in _tiles(nq):
                    ps = self.pb()[:, :w]
                    for k in range(8):
                        S.mm(ps, slot[:, k, mi * 128:(mi + 1) * 128], OT[:, k, t0:t0 + w], k == 0, k == 7)
                    rr = self.X[:, m, q0 + t0:q0 + t0 + w]
                    S.stt("dve", rr, ps, G1[:, m, 0:1], rr, ALU.mult, ALU.add)
                if do_ctx_q:
                    ps = self.pb()[:, :NCTX]
                    for k in range(8):
                        S.mm(ps, slot[:, k, mi * 128:(mi + 1) * 128], OTC[:, k, :], k == 0, k == 7)
                    rr = self.XC[:, m, :]
                    S.stt("dve", rr, ps, G1[:, m, 1:2], rr, ALU.mult, ALU.add)

    def write_out(self, final):
        S = self.S
        S.sb_top = self.SCR0
        self.norm_tmp()
        if final:
            ST = [S.new([8, 512], F32) for _ in range(2)]
            for i, (t0, w) in enumerate(_tiles(NOUT)):
                st = ST[i % 2]
                self.norm_h_f32(self.X[:, :, t0:t0 + w], st, w)
                for k in range(8):
                    S.dma("sp", self.d_out[k][:, t0:t0 + w], st[:, k, :w])
        else:
            for k in range(8):
                S.dma("sp", self.d_out[k], self.X[:, k, 0:NOUT])
        S.wait_all_dma("sp")

    def norm_h_f32(self, src, dst, w):
        S = self.S
        ps = self.pb()[:, :w]
        for k in range(8):
            sq = self.SQ[k % 2][:, :w]
            S.act(sq, src[:, k, :], AF.Square)
            S.mm(ps, self.ONES, sq, k == 0, k == 7, inc=True)
        r = self.RS[:, :w]
        S.ts("dve", r, ps, 1024.0 * EPS, None, ALU.add)
        S.act(r, r, AF.Sqrt)
        S.recip(r, r)
        for k in range(8):
            tmp = self.TMPN[k % 2][:, :w]
            S.tt("dve", tmp, src[:, k, :], r, ALU.mult)
            S.act(dst[:, k, :w], tmp, AF.Identity, scale=self.NFIN[:, k:k + 1])

    def build(self, stop=""):
        self.prologue()
        for l in range(4):
            if stop == "p":
                break
            last = l == 3
            if not last:
                self.bg = self.mod_tasks(l + 1)
            if l % 2 == 0:
                e = l // 2
                self.ab_tables(e)
                nout = AB_OUT[l]
                nin = AB_IN[l]
                for (s, t) in _split(nout // 128, 4):
                    c0, c1 = s * 128, t * 128
                    hs = max(c0 - 128, 0)
                    he = min(c1 + 128, nin)
                    self.ab_mixer(l, e, self.X, c0, c1, hs, he, 0, 0 if c0 == 0 else None, None)
                self.ab_mixer(l, e, self.XC, 0, NCTX, 0, NCTX, 1, 1, 2)
            else:
                o_ = l // 2
                sp = _split(NA_PAIRS[l], 3)
                for i, (a0, a1) in enumerate(sp):
                    self.na_mixer(l, o_, a0, a1, (not last) and i == len(sp) - 1)
            self.bg_flush()
            if stop == "m%d" % l:
                break
            n = FFN_N[l]
            h1 = (n // 128 + 1) // 2 * 128
            segs = [(self.X[:, :, h1:n], n - h1, 0)]
            if not last:
                segs.append((self.XC, NCTX, 1))
            self.ffn(l, [[(self.X[:, :, 0:h1], h1, 0)], segs])
            if stop == "f%d" % l:
                break
        self.write_out(final=(stop == ""))
        return self.nc


def _na_tables():
    out = {}
    for rev in (0, 1):
        dr_i = np.zeros((128, 15, 128), np.int64)
        dc_i = np.zeros((128, 15, 128), np.int64)
        msk = np.zeros((128, 15, 128), np.float32)
        kp = np.arange(128)
        qi = np.arange(128)
        kr2, kcl = kp // 64, kp % 64
        qr2, qcl = qi // 64, qi % 64
        for typ, a in ((0, 0), (1, 1), (2, 4)):
            cs = max(a - 2, 0)
            for jj in range(5):
                krl = 2 * (cs + jj) + kr2[:, None]
                qrl = 2 * a + qr2[None, :]
                kc_ = kcl[:, None] + 0 * qrl
                qc_ = qcl[None, :] + 0 * krl
                if rev:
                    r, c, kr, kc = 63 - qrl, 63 - qc_, 63 - krl, 63 - kc_
                else:
                    r, c, kr, kc = qrl, qc_, krl, kc_
                r = r + 0 * kr
                kr = kr + 0 * r
                rs = np.clip(r - 4, 0, 56)
                cst = np.clip(c - 8, 0, 48)
                ok = (kr >= rs) & (kr < rs + 8) & (kc >= cst) & (kc < cst + 16)
                dr = np.where(ok, kr - r + 7, 0)
                dc = np.clip(kc - c, -15, 15) + 15
                dr_i[:, typ * 5 + jj, :] = dr
                dc_i[:, typ * 5 + jj, :] = dc
                msk[:, typ * 5 + jj, :] = ok
        out[rev] = (dr_i, dc_i, msk)
    return out


def _invc(rev):
    t = np.zeros((128, 3, 4, 8), np.float32)
    for g, wd in enumerate((2, 4, 8, 16)):
        half = wd // 2
        for which, L, locs in ((0, 4096, range(8)), (1, 256, range(8)), (2, 256, range(248, 256))):
            for i, tl in enumerate(locs):
                tg = (L - 1 - tl) if rev else tl
                cnt = min(tg + half, L) - max(tg - half, 0)
                t[:, which, g, i] = np.float32(1.0) / np.float32(cnt)
    return t


_CACHE = {}


def kernel(x, c, ctx, c_ctx, w_mod, b_mod, norm_mix, norm_ffn, w_in_ab, ln_v, w_spatial,
           b_spatial, w_pool, pool_scale, w_out_ab, w_qkv, rpb, w_out_na, w_ffn_in,
           w_ffn_out, norm_final):
    f = lambda a: np.ascontiguousarray(np.asarray(a, dtype=np.float32))
    x, c, ctx, c_ctx = f(x), f(c), f(ctx), f(c_ctx)
    w_mod, b_mod, norm_mix, norm_ffn = f(w_mod), f(b_mod), f(norm_mix), f(norm_ffn)
    w_in_ab, ln_v, w_spatial, b_spatial = f(w_in_ab), f(ln_v), f(w_spatial), f(b_spatial)
    w_pool, pool_scale, w_out_ab, w_qkv = f(w_pool), f(pool_scale), f(w_out_ab), f(w_qkv)
    rpb, w_out_na, w_ffn_in, w_ffn_out, norm_final = f(rpb), f(w_out_na), f(w_ffn_in), f(w_ffn_out), f(norm_final)

    stop = DEBUG_STOP
    if "nc" not in _CACHE:
        _CACHE["nc"] = Prog().build(stop)
    nc = _CACHE["nc"]

    kc = lambda w: w.reshape(8, 128, -1)
    shared = {}
    shared["wmod"] = f(w_mod.reshape(4, 8, 128, 12, 512).transpose(0, 3, 2, 1, 4))
    shared["bmod"] = f(np.repeat(b_mod.reshape(4, 48, 128).transpose(2, 0, 1)[..., None], 2, axis=3))
    shared["nmix"] = f(np.repeat(norm_mix.reshape(4, 8, 128).transpose(2, 0, 1)[..., None], 2, axis=3))
    shared["nffn"] = f(np.repeat(norm_ffn.reshape(4, 8, 128).transpose(2, 0, 1)[..., None], 2, axis=3))
    shared["nfin"] = f(norm_final.reshape(8, 128).T)
    shared["ident"] = np.eye(128, dtype=np.float32)
    shared["winab"] = f(w_in_ab.reshape(2, 8, 128, 3, 512).transpose(0, 3, 2, 1, 4))
    shared["woutab"] = f(w_out_ab.reshape(2, 8, 128, 2, 512).transpose(0, 3, 2, 1, 4))
    shared["lnvB"] = f(np.broadcast_to(ln_v[:, None, :], (2, 128, 512)))
    shared["wpool"] = f(w_pool.transpose(0, 2, 1, 3))
    shared["pscale"] = f(pool_scale.reshape(2, 4, 128).transpose(2, 0, 1))
    shared["wqkv"] = f(w_qkv.reshape(2, 8, 128, 3, 8, 128).transpose(0, 4, 2, 1, 3, 5).reshape(2, 8, 128, 8, 384))
    shared["wona"] = f(w_out_na.reshape(2, 8, 128, 2, 512).transpose(0, 3, 2, 1, 4))
    shared["wfin"] = f(w_ffn_in.reshape(4, 8, 128, 2, 22, 128).transpose(0, 4, 2, 1, 3, 5).reshape(4, 22, 128, 8, 256))
    shared["wfout"] = f(w_ffn_out.reshape(4, 22, 128, 1024))
    nat = _na_tables()
    per_rev = {}
    for rev in (0, 1):
        d = {}
        ws = w_spatial[:, :, ::-1, ::-1] if rev else w_spatial
        d["wsT"] = f(ws.transpose(0, 3, 1, 2))
        bs = b_spatial[:, :, ::-1] if rev else b_spatial
        d["bsB"] = f(np.broadcast_to(bs[:, None, :, :], (2, 128, 4, 128)))
        d["invc"] = _invc(rev)
        d["alpha"] = np.full((128, 1), 0.0 if rev else 1.0, np.float32)
        dr_i, dc_i, msk = nat[rev]
        d["btab"] = f(rpb[:, :, dr_i, dc_i])
        d["mtab"] = msk
        per_rev[rev] = d
    in_maps = []
    for core in range(8):
        b, rev = core // 2, core % 2
        m = dict(shared)
        m.update(per_rev[rev])
        if rev:
            xs = x[b, ::-1][:NTOK]
            cs_ = ctx[b, ::-1]
        else:
            xs = x[b, :NTOK]
            cs_ = ctx[b]
        m["xT"] = f(xs.T.reshape(8, 128, NTOK))
        m["cT"] = f(cs_.T.reshape(8, 128, NCTX))
        m["cc"] = f(np.stack([c[b], c_ctx], axis=1).reshape(8, 128, 2).transpose(1, 0, 2))
        in_maps.append(m)
    res = run_bass_kernel_spmd(nc, in_maps, core_ids=list(range(8)))
    out = np.zeros((4, 4096, D), np.float32)
    for core in range(8):
        b, rev = core // 2, core % 2
        o = np.asarray(res.results[core]["out"], dtype=np.float32).reshape(D, NOUT).T
        if rev:
            out[b, 2048:] = o[::-1]
        else:
            out[b, :2048] = o
    return out
```

```python
import contextlib
import numpy as np
import concourse.bass as bass
import concourse.mybir as mybir

F32 = mybir.dt.float32
BF16 = mybir.dt.bfloat16
AF = mybir.ActivationFunctionType
ALU = mybir.AluOpType
AX = mybir.AxisListType

NLANES = 24
NHW = 10
GRAN = 64
_DTSIZE = {F32: 4, BF16: 2}


class Sched:
    def __init__(self, nc, sb_bytes, ps_bytes=16384):
        self.nc = nc
        self.es = contextlib.ExitStack()
        self.eng = {"pe": nc.tensor, "act": nc.scalar, "dve": nc.vector,
                    "pool": nc.gpsimd, "sp": nc.sync}
        names = ["pe", "act", "dve", "pool"] + [f"ln{i}" for i in range(NLANES)]
        self.names = names
        self.idx = {n: i for i, n in enumerate(names)}
        self.NE = len(names)
        self.sems = [self.es.enter_context(nc.semaphore("s_" + n)) for n in names]
        self.cnt = np.zeros(self.NE, np.int64)
        self.seen = {e: np.zeros(self.NE, np.int64) for e in self.eng}
        self.snap = {}
        self.seen["pe"][self.idx["pe"]] = 1 << 60
        self.lane_rr = 0
        self.lane_rr_sw = 0
        self.sb = self.es.enter_context(nc.sbuf_tensor("arena_sb", [128, sb_bytes // 4], F32))
        self.ps = self.es.enter_context(nc.psum_tensor("arena_ps", [128, ps_bytes // 4], F32))
        self.trk = {
            "arena_sb": (np.zeros((sb_bytes // GRAN + 1, self.NE), np.int64),
                         np.zeros((sb_bytes // GRAN + 1, self.NE), np.int64)),
            "arena_ps": (np.zeros((ps_bytes // GRAN + 1, self.NE), np.int64),
                         np.zeros((ps_bytes // GRAN + 1, self.NE), np.int64)),
        }
        self.sb_top = 0
        self.sb_bytes = sb_bytes
        self.blk_cache = {}
        self.n_wait = 0
        self.n_ins = 0

    def alloc(self, nbytes, align=64):
        off = (self.sb_top + align - 1) // align * align
        assert off + nbytes <= self.sb_bytes, f"SBUF overflow {off + nbytes} > {self.sb_bytes}"
        self.sb_top = off + nbytes
        self.sb_max = max(getattr(self, 'sb_max', 0), self.sb_top)
        return off

    def view(self, off, shape, dt):
        n = int(np.prod(shape))
        nb = n * _DTSIZE[dt]
        assert off % 4 == 0 and nb % 4 == 0
        v = self.sb[:, off // 4:(off + nb) // 4]
        if dt != F32:
            v = v.bitcast(dt)
        if len(shape) == 2:
            v = v.rearrange("p (a b) -> p a b", a=shape[0])
        elif len(shape) == 3:
            v = v.rearrange("p (a b c) -> p a b c", a=shape[0], b=shape[1])
        return v

    def new(self, shape, dt):
        n = int(np.prod(shape)) * _DTSIZE[dt]
        n = (n + 3) // 4 * 4
        off = self.alloc(n)
        return self.view(off, shape, dt)

    def psum(self, bank, cols=512, dt=F32, col0=0):
        v = self.ps[:, bank * 512 + col0: bank * 512 + col0 + cols]
        return v

    def _blocks(self, ap):
        name = ap.tensor.name
        if name not in self.trk:
            return None, None
        key = (name, ap.offset, ap.ap, ap.dtype)
        r = self.blk_cache.get(key)
        if r is None:
            es = _DTSIZE[ap.dtype]
            dims = ap.ap
            pstride = dims[0][0]
            off = (ap.offset % pstride) * es
            inner = [(s * es, c) for s, c in dims[1:]]
            starts = np.array([off], np.int64)
            run = es
            if inner:
                s_last, c_last = inner[-1]
                if s_last == es:
                    run = es * c_last
                    inner = inner[:-1]
                for s, c in inner:
                    starts = (starts[:, None] + (np.arange(c, dtype=np.int64) * s)[None, :]).ravel()
            lo = starts // GRAN
            hi = (starts + run - 1) // GRAN
            if len(starts) == 1:
                r = np.arange(lo[0], hi[0] + 1)
            else:
                r = np.unique(np.concatenate([np.arange(a, b + 1) for a, b in zip(lo, hi)]))
            self.blk_cache[key] = r
        return self.trk[name], r

    def _need(self, reads, writes):
        need = np.zeros(self.NE, np.int64)
        for ap in reads:
            t, b = self._blocks(ap)
            if t is not None:
                np.maximum(need, t[0][b].max(0), out=need)
        for ap in writes:
            t, b = self._blocks(ap)
            if t is not None:
                np.maximum(need, t[0][b].max(0), out=need)
                np.maximum(need, t[1][b].max(0), out=need)
        return need

    def _do_waits(self, e, need):
        seen = self.seen[e]
        eng = self.eng[e]
        for j in np.argsort(-(need - seen)):
            if need[j] > seen[j]:
                eng.wait_ge(self.sems[j], int(need[j]))
                self.n_wait += 1
                seen[j] = need[j]
                sn = self.snap.get((int(j), int(need[j])))
                if sn is not None:
                    np.maximum(seen, sn, out=seen)

    def _record(self, ei, val, reads, writes):
        for ap in reads:
            t, b = self._blocks(ap)
            if t is not None:
                t[1][b, ei] = val
        for ap in writes:
            t, b = self._blocks(ap)
            if t is not None:
                t[0][b, ei] = val

    def op(self, e, fn, reads, writes, inc=True):
        ei = self.idx[e]
        need = self._need(reads, writes)
        self._do_waits(e, need)
        ins = fn()
        self.n_ins += 1
        val = int(self.cnt[ei]) + 1
        self._record(ei, val, reads, writes)
        if inc:
            ins.then_inc(self.sems[ei], 1)
            self.cnt[ei] = val
            sn = self.seen[e].copy()
            sn[ei] = val
            self.snap[(ei, val)] = sn
        return ins

    def dma(self, q, out, in_):
        if q == "pool":
            lane = 4 + NHW + self.lane_rr_sw
            self.lane_rr_sw = (self.lane_rr_sw + 1) % (NLANES - NHW)
        else:
            lane = 4 + self.lane_rr
            self.lane_rr = (self.lane_rr + 1) % NHW
        need = self._need([in_], [out])
        need[lane] = max(need[lane], self.cnt[lane])
        self._do_waits(q, need)
        val = int(self.cnt[lane]) + 16
        ins = self.eng[q].dma_start(out=out, in_=in_)
        ins.then_inc(self.sems[lane], 16)
        self.n_ins += 1
        self.cnt[lane] = val
        self._record(lane, val, [in_], [out])
        self.snap[(lane, val)] = self.seen[q].copy()
        return (lane, val)

    def wait_all_dma(self, e="sp"):
        need = np.zeros(self.NE, np.int64)
        need[4:] = self.cnt[4:]
        self._do_waits(e, need)

    def mm(self, out, lhsT, rhs, start, stop, inc=None):
        return self.op("pe", lambda: self.nc.tensor.matmul(out, lhsT, rhs, start=start, stop=stop),
                       [lhsT, rhs], [out], inc=(stop if inc is None else inc))

    def transpose(self, out, in_, ident):
        return self.op("pe", lambda: self.nc.tensor.transpose(out, in_, ident), [in_, ident], [out])

    def act(self, out, in_, func, bias=None, scale=None, accum_out=None, e="act"):
        kw = {}
        reads = [in_]
        writes = [out]
        if bias is not None:
            kw["bias"] = bias
            if not isinstance(bias, (int, float)):
                reads.append(bias)
        if scale is not None:
            kw["scale"] = scale
            if not isinstance(scale, (int, float)):
                reads.append(scale)
        if accum_out is not None:
            kw["accum_out"] = accum_out
            writes.append(accum_out)
        return self.op("act", lambda: self.nc.scalar.activation(out, in_, func, **kw), reads, writes)

    def tt(self, e, out, in0, in1, op):
        return self.op(e, lambda: self.eng[e].tensor_tensor(out, in0, in1, op), [in0, in1], [out])

    def ts(self, e, out, in0, s1, s2, op0, op1=None, accum_out=None):
        reads = [in0] + [s for s in (s1, s2) if s is not None and not isinstance(s, (int, float))]
        writes = [out] + ([accum_out] if accum_out is not None else [])
        kw = {}
        if op1 is not None:
            kw["op1"] = op1
        if accum_out is not None:
            kw["accum_out"] = accum_out
        return self.op(e, lambda: self.eng[e].tensor_scalar(out, in0, s1, s2, op0, **kw), reads, writes)

    def stt(self, e, out, in0, scalar, in1, op0, op1):
        reads = [in0, in1] + ([] if isinstance(scalar, (int, float)) else [scalar])
        return self.op(e, lambda: self.eng[e].scalar_tensor_tensor(out, in0, scalar, in1, op0, op1),
                       reads, [out])

    def copy(self, e, out, in_):
        if e == "act":
            return self.op(e, lambda: self.nc.scalar.copy(out, in_), [in_], [out])
        return self.op(e, lambda: self.eng[e].tensor_copy(out, in_), [in_], [out])

    def memset(self, e, out, val):
        return self.op(e, lambda: self.eng[e].memset(out, val), [], [out])

    def recip(self, out, in_):
        return self.op("dve", lambda: self.nc.vector.reciprocal(out, in_), [in_], [out])


import os
from concourse.bass_utils import run_bass_kernel_spmd

D = 1024
NTOK = 2816
NOUT = 2048
NCTX = 256
EPS = 1e-6
DFF = 2816
SB_BYTES = 212800
NRING = 3
DEBUG_STOP = os.environ.get("MK_STOP", "")

AB_OUT = {0: 2688, 2: 2304}
AB_IN = {0: 2816, 2: 2432}
NA_PAIRS = {1: 19, 3: 16}
NA_NCH = {1: 21, 3: 18}
FFN_N = {0: 2688, 1: 2432, 2: 2304, 3: 2048}


def _split(n, parts):
    base, rem = divmod(n, parts)
    out = []
    s = 0
    for i in range(parts):
        e = s + base + (1 if i < rem else 0)
        out.append((s, e))
        s = e
    return out


def _tiles(n, w=512):
    return [(t, min(w, n - t)) for t in range(0, n, w)]


class Prog:
    def __init__(self):
        nc = bass.Bass("TRN2", target_bir_lowering=False)
        self.nc = nc
        dt = lambda name, shape, kind="ExternalInput": nc.dram_tensor(name, list(shape), F32, kind=kind).ap()
        self.d_x = dt("xT", [8, 128, NTOK])
        self.d_c = dt("cT", [8, 128, NCTX])
        self.d_cc = dt("cc", [128, 8, 2])
        self.d_wmod = dt("wmod", [4, 12, 128, 8, 512])
        self.d_bmod = dt("bmod", [128, 4, 48, 2])
        self.d_nmix = dt("nmix", [128, 4, 8, 2])
        self.d_nffn = dt("nffn", [128, 4, 8, 2])
        self.d_nfin = dt("nfin", [128, 8])
        self.d_ident = dt("ident", [128, 128])
        self.d_winab = dt("winab", [2, 3, 128, 8, 512])
        self.d_woutab = dt("woutab", [2, 2, 128, 8, 512])
        self.d_wsT = dt("wsT", [2, 128, 4, 128])
        self.d_bsB = dt("bsB", [2, 128, 4, 128])
        self.d_lnvB = dt("lnvB", [2, 128, 512])
        self.d_wpool = dt("wpool", [2, 128, 4, 128])
        self.d_pscale = dt("pscale", [128, 2, 4])
        self.d_invc = dt("invc", [128, 3, 4, 8])
        self.d_alpha = dt("alpha", [128, 1])
        self.d_wqkv = dt("wqkv", [2, 8, 128, 8, 384])
        self.d_wona = dt("wona", [2, 2, 128, 8, 512])
        self.d_btab = dt("btab", [2, 16, 128, 15, 128])
        self.d_mtab = dt("mtab", [128, 15, 128])
        self.d_wfin = dt("wfin", [4, 22, 128, 8, 256])
        self.d_wfout = dt("wfout", [4, 22, 128, 1024])
        self.d_out = dt("out", [8, 128, NOUT], kind="ExternalOutput")

        S = Sched(nc, SB_BYTES)
        self.S = S
        self.pbc = 0
        self.ringc = 0
        self.bg = []
        self.X = S.new([8, NTOK], F32)
        self.XC = S.new([8, NCTX], F32)
        self.RING = [S.new([4096], BF16) for _ in range(NRING)]
        self.MOD = S.new([4, 48, 2], F32)
        self.A1 = S.new([4, 8, 2], F32)
        self.A2 = S.new([4, 8, 2], F32)
        self.ONES = S.new([128], BF16)
        self.IDENT = S.new([128], BF16)
        self.NFIN = S.new([8], F32)
        self.ALPHA = S.new([1], F32)
        self.INVC = S.new([3, 4, 8], F32)
        self.PSC = S.new([2, 4], F32)
        self.MTAB = S.new([15, 128], BF16)
        self.WST = S.new([4, 128], BF16)
        self.BSB = S.new([512], F32)
        self.LNVB = S.new([512], F32)
        self.WPOOL = S.new([4, 128], BF16)
        self.HPREV = S.new([8, 256], BF16)
        self.SCR0 = S.sb_top

    def pb(self, cols=512):
        b = self.pbc % 7
        self.pbc += 1
        return self.S.ps[:, b * 512: b * 512 + cols]

    def pb2(self, cols):
        while (self.pbc % 7) not in (0, 2, 4):
            self.pbc += 1
        b = self.pbc % 7
        self.pbc += 2
        return self.S.ps[:, b * 512: b * 512 + cols]

    def bg_step(self, n=1):
        for _ in range(n):
            if self.bg:
                self.bg[0]()
                self.bg.pop(0)

    def bg_flush(self):
        while self.bg:
            self.bg_step()

    def ring(self, bg=False):
        if bg:
            return self.RING[NRING - 1]
        nr = NRING - 1 if self.bg else NRING
        r = self.RING[self.ringc % nr]
        self.ringc += 1
        return r

    def load_w(self, src, k, n, bg=False):
        slot = self.ring(bg)[:, 0:k * n].rearrange("p (k n) -> p k n", k=k)
        self.S.dma("pool", slot, src)
        return slot

    def norm_tmp(self, nmax=512):
        S = self.S
        self.SQ = [S.new([512], BF16) for _ in range(2)]
        self.RSA = S.new([max(nmax, 512)], F32)
        self.TMPN = [S.new([512], F32) for _ in range(2)]

    def norm_multi(self, jobs):
        S = self.S
        offs = []
        o = 0
        for (A, B, j, src, dst, n) in jobs:
            offs.append(o)
            o += n
        ntot = o
        for (A, B, j, src, dst, n), o in zip(jobs, offs):
            for (t0, w) in _tiles(n):
                ps = self.pb()[:, :w]
                for k in range(8):
                    sq = self.SQ[k % 2][:, :w]
                    S.act(sq, src[:, k, t0:t0 + w], AF.Square)
                    S.mm(ps, self.ONES, sq, k == 0, k == 7, inc=True)
                S.ts("dve", self.RSA[:, o + t0:o + t0 + w], ps, 1024.0 * EPS, None, ALU.add)
        r = self.RSA[:, 0:ntot]
        S.act(r, r, AF.Sqrt)
        S.recip(r, r)
        for (A, B, j, src, dst, n), o in zip(jobs, offs):
            for (t0, w) in _tiles(n):
                for k in range(8):
                    tmp = self.TMPN[k % 2][:, :w]
                    S.tt("dve", tmp, src[:, k, t0:t0 + w], self.RSA[:, o + t0:o + t0 + w], ALU.mult)
                    if B is None:
                        S.act(dst[:, k, t0:t0 + w], tmp, AF.Identity, scale=A[:, k, j:j + 1])
                    else:
                        S.act(dst[:, k, t0:t0 + w], tmp, AF.Identity, bias=B[:, k, j:j + 1], scale=A[:, k, j:j + 1])

    def norm_h(self, A, B, j, src, dst, n):
        self.norm_multi([(A, B, j, src, dst, n)])

    def prologue(self):
        S = self.S
        nc = self.nc
        S.sb_top = self.SCR0
        for k in range(8):
            S.dma("sp", self.X[:, k, :], self.d_x[k])
            S.dma("sp", self.XC[:, k, :], self.d_c[k])
        ccs = S.new([8, 2], F32)
        csil = S.new([8, 2], BF16)
        bmod = S.new([4, 48, 2], F32)
        nmix = S.new([4, 8, 2], F32)
        nffn = S.new([4, 8, 2], F32)
        onesf = S.new([128], F32)
        S.dma("sp", ccs, self.d_cc)
        S.dma("sp", bmod, self.d_bmod)
        S.dma("sp", nmix, self.d_nmix)
        S.dma("sp", nffn, self.d_nffn)
        S.dma("sp", self.NFIN, self.d_nfin)
        S.dma("sp", self.ALPHA, self.d_alpha)
        S.dma("sp", self.INVC, self.d_invc)
        S.dma("sp", self.PSC, self.d_pscale)
        S.dma("pool", self.IDENT, self.d_ident)
        S.dma("pool", self.MTAB, self.d_mtab)
        S.memset("dve", onesf, 1.0)
        S.copy("dve", self.ONES, onesf)
        S.act(csil, ccs, AF.Silu)
        self.p_csil, self.p_bmod, self.p_nmix, self.p_nffn = csil, bmod, nmix, nffn
        self.SCR0 = S.sb_top
        for t in self.mod_tasks(0, bg=False):
            t()
        S.ts("dve", self.NFIN, self.NFIN, 32.0, None, ALU.mult)

    def mod_tasks(self, l, bg=True):
        S = self.S
        ps = S.ps[:, 7 * 512:7 * 512 + 96]
        csil, bmod = self.p_csil, self.p_bmod

        def piece(n):
            def f():
                slot = self.load_w(self.d_wmod[l, n], 8, 512, bg=bg)
                for mi in range(4):
                    m = n * 4 + mi
                    for k in range(8):
                        S.mm(ps[:, m * 2:(m + 1) * 2], slot[:, k, mi * 128:(mi + 1) * 128], csil[:, k, :], k == 0, k == 7)
            return f

        def fin():
            S.tt("dve", self.MOD[:, l].rearrange("p a b -> p (a b)"), ps, bmod[:, l].rearrange("p a b -> p (a b)"), ALU.add)
            for (Adst, gain, c0) in ((self.A1, self.p_nmix, 8), (self.A2, self.p_nffn, 32)):
                S.stt("dve", Adst[:, l], self.MOD[:, l, c0:c0 + 8, :], 1.0, gain[:, l], ALU.add, ALU.mult)
                S.ts("dve", Adst[:, l], Adst[:, l], 32.0, None, ALU.mult)
        return [piece(n) for n in range(12)] + [fin]

    def ffn(self, l, seglists):
        S = self.S
        S.sb_top = self.SCR0
        nmax = max(sum(sg[1] for sg in segs) for segs in seglists)
        self.norm_tmp(nmax)
        H = S.new([8, nmax], BF16)
        HID = S.new([8, nmax], BF16)
        SA = [S.new([512], F32) for _ in range(2)]
        B2 = self.MOD[:, l, 24:32, :]
        G2 = self.MOD[:, l, 40:48, :]

        def prep(segs):
            off = 0
            tl = []
            jobs = []
            for (res, n, j) in segs:
                jobs.append((self.A2[:, l], B2, j, res, H[:, :, off:off + n], n))
                for (t0, w) in _tiles(n):
                    tl.append((off + t0, w, res, t0, j))
                off += n
            self.norm_multi(jobs)
            return tl

        it = 0
        blocks = ((0, 8), (8, 15), (15, 22))
        tl_next = prep(seglists[0])
        for si in range(len(seglists)):
            tl = tl_next
            for bi, (j0, j1) in enumerate(blocks):
                for jh in range(j0, j1):
                    slot = self.load_w(self.d_wfin[l, jh], 8, 256)
                    jl = jh - j0
                    for (c0, w, res, t0, j) in tl:
                        pa = self.pb()[:, :w]
                        pg = self.pb()[:, :w]
                        for k in range(8):
                            S.mm(pa, slot[:, k, 0:128], H[:, k, c0:c0 + w], k == 0, k == 7)
                        for k in range(8):
                            S.mm(pg, slot[:, k, 128:256], H[:, k, c0:c0 + w], k == 0, k == 7)
                        sa = SA[it % 2][:, :w]
                        it += 1
                        S.act(sa, pa, AF.Silu)
                        S.tt("dve", HID[:, jl, c0:c0 + w], sa, pg, ALU.mult)
                if bi == len(blocks) - 1 and si + 1 < len(seglists):
                    tl_next = prep(seglists[si + 1])
                nb = j1 - j0
                wo = []
                for q0 in range(0, nb, 4):
                    qn = min(4, nb - q0)
                    slot = self.ring()[:, 0:qn * 1024].rearrange("p (k n) -> p k n", k=qn)
                    S.dma("pool", slot, self.d_wfout[l, j0 + q0:j0 + q0 + qn].rearrange("j p n -> p j n"))
                    for qi in range(qn):
                        wo.append(slot[:, qi, :])
                for m in range(8):
                    for (c0, w, res, t0, j) in tl:
                        py = self.pb()[:, :w]
                        for jl in range(nb):
                            S.mm(py, wo[jl][:, m * 128:(m + 1) * 128], HID[:, jl, c0:c0 + w], jl == 0, jl == nb - 1)
                        rr = res[:, m, t0:t0 + w]
                        S.stt("dve", rr, py, G2[:, m, j:j + 1], rr, ALU.mult, ALU.add)

    def ab_tables(self, e):
        S = self.S
        S.dma("pool", self.WST, self.d_wsT[e])
        S.dma("sp", self.BSB.rearrange("p (a b) -> p a b", a=4), self.d_bsB[e])
        S.dma("sp", self.LNVB, self.d_lnvB[e])
        S.dma("pool", self.WPOOL, self.d_wpool[e])

    def ab_mixer(self, l, e, Xb, c0, c1, hs, he, j, fix_head, fix_tail):
        S = self.S
        nc = self.nc
        S.sb_top = self.SCR0
        nh = he - hs
        n = c1 - c0
        o = c0 - hs
        H = S.new([8, nh], BF16)
        B1 = self.MOD[:, l, 0:8, :]
        G1 = self.MOD[:, l, 16:24, :]
        mark = S.sb_top
        self.norm_tmp(nh)
        if o > 0:
            S.copy("act", H[:, :, 0:o], self.HPREV[:, :, 256 - o:256])
        self.norm_h(self.A1[:, l], B1, j, Xb[:, :, c0:he], H[:, :, o:nh], nh - o)
        if j == 0:
            S.copy("act", self.HPREV[:, :, 128:256], H[:, :, o + n - 128:o + n])
        S.sb_top = mark
        YA = S.new([4, n], BF16)
        YB = S.new([4, n], BF16)
        PB = [S.new([nh + 16], F32) for _ in range(2)]
        Aa = S.new([nh + 16], F32)
        Ab = S.new([nh + 16], F32)
        DF = S.new([n], BF16)
        VT = [S.new([512], F32) for _ in range(3)]
        CEN = [S.new([512], F32) for _ in range(3)]
        VH = [S.new([512], BF16) for _ in range(3)]
        SQJ = S.new([512], F32)
        SM = [S.new([4], F32) for _ in range(3)]
        T8 = S.new([8], F32)
        slot_p = self.load_w(self.d_winab[e, 2], 8, 512)
        slot_u = self.load_w(self.d_winab[e, 0], 8, 512)

        def u_proj(m):
            for (t0, w) in _tiles(n):
                ps = self.pb()[:, :w]
                for k in range(8):
                    S.mm(ps, slot_u[:, k, m * 128:(m + 1) * 128], H[:, k, o + t0:o + t0 + w], k == 0, k == 7)
                S.act(YA[:, m, t0:t0 + w], ps, AF.Gelu_apprx_tanh)

        def p_proj(g):
            P = PB[g % 2]
            S.memset("dve", P[:, 0:8], 0.0)
            S.memset("dve", P[:, 8 + nh:16 + nh], 0.0)
            for (t0, w) in _tiles(nh):
                ps = self.pb()[:, :w]
                for k in range(8):
                    S.mm(ps, slot_p[:, k, g * 128:(g + 1) * 128], H[:, k, t0:t0 + w], k == 0, k == 7)
                S.copy("act", P[:, 8 + t0:8 + t0 + w], ps)

        def pooling(g):
            P = PB[g % 2]
            wd = 2 ** (g + 1)
            half = wd // 2
            cur = P
            length = nh + 16
            step = 1
            bufs = [Aa, Ab]
            bi = 0
            while step < wd:
                nxt = bufs[bi]
                bi ^= 1
                L2 = length - step
                S.tt("dve", nxt[:, 0:L2], cur[:, 0:L2], cur[:, step:step + L2], ALU.add)
                cur = nxt
                length = L2
                step *= 2
            i0 = o + 8
            Dd = bufs[bi][:, 0:n]
            fw_ = cur[:, i0 - half:i0 - half + n]
            rv_ = cur[:, i0 - half + 1:i0 - half + 1 + n]
            S.tt("dve", Dd, fw_, rv_, ALU.subtract)
            S.stt("dve", Dd, Dd, self.ALPHA[:, 0:1], rv_, ALU.mult, ALU.add)
            S.stt("dve", DF, Dd, 1.0 / wd, P[:, i0:i0 + n], ALU.mult, ALU.subtract)
            if fix_head is not None:
                S.tt("dve", T8, Dd[:, 0:8], self.INVC[:, fix_head, g, :], ALU.mult)
                S.tt("dve", DF[:, 0:8], T8, P[:, i0:i0 + 8], ALU.subtract)
            if fix_tail is not None:
                S.tt("dve", T8, Dd[:, n - 8:n], self.INVC[:, fix_tail, g, :], ALU.mult)
                S.tt("dve", DF[:, n - 8:n], T8, P[:, i0 + n - 8:i0 + n], ALU.subtract)

        def yb_proj(g):
            for (t0, w) in _tiles(n):
                ps = self.pb()[:, :w]
                S.mm(ps, self.WPOOL[:, g, :], DF[:, t0:t0 + w], True, True)
                S.act(YB[:, g, t0:t0 + w], ps, AF.Identity, scale=self.PSC[:, e, g:g + 1])

        p_proj(0)
        for g in range(4):
            if j == 0:
                self.bg_step()
            u_proj(g)
            if g < 3:
                p_proj(g + 1)
            pooling(g)
            yb_proj(g)

        slot_v = self.load_w(self.d_winab[e, 1], 8, 512)
        nchunk = n // 128

        def v_a(ci):
            tc = o + ci * 128
            ps = self.pb()
            for k in range(8):
                S.mm(ps, H[:, k, tc:tc + 128], slot_v[:, k, :], k == 0, k == 7)
            S.act(VT[ci % 3], ps, AF.Gelu_apprx_tanh)

        def v_b1(ci):
            vt = VT[ci % 3]
            cen = CEN[ci % 3]
            vh = VH[ci % 3]
            sm = SM[ci % 3]
            S.op("dve", lambda: nc.vector.reduce_sum(sm[:, 0:1], vt, AX.X), [vt], [sm[:, 0:1]])
            S.ts("dve", sm[:, 1:2], sm[:, 0:1], -1.0 / 512.0, None, ALU.mult)
            S.ts("dve", cen, vt, sm[:, 1:2], None, ALU.add)
            S.act(SQJ, cen, AF.Square)
            S.op("dve", lambda: nc.vector.reduce_sum(sm[:, 2:3], SQJ, AX.X), [SQJ], [sm[:, 2:3]])
            S.ts("dve", sm[:, 3:4], sm[:, 2:3], 1.0 / 512.0, EPS, ALU.mult, ALU.add)
            S.act(sm[:, 3:4], sm[:, 3:4], AF.Sqrt)
            S.recip(sm[:, 3:4], sm[:, 3:4])
            S.stt("dve", vh, cen, sm[:, 3:4], self.LNVB, ALU.mult, ALU.mult)

        def v_b2(ci):
            cen = CEN[ci % 3]
            vh = VH[ci % 3]
            psg = self.pb()
            for g in range(4):
                S.mm(psg[:, g * 128:(g + 1) * 128], vh[:, g * 128:(g + 1) * 128], self.WST[:, g, :], True, True)
            S.tt("dve", cen, psg, self.BSB, ALU.add)
            ya = YA[:, :, ci * 128:(ci + 1) * 128]
            S.tt("dve", ya, cen.rearrange("p (a b) -> p a b", a=4), ya, ALU.mult)

        for i in range(nchunk + 2):
            if i < nchunk:
                v_a(i)
            if 1 <= i <= nchunk:
                v_b1(i - 1)
            if i >= 2:
                v_b2(i - 2)
        for hf in range(2):
            slot = self.load_w(self.d_woutab[e, hf], 8, 512)
            for mi in range(4):
                m = hf * 4 + mi
                for (t0, w) in _tiles(n):
                    ps = self.pb()[:, :w]
                    for k in range(8):
                        rhs = (YA if k < 4 else YB)[:, k % 4, t0:t0 + w]
                        S.mm(ps, slot[:, k, mi * 128:(mi + 1) * 128], rhs, k == 0, k == 7)
                    rr = Xb[:, m, c0 + t0:c0 + t0 + w]
                    S.stt("dve", rr, ps, G1[:, m, j:j + 1], rr, ALU.mult, ALU.add)

    def na_mixer(self, l, o_, a0, a1, do_ctx_q):
        S = self.S
        nc = self.nc
        S.sb_top = self.SCR0
        NCH = NA_NCH[l]
        cs = lambda a: min(max(a - 2, 0), NCH - 5)
        k0 = cs(a0)
        k1 = cs(a1 - 1) + 5
        nkc = k1 - k0
        nk = nkc * 128
        npair = a1 - a0
        nq = npair * 128
        qoff = (a0 - k0) * 128
        H = S.new([8, nk], BF16)
        HC = S.new([8, NCTX], BF16)
        OT = S.new([8, nq], BF16)
        OTC = S.new([8, NCTX], BF16) if do_ctx_q else None
        QT = S.new([nq], BF16)
        KT = S.new([nk], BF16)
        V1 = S.new([nkc, 2, 65], BF16)
        QC = S.new([NCTX], BF16)
        KC = S.new([NCTX], BF16)
        VC = S.new([2, 2, 65], BF16)
        B1 = self.MOD[:, l, 0:8, :]
        G1 = self.MOD[:, l, 16:24, :]
        mark = S.sb_top
        self.norm_tmp(nk + NCTX)
        lh = (a0 - k0) * 128
        if lh > 0:
            S.copy("act", H[:, :, 0:lh], self.HPREV[:, :, 256 - lh:256])
        self.norm_multi([(self.A1[:, l], B1, 0, self.X[:, :, a0 * 128:k1 * 128], H[:, :, lh:nk], nk - lh),
                         (self.A1[:, l], B1, 1, self.XC, HC, NCTX)])
        S.copy("act", self.HPREV, H[:, :, (a1 - 2 - k0) * 128:(a1 - k0) * 128])
        S.sb_top = mark
        OTOK = S.new([npair, 128], BF16)
        OTOKC = S.new([2, 128], BF16)
        PT = [S.new([7, 128], BF16) for _ in range(3)]
        tbase = 0 if a0 < 2 else 2
        ntyp = 3 - tbase
        E = [S.new([5 * ntyp, 128], BF16) for _ in range(2)]
        VTF = S.new([nk], BF16)
        VCF = S.new([NCTX], BF16)
        RINV = [S.new([1], F32) for _ in range(3)]
        for c in range(nkc):
            S.memset("dve", V1[:, c, :, 64:65], 1.0)
        for c in range(2):
            S.memset("dve", VC[:, c, :, 64:65], 1.0)
        it = 0
        for hp in range(8):
            self.bg_step()
            slot = self.load_w(self.d_wqkv[o_, hp], 8, 384)
            for (t0, w) in _tiles(nq):
                ps = self.pb()[:, :w]
                for k in range(8):
                    S.mm(ps, slot[:, k, 0:128], H[:, k, qoff + t0:qoff + t0 + w], k == 0, k == 7)
                S.copy("act", QT[:, t0:t0 + w], ps)
            for (t0, w) in _tiles(nk):
                ps = self.pb()[:, :w]
                for k in range(8):
                    S.mm(ps, slot[:, k, 128:256], H[:, k, t0:t0 + w], k == 0, k == 7)
                S.copy("act", KT[:, t0:t0 + w], ps)
            for (t0, w) in _tiles(nk):
                ps = self.pb()[:, :w]
                for k in range(8):
                    S.mm(ps, slot[:, k, 256:384], H[:, k, t0:t0 + w], k == 0, k == 7)
                S.copy("act", VTF[:, t0:t0 + w], ps)
            for c in range(nkc):
                pst = self.pb()[:, 0:64].bitcast(BF16)
                S.transpose(pst, VTF[:, c * 128:(c + 1) * 128], self.IDENT)
                S.copy("dve", V1[:, c, :, 0:64], pst.rearrange("p (a b) -> p a b", a=2))
            ps = self.pb()[:, :NCTX]
            for k in range(8):
                S.mm(ps, slot[:, k, 128:256], HC[:, k, :], k == 0, k == 7)
            S.copy("act", KC, ps)
            ps = self.pb()[:, :NCTX]
            for k in range(8):
                S.mm(ps, slot[:, k, 256:384], HC[:, k, :], k == 0, k == 7)
            S.copy("act", VCF, ps)
            for c in range(2):
                pst = self.pb()[:, 0:64].bitcast(BF16)
                S.transpose(pst, VCF[:, c * 128:(c + 1) * 128], self.IDENT)
                S.copy("dve", VC[:, c, :, 0:64], pst.rearrange("p (a b) -> p a b", a=2))
            if do_ctx_q:
                ps = self.pb()[:, :NCTX]
                for k in range(8):
                    S.mm(ps, slot[:, k, 0:128], HC[:, k, :], k == 0, k == 7)
                S.copy("act", QC, ps)
            for hh in range(2):
                S.dma("pool", E[hh], self.d_btab[o_, hp * 2 + hh][:, tbase * 5:15, :])
                S.act(E[hh], E[hh], AF.Exp)
                S.tt("dve", E[hh], E[hh], self.MTAB[:, tbase * 5:15, :], ALU.mult)
            blocks = [("x", a) for a in range(a0, a1)]
            if do_ctx_q:
                blocks += [("c", 0), ("c", 1)]
            its = [(kind, a, hh) for (kind, a) in blocks for hh in range(2)]
            LA = 2

            def stage1(i):
                kind, a, hh = its[i]
                hsl = slice(hh * 64, hh * 64 + 64)
                pt = PT[i % 3]
                ptf = pt.rearrange("p a b -> p (a b)")
                if kind == "x":
                    typ = min(a, 2)
                    c_s = cs(a) - k0
                    q = QT[hsl, (a - a0) * 128:(a - a0 + 1) * 128]
                    pss = self.pb2(896)
                    for jj in range(5):
                        S.mm(pss[:, jj * 128:(jj + 1) * 128], KT[hsl, (c_s + jj) * 128:(c_s + jj + 1) * 128], q, True, True)
                    for jc in range(2):
                        S.mm(pss[:, (5 + jc) * 128:(6 + jc) * 128], KC[hsl, jc * 128:(jc + 1) * 128], q, True, True)
                    S.act(ptf[:, 0:512], pss[:, 0:512], AF.Exp, scale=0.125)
                    S.act(ptf[:, 512:896], pss[:, 512:896], AF.Exp, scale=0.125)
                    S.tt("dve", pt[:, 0:5, :], pt[:, 0:5, :], E[hh][:, (typ - tbase) * 5:(typ - tbase) * 5 + 5, :], ALU.mult)
                else:
                    q = QC[hsl, a * 128:(a + 1) * 128]
                    pss = self.pb()[:, 0:256]
                    for jc in range(2):
                        S.mm(pss[:, jc * 128:(jc + 1) * 128], KC[hsl, jc * 128:(jc + 1) * 128], q, True, True)
                    S.act(ptf[:, 0:256], pss, AF.Exp, scale=0.125)

            def stage2(i):
                kind, a, hh = its[i]
                pt = PT[i % 3]
                rinv = RINV[i % 3]
                pso = self.pb()[:, 0:65]
                if kind == "x":
                    c_s = cs(a) - k0
                    for jj in range(5):
                        S.mm(pso, pt[:, jj, :], V1[:, c_s + jj, hh, :], jj == 0, False)
                    for jc in range(2):
                        S.mm(pso, pt[:, 5 + jc, :], VC[:, jc, hh, :], False, jc == 1)
                    dst = OTOK[:, a - a0, hh * 64:hh * 64 + 64]
                else:
                    for jc in range(2):
                        S.mm(pso, pt[:, jc, :], VC[:, jc, hh, :], jc == 0, jc == 1)
                    dst = OTOKC[:, a, hh * 64:hh * 64 + 64]
                S.recip(rinv, pso[:, 64:65])
                S.ts("dve", dst, pso[:, 0:64], rinv[:, 0:1], None, ALU.mult)

            for i in range(len(its) + LA):
                if i < len(its):
                    stage1(i)
                if i >= LA:
                    stage2(i - LA)
            for (kind, a) in blocks:
                pst = self.pb()[:, 0:64].bitcast(BF16)
                if kind == "x":
                    S.transpose(pst, OTOK[:, a - a0, :], self.IDENT)
                    S.copy("dve", OT[:, hp, (a - a0) * 128:(a - a0 + 1) * 128], pst)
                else:
                    S.transpose(pst, OTOKC[:, a, :], self.IDENT)
                    S.copy("dve", OTC[:, hp, a * 128:(a + 1) * 128], pst)
        q0 = a0 * 128
        for hf in range(2):
            slot = self.load_w(self.d_wona[o_, hf], 8, 512)
            for mi in range(4):
                m = hf * 4 + mi
                for (t0, w) in _tiles(nq):
                    ps = self.pb()[:, :w]
                    for k in range(8):
                        S.mm(ps, slot[:, k, mi * 128:(mi + 1) * 128], OT[:, k, t0:t0 + w], k == 0, k == 7)
                    rr = self.X[:, m, q0 + t0:q0 + t0 + w]
                    S.stt("dve", rr, ps, G1[:, m, 0:1], rr, ALU.mult, ALU.add)
                if do_ctx_q:
                    ps = self.pb()[:, :NCTX]
                    for k in range(8):
                        S.mm(ps, slot[:, k, mi * 128:(mi + 1) * 128], OTC[:, k, :], k == 0, k == 7)
                    rr = self.XC[:, m, :]
                    S.stt("dve", rr, ps, G1[:, m, 1:2], rr, ALU.mult, ALU.add)

    def write_out(self, final):
        S = self.S
        S.sb_top = self.SCR0
        self.norm_tmp()
        if final:
            ST = [S.new([8, 512], F32) for _ in range(2)]
            for i, (t0, w) in enumerate(_tiles(NOUT)):
                st = ST[i % 2]
                self.norm_h_f32(self.X[:, :, t0:t0 + w], st, w)
                for k in range(8):
                    S.dma("sp", self.d_out[k][:, t0:t0 + w], st[:, k, :w])
        else:
            for k in range(8):
                S.dma("sp", self.d_out[k], self.X[:, k, 0:NOUT])
        S.wait_all_dma("sp")

    def norm_h_f32(self, src, dst, w):
        S = self.S
        ps = self.pb()[:, :w]
        for k in range(8):
            sq = self.SQ[k % 2][:, :w]
            S.act(sq, src[:, k, :], AF.Square)
            S.mm(ps, self.ONES, sq, k == 0, k == 7, inc=True)
        r = self.RSA[:, :w]
        S.ts("dve", r, ps, 1024.0 * EPS, None, ALU.add)
        S.act(r, r, AF.Sqrt)
        S.recip(r, r)
        for k in range(8):
            tmp = self.TMPN[k % 2][:, :w]
            S.tt("dve", tmp, src[:, k, :], r, ALU.mult)
            S.act(dst[:, k, :w], tmp, AF.Identity, scale=self.NFIN[:, k:k + 1])

    def build(self, stop=""):
        self.prologue()
        for l in range(4):
            if stop == "p":
                break
            last = l == 3
            if not last:
                self.bg = self.mod_tasks(l + 1)
            if l % 2 == 0:
                e = l // 2
                self.ab_tables(e)
                nout = AB_OUT[l]
                nin = AB_IN[l]
                for (s, t) in _split(nout // 128, 4):
                    c0, c1 = s * 128, t * 128
                    hs = max(c0 - 128, 0)
                    he = min(c1 + 128, nin)
                    self.ab_mixer(l, e, self.X, c0, c1, hs, he, 0, 0 if c0 == 0 else None, None)
                self.ab_mixer(l, e, self.XC, 0, NCTX, 0, NCTX, 1, 1, 2)
            else:
                o_ = l // 2
                sp = _split(NA_PAIRS[l], 3)
                for i, (a0, a1) in enumerate(sp):
                    self.na_mixer(l, o_, a0, a1, (not last) and i == len(sp) - 1)
            self.bg_flush()
            if stop == "m%d" % l:
                break
            n = FFN_N[l]
            h1 = (n // 128 + 1) // 2 * 128
            segs = [(self.X[:, :, h1:n], n - h1, 0)]
            if not last:
                segs.append((self.XC, NCTX, 1))
            self.ffn(l, [[(self.X[:, :, 0:h1], h1, 0)], segs])
            if stop == "f%d" % l:
                break
        self.write_out(final=(stop == ""))
        return self.nc


def _na_tables():
    out = {}
    for rev in (0, 1):
        dr_i = np.zeros((128, 15, 128), np.int64)
        dc_i = np.zeros((128, 15, 128), np.int64)
        msk = np.zeros((128, 15, 128), np.float32)
        kp = np.arange(128)
        qi = np.arange(128)
        kr2, kcl = kp // 64, kp % 64
        qr2, qcl = qi // 64, qi % 64
        for typ, a in ((0, 0), (1, 1), (2, 4)):
            cs = max(a - 2, 0)
            for jj in range(5):
                krl = 2 * (cs + jj) + kr2[:, None]
                qrl = 2 * a + qr2[None, :]
                kc_ = kcl[:, None] + 0 * qrl
                qc_ = qcl[None, :] + 0 * krl
                if rev:
                    r, c, kr, kc = 63 - qrl, 63 - qc_, 63 - krl, 63 - kc_
                else:
                    r, c, kr, kc = qrl, qc_, krl, kc_
                r = r + 0 * kr
                kr = kr + 0 * r
                rs = np.clip(r - 4, 0, 56)
                cst = np.clip(c - 8, 0, 48)
                ok = (kr >= rs) & (kr < rs + 8) & (kc >= cst) & (kc < cst + 16)
                dr = np.where(ok, kr - r + 7, 0)
                dc = np.clip(kc - c, -15, 15) + 15
                dr_i[:, typ * 5 + jj, :] = dr
                dc_i[:, typ * 5 + jj, :] = dc
                msk[:, typ * 5 + jj, :] = ok
        out[rev] = (dr_i, dc_i, msk)
    return out


def _invc(rev):
    t = np.zeros((128, 3, 4, 8), np.float32)
    for g, wd in enumerate((2, 4, 8, 16)):
        half = wd // 2
        for which, L, locs in ((0, 4096, range(8)), (1, 256, range(8)), (2, 256, range(248, 256))):
            for i, tl in enumerate(locs):
                tg = (L - 1 - tl) if rev else tl
                cnt = min(tg + half, L) - max(tg - half, 0)
                t[:, which, g, i] = np.float32(1.0) / np.float32(cnt)
    return t


_CACHE = {}


def kernel(x, c, ctx, c_ctx, w_mod, b_mod, norm_mix, norm_ffn, w_in_ab, ln_v, w_spatial,
           b_spatial, w_pool, pool_scale, w_out_ab, w_qkv, rpb, w_out_na, w_ffn_in,
           w_ffn_out, norm_final):
    f = lambda a: np.ascontiguousarray(np.asarray(a, dtype=np.float32))
    x, c, ctx, c_ctx = f(x), f(c), f(ctx), f(c_ctx)
    w_mod, b_mod, norm_mix, norm_ffn = f(w_mod), f(b_mod), f(norm_mix), f(norm_ffn)
    w_in_ab, ln_v, w_spatial, b_spatial = f(w_in_ab), f(ln_v), f(w_spatial), f(b_spatial)
    w_pool, pool_scale, w_out_ab, w_qkv = f(w_pool), f(pool_scale), f(w_out_ab), f(w_qkv)
    rpb, w_out_na, w_ffn_in, w_ffn_out, norm_final = f(rpb), f(w_out_na), f(w_ffn_in), f(w_ffn_out), f(norm_final)

    stop = DEBUG_STOP
    if "nc" not in _CACHE:
        _CACHE["nc"] = Prog().build(stop)
    nc = _CACHE["nc"]

    kc = lambda w: w.reshape(8, 128, -1)
    shared = {}
    shared["wmod"] = f(w_mod.reshape(4, 8, 128, 12, 512).transpose(0, 3, 2, 1, 4))
    shared["bmod"] = f(np.repeat(b_mod.reshape(4, 48, 128).transpose(2, 0, 1)[..., None], 2, axis=3))
    shared["nmix"] = f(np.repeat(norm_mix.reshape(4, 8, 128).transpose(2, 0, 1)[..., None], 2, axis=3))
    shared["nffn"] = f(np.repeat(norm_ffn.reshape(4, 8, 128).transpose(2, 0, 1)[..., None], 2, axis=3))
    shared["nfin"] = f(norm_final.reshape(8, 128).T)
    shared["ident"] = np.eye(128, dtype=np.float32)
    shared["winab"] = f(w_in_ab.reshape(2, 8, 128, 3, 512).transpose(0, 3, 2, 1, 4))
    shared["woutab"] = f(w_out_ab.reshape(2, 8, 128, 2, 512).transpose(0, 3, 2, 1, 4))
    shared["lnvB"] = f(np.broadcast_to(ln_v[:, None, :], (2, 128, 512)))
    shared["wpool"] = f(w_pool.transpose(0, 2, 1, 3))
    shared["pscale"] = f(pool_scale.reshape(2, 4, 128).transpose(2, 0, 1))
    shared["wqkv"] = f(w_qkv.reshape(2, 8, 128, 3, 8, 128).transpose(0, 4, 2, 1, 3, 5).reshape(2, 8, 128, 8, 384))
    shared["wona"] = f(w_out_na.reshape(2, 8, 128, 2, 512).transpose(0, 3, 2, 1, 4))
    shared["wfin"] = f(w_ffn_in.reshape(4, 8, 128, 2, 22, 128).transpose(0, 4, 2, 1, 3, 5).reshape(4, 22, 128, 8, 256))
    shared["wfout"] = f(w_ffn_out.reshape(4, 22, 128, 1024))
    nat = _na_tables()
    per_rev = {}
    for rev in (0, 1):
        d = {}
        ws = w_spatial[:, :, ::-1, ::-1] if rev else w_spatial
        d["wsT"] = f(ws.transpose(0, 3, 1, 2))
        bs = b_spatial[:, :, ::-1] if rev else b_spatial
        d["bsB"] = f(np.broadcast_to(bs[:, None, :, :], (2, 128, 4, 128)))
        d["invc"] = _invc(rev)
        d["alpha"] = np.full((128, 1), 0.0 if rev else 1.0, np.float32)
        dr_i, dc_i, msk = nat[rev]
        d["btab"] = f(rpb[:, :, dr_i, dc_i])
        d["mtab"] = msk
        per_rev[rev] = d
    in_maps = []
    for core in range(8):
        b, rev = core // 2, core % 2
        m = dict(shared)
        m.update(per_rev[rev])
        if rev:
            xs = x[b, ::-1][:NTOK]
            cs_ = ctx[b, ::-1]
        else:
            xs = x[b, :NTOK]
            cs_ = ctx[b]
        m["xT"] = f(xs.T.reshape(8, 128, NTOK))
        m["cT"] = f(cs_.T.reshape(8, 128, NCTX))
        m["cc"] = f(np.stack([c[b], c_ctx], axis=1).reshape(8, 128, 2).transpose(1, 0, 2))
        in_maps.append(m)
    res = run_bass_kernel_spmd(nc, in_maps, core_ids=list(range(8)))
    out = np.zeros((4, 4096, D), np.float32)
    for core in range(8):
        b, rev = core // 2, core % 2
        o = np.asarray(res.results[core]["out"], dtype=np.float32).reshape(D, NOUT).T
        if rev:
            out[b, 2048:] = o[::-1]
        else:
            out[b, :2048] = o
    return out
```

```python
import contextlib
import numpy as np
import concourse.bass as bass
import concourse.mybir as mybir

F32 = mybir.dt.float32
BF16 = mybir.dt.bfloat16
AF = mybir.ActivationFunctionType
ALU = mybir.AluOpType
AX = mybir.AxisListType

NLANES = 24
NHW = 10
GRAN = 64
_DTSIZE = {F32: 4, BF16: 2}


class Sched:
    def __init__(self, nc, sb_bytes, ps_bytes=16384):
        self.nc = nc
        self.es = contextlib.ExitStack()
        self.eng = {"pe": nc.tensor, "act": nc.scalar, "dve": nc.vector,
                    "pool": nc.gpsimd, "sp": nc.sync}
        names = ["pe", "act", "dve", "pool"] + [f"ln{i}" for i in range(NLANES)]
        self.names = names
        self.idx = {n: i for i, n in enumerate(names)}
        self.NE = len(names)
        self.sems = [self.es.enter_context(nc.semaphore("s_" + n)) for n in names]
        self.cnt = np.zeros(self.NE, np.int64)
        self.seen = {e: np.zeros(self.NE, np.int64) for e in self.eng}
        self.snap = {}
        self.seen["pe"][self.idx["pe"]] = 1 << 60
        self.lane_rr = 0
        self.lane_rr_sw = 0
        self.sb = self.es.enter_context(nc.sbuf_tensor("arena_sb", [128, sb_bytes // 4], F32))
        self.ps = self.es.enter_context(nc.psum_tensor("arena_ps", [128, ps_bytes // 4], F32))
        self.trk = {
            "arena_sb": (np.zeros((sb_bytes // GRAN + 1, self.NE), np.int64),
                         np.zeros((sb_bytes // GRAN + 1, self.NE), np.int64)),
            "arena_ps": (np.zeros((ps_bytes // GRAN + 1, self.NE), np.int64),
                         np.zeros((ps_bytes // GRAN + 1, self.NE), np.int64)),
        }
        self.sb_top = 0
        self.sb_bytes = sb_bytes
        self.blk_cache = {}
        self.n_wait = 0
        self.n_ins = 0

    def alloc(self, nbytes, align=64):
        off = (self.sb_top + align - 1) // align * align
        assert off + nbytes <= self.sb_bytes, f"SBUF overflow {off + nbytes} > {self.sb_bytes}"
        self.sb_top = off + nbytes
        self.sb_max = max(getattr(self, 'sb_max', 0), self.sb_top)
        return off

    def view(self, off, shape, dt):
        n = int(np.prod(shape))
        nb = n * _DTSIZE[dt]
        assert off % 4 == 0 and nb % 4 == 0
        v = self.sb[:, off // 4:(off + nb) // 4]
        if dt != F32:
            v = v.bitcast(dt)
        if len(shape) == 2:
            v = v.rearrange("p (a b) -> p a b", a=shape[0])
        elif len(shape) == 3:
            v = v.rearrange("p (a b c) -> p a b c", a=shape[0], b=shape[1])
        return v

    def new(self, shape, dt):
        n = int(np.prod(shape)) * _DTSIZE[dt]
        n = (n + 3) // 4 * 4
        off = self.alloc(n)
        return self.view(off, shape, dt)

    def psum(self, bank, cols=512, dt=F32, col0=0):
        v = self.ps[:, bank * 512 + col0: bank * 512 + col0 + cols]
        return v

    def _blocks(self, ap):
        name = ap.tensor.name
        if name not in self.trk:
            return None, None
        key = (name, ap.offset, ap.ap, ap.dtype)
        r = self.blk_cache.get(key)
        if r is None:
            es = _DTSIZE[ap.dtype]
            dims = ap.ap
            pstride = dims[0][0]
            off = (ap.offset % pstride) * es
            inner = [(s * es, c) for s, c in dims[1:]]
            starts = np.array([off], np.int64)
            run = es
            if inner:
                s_last, c_last = inner[-1]
                if s_last == es:
                    run = es * c_last
                    inner = inner[:-1]
                for s, c in inner:
                    starts = (starts[:, None] + (np.arange(c, dtype=np.int64) * s)[None, :]).ravel()
            lo = starts // GRAN
            hi = (starts + run - 1) // GRAN
            if len(starts) == 1:
                r = np.arange(lo[0], hi[0] + 1)
            else:
                r = np.unique(np.concatenate([np.arange(a, b + 1) for a, b in zip(lo, hi)]))
            self.blk_cache[key] = r
        return self.trk[name], r

    def _need(self, reads, writes):
        need = np.zeros(self.NE, np.int64)
        for ap in reads:
            t, b = self._blocks(ap)
            if t is not None:
                np.maximum(need, t[0][b].max(0), out=need)
        for ap in writes:
            t, b = self._blocks(ap)
            if t is not None:
                np.maximum(need, t[0][b].max(0), out=need)
                np.maximum(need, t[1][b].max(0), out=need)
        return need

    def _do_waits(self, e, need):
        seen = self.seen[e]
        eng = self.eng[e]
        for j in np.argsort(-(need - seen)):
            if need[j] > seen[j]:
                eng.wait_ge(self.sems[j], int(need[j]))
                self.n_wait += 1
                seen[j] = need[j]
                sn = self.snap.get((int(j), int(need[j])))
                if sn is not None:
                    np.maximum(seen, sn, out=seen)

    def _record(self, ei, val, reads, writes):
        for ap in reads:
            t, b = self._blocks(ap)
            if t is not None:
                t[1][b, ei] = val
        for ap in writes:
            t, b = self._blocks(ap)
            if t is not None:
                t[0][b, ei] = val

    def op(self, e, fn, reads, writes, inc=True):
        ei = self.idx[e]
        need = self._need(reads, writes)
        self._do_waits(e, need)
        ins = fn()
        self.n_ins += 1
        val = int(self.cnt[ei]) + 1
        self._record(ei, val, reads, writes)
        if inc:
            ins.then_inc(self.sems[ei], 1)
            self.cnt[ei] = val
            sn = self.seen[e].copy()
            sn[ei] = val
            self.snap[(ei, val)] = sn
        return ins

    def dma(self, q, out, in_):
        if q == "pool":
            lane = 4 + NHW + self.lane_rr_sw
            self.lane_rr_sw = (self.lane_rr_sw + 1) % (NLANES - NHW)
        else:
            lane = 4 + self.lane_rr
            self.lane_rr = (self.lane_rr + 1) % NHW
        need = self._need([in_], [out])
        need[lane] = max(need[lane], self.cnt[lane])
        self._do_waits(q, need)
        val = int(self.cnt[lane]) + 16
        ins = self.eng[q].dma_start(out=out, in_=in_)
        ins.then_inc(self.sems[lane], 16)
        self.n_ins += 1
        self.cnt[lane] = val
        self._record(lane, val, [in_], [out])
        self.snap[(lane, val)] = self.seen[q].copy()
        return (lane, val)

    def wait_all_dma(self, e="sp"):
        need = np.zeros(self.NE, np.int64)
        need[4:] = self.cnt[4:]
        self._do_waits(e, need)

    def mm(self, out, lhsT, rhs, start, stop, inc=None):
        return self.op("pe", lambda: self.nc.tensor.matmul(out, lhsT, rhs, start=start, stop=stop),
                       [lhsT, rhs], [out], inc=(stop if inc is None else inc))

    def transpose(self, out, in_, ident):
        return self.op("pe", lambda: self.nc.tensor.transpose(out, in_, ident), [in_, ident], [out])

    def act(self, out, in_, func, bias=None, scale=None, accum_out=None, e="act"):
        kw = {}
        reads = [in_]
        writes = [out]
        if bias is not None:
            kw["bias"] = bias
            if not isinstance(bias, (int, float)):
                reads.append(bias)
        if scale is not None:
            kw["scale"] = scale
            if not isinstance(scale, (int, float)):
                reads.append(scale)
        if accum_out is not None:
            kw["accum_out"] = accum_out
            writes.append(accum_out)
        return self.op("act", lambda: self.nc.scalar.activation(out, in_, func, **kw), reads, writes)

    def tt(self, e, out, in0, in1, op):
        return self.op(e, lambda: self.eng[e].tensor_tensor(out, in0, in1, op), [in0, in1], [out])

    def ts(self, e, out, in0, s1, s2, op0, op1=None, accum_out=None):
        reads = [in0] + [s for s in (s1, s2) if s is not None and not isinstance(s, (int, float))]
        writes = [out] + ([accum_out] if accum_out is not None else [])
        kw = {}
        if op1 is not None:
            kw["op1"] = op1
        if accum_out is not None:
            kw["accum_out"] = accum_out
        return self.op(e, lambda: self.eng[e].tensor_scalar(out, in0, s1, s2, op0, **kw), reads, writes)

    def stt(self, e, out, in0, scalar, in1, op0, op1):
        reads = [in0, in1] + ([] if isinstance(scalar, (int, float)) else [scalar])
        return self.op(e, lambda: self.eng[e].scalar_tensor_tensor(out, in0, scalar, in1, op0, op1),
                       reads, [out])

    def copy(self, e, out, in_):
        if e == "act":
            return self.op(e, lambda: self.nc.scalar.copy(out, in_), [in_], [out])
        return self.op(e, lambda: self.eng[e].tensor_copy(out, in_), [in_], [out])

    def memset(self, e, out, val):
        return self.op(e, lambda: self.eng[e].memset(out, val), [], [out])

    def recip(self, out, in_):
        return self.op("dve", lambda: self.nc.vector.reciprocal(out, in_), [in_], [out])


import os
from concourse.bass_utils import run_bass_kernel_spmd

D = 1024
NTOK = 2816
NOUT = 2048
NCTX = 256
EPS = 1e-6
DFF = 2816
SB_BYTES = 212800
NRING = 3
DEBUG_STOP = os.environ.get("MK_STOP", "")

AB_OUT = {0: 2688, 2: 2304}
AB_IN = {0: 2816, 2: 2432}
NA_PAIRS = {1: 19, 3: 16}
NA_NCH = {1: 21, 3: 18}
FFN_N = {0: 2688, 1: 2432, 2: 2304, 3: 2048}


def _split(n, parts):
    base, rem = divmod(n, parts)
    out = []
    s = 0
    for i in range(parts):
        e = s + base + (1 if i < rem else 0)
        out.append((s, e))
        s = e
    return out


def _tiles(n, w=512):
    return [(t, min(w, n - t)) for t in range(0, n, w)]


class Prog:
    def __init__(self):
        nc = bass.Bass("TRN2", target_bir_lowering=False)
        self.nc = nc
        dt = lambda name, shape, kind="ExternalInput": nc.dram_tensor(name, list(shape), F32, kind=kind).ap()
        self.d_x = dt("xT", [8, 128, NTOK])
        self.d_c = dt("cT", [8, 128, NCTX])
        self.d_cc = dt("cc", [128, 8, 2])
        self.d_wmod = dt("wmod", [4, 12, 128, 8, 512])
        self.d_bmod = dt("bmod", [128, 4, 48, 2])
        self.d_nmix = dt("nmix", [128, 4, 8, 2])
        self.d_nffn = dt("nffn", [128, 4, 8, 2])
        self.d_nfin = dt("nfin", [128, 8])
        self.d_ident = dt("ident", [128, 128])
        self.d_winab = dt("winab", [2, 3, 128, 8, 512])
        self.d_woutab = dt("woutab", [2, 2, 128, 8, 512])
        self.d_wsT = dt("wsT", [2, 128, 4, 128])
        self.d_bsB = dt("bsB", [2, 128, 4, 128])
        self.d_lnvB = dt("lnvB", [2, 128, 512])
        self.d_wpool = dt("wpool", [2, 128, 4, 128])
        self.d_pscale = dt("pscale", [128, 2, 4])
        self.d_invc = dt("invc", [128, 3, 4, 8])
        self.d_alpha = dt("alpha", [128, 1])
        self.d_wqkv = dt("wqkv", [2, 8, 128, 8, 384])
        self.d_wona = dt("wona", [2, 2, 128, 8, 512])
        self.d_btab = dt("btab", [2, 16, 128, 15, 128])
        self.d_mtab = dt("mtab", [128, 15, 128])
        self.d_wfin = dt("wfin", [4, 22, 128, 8, 256])
        self.d_wfout = dt("wfout", [4, 22, 128, 1024])
        self.d_out = dt("out", [8, 128, NOUT], kind="ExternalOutput")

        S = Sched(nc, SB_BYTES)
        self.S = S
        self.pbc = 0
        self.ringc = 0
        self.bg = []
        self.X = S.new([8, NTOK], F32)
        self.XC = S.new([8, NCTX], F32)
        self.RING = [S.new([4096], BF16) for _ in range(NRING)]
        self.MOD = S.new([4, 48, 2], F32)
        self.A1 = S.new([4, 8, 2], F32)
        self.A2 = S.new([4, 8, 2], F32)
        self.ONES = S.new([128], BF16)
        self.IDENT = S.new([128], BF16)
        self.NFIN = S.new([8], F32)
        self.ALPHA = S.new([1], F32)
        self.INVC = S.new([3, 4, 8], F32)
        self.PSC = S.new([2, 4], F32)
        self.MTAB = S.new([15, 128], BF16)
        self.WST = S.new([4, 128], BF16)
        self.BSB = S.new([512], F32)
        self.LNVB = S.new([512], F32)
        self.WPOOL = S.new([4, 128], BF16)
        self.HPREV = S.new([8, 256], BF16)
        self.SCR0 = S.sb_top

    def pb(self, cols=512):
        b = self.pbc % 7
        self.pbc += 1
        return self.S.ps[:, b * 512: b * 512 + cols]

    def pb2(self, cols):
        while (self.pbc % 7) not in (0, 2, 4):
            self.pbc += 1
        b = self.pbc % 7
        self.pbc += 2
        return self.S.ps[:, b * 512: b * 512 + cols]

    def bg_step(self, n=1):
        for _ in range(n):
            if self.bg:
                self.bg[0]()
                self.bg.pop(0)

    def bg_flush(self):
        while self.bg:
            self.bg_step()

    def ring(self, bg=False):
        if bg:
            return self.RING[NRING - 1]
        nr = NRING - 1 if self.bg else NRING
        r = self.RING[self.ringc % nr]
        self.ringc += 1
        return r

    def load_w(self, src, k, n, bg=False):
        slot = self.ring(bg)[:, 0:k * n].rearrange("p (k n) -> p k n", k=k)
        self.S.dma("pool", slot, src)
        return slot

    def norm_tmp(self, nmax=512):
        S = self.S
        self.SQ = [S.new([512], BF16) for _ in range(2)]
        self.RSA = S.new([max(nmax, 512)], F32)
        self.TMPN = [S.new([512], F32) for _ in range(2)]

    def norm_multi(self, jobs):
        S = self.S
        offs = []
        o = 0
        for (A, B, j, src, dst, n) in jobs:
            offs.append(o)
            o += n
        ntot = o
        for (A, B, j, src, dst, n), o in zip(jobs, offs):
            for (t0, w) in _tiles(n):
                ps = self.pb()[:, :w]
                for k in range(8):
                    sq = self.SQ[k % 2][:, :w]
                    S.act(sq, src[:, k, t0:t0 + w], AF.Square)
                    S.mm(ps, self.ONES, sq, k == 0, k == 7, inc=True)
                S.ts("dve", self.RSA[:, o + t0:o + t0 + w], ps, 1024.0 * EPS, None, ALU.add)
        r = self.RSA[:, 0:ntot]
        S.act(r, r, AF.Sqrt)
        S.recip(r, r)
        for (A, B, j, src, dst, n), o in zip(jobs, offs):
            for (t0, w) in _tiles(n):
                for k in range(8):
                    tmp = self.TMPN[k % 2][:, :w]
                    S.tt("dve", tmp, src[:, k, t0:t0 + w], self.RSA[:, o + t0:o + t0 + w], ALU.mult)
                    if B is None:
                        S.act(dst[:, k, t0:t0 + w], tmp, AF.Identity, scale=A[:, k, j:j + 1])
                    else:
                        S.act(dst[:, k, t0:t0 + w], tmp, AF.Identity, bias=B[:, k, j:j + 1], scale=A[:, k, j:j + 1])

    def norm_h(self, A, B, j, src, dst, n):
        self.norm_multi([(A, B, j, src, dst, n)])

    def prologue(self):
        S = self.S
        nc = self.nc
        S.sb_top = self.SCR0
        for k in range(8):
            S.dma("sp", self.X[:, k, :], self.d_x[k])
            S.dma("sp", self.XC[:, k, :], self.d_c[k])
        ccs = S.new([8, 2], F32)
        csil = S.new([8, 2], BF16)
        bmod = S.new([4, 48, 2], F32)
        nmix = S.new([4, 8, 2], F32)
        nffn = S.new([4, 8, 2], F32)
        onesf = S.new([128], F32)
        S.dma("sp", ccs, self.d_cc)
        S.dma("sp", bmod, self.d_bmod)
        S.dma("sp", nmix, self.d_nmix)
        S.dma("sp", nffn, self.d_nffn)
        S.dma("sp", self.NFIN, self.d_nfin)
        S.dma("sp", self.ALPHA, self.d_alpha)
        S.dma("sp", self.INVC, self.d_invc)
        S.dma("sp", self.PSC, self.d_pscale)
        S.dma("pool", self.IDENT, self.d_ident)
        S.dma("pool", self.MTAB, self.d_mtab)
        S.memset("dve", onesf, 1.0)
        S.copy("dve", self.ONES, onesf)
        S.act(csil, ccs, AF.Silu)
        self.p_csil, self.p_bmod, self.p_nmix, self.p_nffn = csil, bmod, nmix, nffn
        self.SCR0 = S.sb_top
        for t in self.mod_tasks(0, bg=False):
            t()
        S.ts("dve", self.NFIN, self.NFIN, 32.0, None, ALU.mult)

    def mod_tasks(self, l, bg=True):
        S = self.S
        ps = S.ps[:, 7 * 512:7 * 512 + 96]
        csil, bmod = self.p_csil, self.p_bmod

        def piece(n):
            def f():
                slot = self.load_w(self.d_wmod[l, n], 8, 512, bg=bg)
                for mi in range(4):
                    m = n * 4 + mi
                    for k in range(8):
                        S.mm(ps[:, m * 2:(m + 1) * 2], slot[:, k, mi * 128:(mi + 1) * 128], csil[:, k, :], k == 0, k == 7)
            return f

        def fin():
            S.tt("dve", self.MOD[:, l].rearrange("p a b -> p (a b)"), ps, bmod[:, l].rearrange("p a b -> p (a b)"), ALU.add)
            for (Adst, gain, c0) in ((self.A1, self.p_nmix, 8), (self.A2, self.p_nffn, 32)):
                S.stt("dve", Adst[:, l], self.MOD[:, l, c0:c0 + 8, :], 1.0, gain[:, l], ALU.add, ALU.mult)
                S.ts("dve", Adst[:, l], Adst[:, l], 32.0, None, ALU.mult)
        return [piece(n) for n in range(12)] + [fin]

    def ffn(self, l, seglists):
        S = self.S
        S.sb_top = self.SCR0
        nmax = max(sum(sg[1] for sg in segs) for segs in seglists)
        self.norm_tmp(nmax)
        H = S.new([8, nmax], BF16)
        HID = S.new([8, nmax], BF16)
        SA = [S.new([512], F32) for _ in range(2)]
        B2 = self.MOD[:, l, 24:32, :]
        G2 = self.MOD[:, l, 40:48, :]

        def prep(segs):
            off = 0
            tl = []
            jobs = []
            for (res, n, j) in segs:
                jobs.append((self.A2[:, l], B2, j, res, H[:, :, off:off + n], n))
                for (t0, w) in _tiles(n):
                    tl.append((off + t0, w, res, t0, j))
                off += n
            self.norm_multi(jobs)
            return tl

        it = 0
        blocks = ((0, 8), (8, 15), (15, 22))
        tl_next = prep(seglists[0])
        for si in range(len(seglists)):
            tl = tl_next
            for bi, (j0, j1) in enumerate(blocks):
                for jh in range(j0, j1):
                    slot = self.load_w(self.d_wfin[l, jh], 8, 256)
                    jl = jh - j0
                    for (c0, w, res, t0, j) in tl:
                        pa = self.pb()[:, :w]
                        pg = self.pb()[:, :w]
                        for k in range(8):
                            S.mm(pa, slot[:, k, 0:128], H[:, k, c0:c0 + w], k == 0, k == 7)
                        for k in range(8):
                            S.mm(pg, slot[:, k, 128:256], H[:, k, c0:c0 + w], k == 0, k == 7)
                        sa = SA[it % 2][:, :w]
                        it += 1
                        S.act(sa, pa, AF.Silu)
                        S.tt("dve", HID[:, jl, c0:c0 + w], sa, pg, ALU.mult)
                if bi == len(blocks) - 1 and si + 1 < len(seglists):
                    tl_next = prep(seglists[si + 1])
                nb = j1 - j0
                wo = []
                for q0 in range(0, nb, 4):
                    qn = min(4, nb - q0)
                    slot = self.ring()[:, 0:qn * 1024].rearrange("p (k n) -> p k n", k=qn)
                    S.dma("pool", slot, self.d_wfout[l, j0 + q0:j0 + q0 + qn].rearrange("j p n -> p j n"))
                    for qi in range(qn):
                        wo.append(slot[:, qi, :])
                for m in range(8):
                    for (c0, w, res, t0, j) in tl:
                        py = self.pb()[:, :w]
                        for jl in range(nb):
                            S.mm(py, wo[jl][:, m * 128:(m + 1) * 128], HID[:, jl, c0:c0 + w], jl == 0, jl == nb - 1)
                        rr = res[:, m, t0:t0 + w]
                        S.stt("dve", rr, py, G2[:, m, j:j + 1], rr, ALU.mult, ALU.add)

    def ab_tables(self, e):
        S = self.S
        S.dma("pool", self.WST, self.d_wsT[e])
        S.dma("sp", self.BSB.rearrange("p (a b) -> p a b", a=4), self.d_bsB[e])
        S.dma("sp", self.LNVB, self.d_lnvB[e])
        S.dma("pool", self.WPOOL, self.d_wpool[e])

    def ab_mixer(self, l, e, Xb, c0, c1, hs, he, j, fix_head, fix_tail, nh_max):
        S = self.S
        nc = self.nc
        S.sb_top = self.SCR0
        nh = he - hs
        n = c1 - c0
        o = c0 - hs
        H = S.new([8, nh_max], BF16)[:, :, 0:nh]
        B1 = self.MOD[:, l, 0:8, :]
        G1 = self.MOD[:, l, 16:24, :]
        self.norm_tmp(nh_max)
        if o > 0:
            S.copy("act", H[:, :, 0:o], self.HPREV[:, :, 256 - o:256])
        self.norm_h(self.A1[:, l], B1, j, Xb[:, :, c0:he], H[:, :, o:nh], nh - o)
        if j == 0:
            S.copy("act", self.HPREV[:, :, 128:256], H[:, :, o + n - 128:o + n])
        top = S.sb_top
        yield
        S.sb_top = top
        YA = S.new([4, n], BF16)
        YB = S.new([4, n], BF16)
        PB = [S.new([nh + 16], F32) for _ in range(2)]
        Aa = S.new([nh + 16], F32)
        Ab = S.new([nh + 16], F32)
        DF = S.new([n], BF16)
        VT = [S.new([512], F32) for _ in range(2)]
        CEN = [S.new([512], F32) for _ in range(2)]
        VH = [S.new([512], BF16) for _ in range(2)]
        SQJ = S.new([512], F32)
        SM = [S.new([4], F32) for _ in range(2)]
        T8 = S.new([8], F32)
        slot_p = self.load_w(self.d_winab[e, 2], 8, 512)
        slot_u = self.load_w(self.d_winab[e, 0], 8, 512)

        def u_proj(m):
            for (t0, w) in _tiles(n):
                ps = self.pb()[:, :w]
                for k in range(8):
                    S.mm(ps, slot_u[:, k, m * 128:(m + 1) * 128], H[:, k, o + t0:o + t0 + w], k == 0, k == 7)
                S.act(YA[:, m, t0:t0 + w], ps, AF.Gelu_apprx_tanh)

        def p_proj(g):
            P = PB[g % 2]
            S.memset("dve", P[:, 0:8], 0.0)
            S.memset("dve", P[:, 8 + nh:16 + nh], 0.0)
            for (t0, w) in _tiles(nh):
                ps = self.pb()[:, :w]
                for k in range(8):
                    S.mm(ps, slot_p[:, k, g * 128:(g + 1) * 128], H[:, k, t0:t0 + w], k == 0, k == 7)
                S.copy("act", P[:, 8 + t0:8 + t0 + w], ps)

        def pooling(g):
            P = PB[g % 2]
            wd = 2 ** (g + 1)
            half = wd // 2
            cur = P
            length = nh + 16
            step = 1
            bufs = [Aa, Ab]
            bi = 0
            while step < wd:
                nxt = bufs[bi]
                bi ^= 1
                L2 = length - step
                S.tt("dve", nxt[:, 0:L2], cur[:, 0:L2], cur[:, step:step + L2], ALU.add)
                cur = nxt
                length = L2
                step *= 2
            i0 = o + 8
            Dd = bufs[bi][:, 0:n]
            fw_ = cur[:, i0 - half:i0 - half + n]
            rv_ = cur[:, i0 - half + 1:i0 - half + 1 + n]
            S.tt("dve", Dd, fw_, rv_, ALU.subtract)
            S.stt("dve", Dd, Dd, self.ALPHA[:, 0:1], rv_, ALU.mult, ALU.add)
            S.stt("dve", DF, Dd, 1.0 / wd, P[:, i0:i0 + n], ALU.mult, ALU.subtract)
            if fix_head is not None:
                S.tt("dve", T8, Dd[:, 0:8], self.INVC[:, fix_head, g, :], ALU.mult)
                S.tt("dve", DF[:, 0:8], T8, P[:, i0:i0 + 8], ALU.subtract)
            if fix_tail is not None:
                S.tt("dve", T8, Dd[:, n - 8:n], self.INVC[:, fix_tail, g, :], ALU.mult)
                S.tt("dve", DF[:, n - 8:n], T8, P[:, i0 + n - 8:i0 + n], ALU.subtract)

        def yb_proj(g):
            for (t0, w) in _tiles(n):
                ps = self.pb()[:, :w]
                S.mm(ps, self.WPOOL[:, g, :], DF[:, t0:t0 + w], True, True)
                S.act(YB[:, g, t0:t0 + w], ps, AF.Identity, scale=self.PSC[:, e, g:g + 1])

        p_proj(0)
        for g in range(4):
            if j == 0:
                self.bg_step()
            u_proj(g)
            if g < 3:
                p_proj(g + 1)
            pooling(g)
            yb_proj(g)

        slot_v = self.load_w(self.d_winab[e, 1], 8, 512)
        nchunk = n // 128

        def v_a(ci):
            tc = o + ci * 128
            ps = self.pb()
            for k in range(8):
                S.mm(ps, H[:, k, tc:tc + 128], slot_v[:, k, :], k == 0, k == 7)
            S.act(VT[ci % 2], ps, AF.Gelu_apprx_tanh)

        def v_b1(ci):
            vt = VT[ci % 2]
            cen = CEN[ci % 2]
            vh = VH[ci % 2]
            sm = SM[ci % 2]
            S.op("dve", lambda: nc.vector.reduce_sum(sm[:, 0:1], vt, AX.X), [vt], [sm[:, 0:1]])
            S.ts("dve", sm[:, 1:2], sm[:, 0:1], -1.0 / 512.0, None, ALU.mult)
            S.ts("dve", cen, vt, sm[:, 1:2], None, ALU.add)
            S.act(SQJ, cen, AF.Square)
            S.op("dve", lambda: nc.vector.reduce_sum(sm[:, 2:3], SQJ, AX.X), [SQJ], [sm[:, 2:3]])
            S.ts("dve", sm[:, 3:4], sm[:, 2:3], 1.0 / 512.0, EPS, ALU.mult, ALU.add)
            S.act(sm[:, 3:4], sm[:, 3:4], AF.Sqrt)
            S.recip(sm[:, 3:4], sm[:, 3:4])
            S.stt("dve", vh, cen, sm[:, 3:4], self.LNVB, ALU.mult, ALU.mult)

        def v_b2(ci):
            cen = CEN[ci % 2]
            vh = VH[ci % 2]
            psg = self.pb()
            for g in range(4):
                S.mm(psg[:, g * 128:(g + 1) * 128], vh[:, g * 128:(g + 1) * 128], self.WST[:, g, :], True, True)
            S.tt("dve", cen, psg, self.BSB, ALU.add)
            ya = YA[:, :, ci * 128:(ci + 1) * 128]
            S.tt("dve", ya, cen.rearrange("p (a b) -> p a b", a=4), ya, ALU.mult)

        for i in range(nchunk + 2):
            if i < nchunk:
                v_a(i)
            if 1 <= i <= nchunk:
                v_b1(i - 1)
            if i >= 2:
                v_b2(i - 2)
        yield
        for hf in range(2):
            slot = self.load_w(self.d_woutab[e, hf], 8, 512)
            for mi in range(4):
                m = hf * 4 + mi
                for (t0, w) in _tiles(n):
                    ps = self.pb()[:, :w]
                    for k in range(8):
                        rhs = (YA if k < 4 else YB)[:, k % 4, t0:t0 + w]
                        S.mm(ps, slot[:, k, mi * 128:(mi + 1) * 128], rhs, k == 0, k == 7)
                    rr = Xb[:, m, c0 + t0:c0 + t0 + w]
                    S.stt("dve", rr, ps, G1[:, m, j:j + 1], rr, ALU.mult, ALU.add)

    def na_mixer(self, l, o_, a0, a1, do_ctx_q, npair_max, nkc_max):
        S = self.S
        nc = self.nc
        S.sb_top = self.SCR0
        NCH = NA_NCH[l]
        cs = lambda a: min(max(a - 2, 0), NCH - 5)
        k0 = cs(a0)
        k1 = cs(a1 - 1) + 5
        nkc = k1 - k0
        nk = nkc * 128
        npair = a1 - a0
        nq = npair * 128
        qoff = (a0 - k0) * 128
        H = S.new([8, nkc_max * 128], BF16)[:, :, 0:nk]
        HC = S.new([8, NCTX], BF16)
        OT = S.new([8, npair_max * 128], BF16)[:, :, 0:nq]
        OTC = S.new([8, NCTX], BF16)
        QT = S.new([npair_max * 128], BF16)
        KT = S.new([nkc_max * 128], BF16)
        V1 = S.new([nkc_max, 2, 65], BF16)
        QC = S.new([NCTX], BF16)
        KC = S.new([NCTX], BF16)
        VC = S.new([2, 2, 65], BF16)
        B1 = self.MOD[:, l, 0:8, :]
        G1 = self.MOD[:, l, 16:24, :]
        mark = S.sb_top
        self.norm_tmp(nkc_max * 128 + NCTX)
        lh = (a0 - k0) * 128
        if lh > 0:
            S.copy("act", H[:, :, 0:lh], self.HPREV[:, :, 256 - lh:256])
        self.norm_multi([(self.A1[:, l], B1, 0, self.X[:, :, a0 * 128:k1 * 128], H[:, :, lh:nk], nk - lh),
                         (self.A1[:, l], B1, 1, self.XC, HC, NCTX)])
        S.copy("act", self.HPREV, H[:, :, (a1 - 2 - k0) * 128:(a1 - k0) * 128])
        yield
        S.sb_top = mark
        OTOK = S.new([npair, 128], BF16)
        OTOKC = S.new([2, 128], BF16)
        PT = [S.new([7, 128], BF16) for _ in range(3)]
        tbase = 0 if a0 < 2 else 2
        ntyp = 3 - tbase
        E = [S.new([5 * ntyp, 128], BF16) for _ in range(2)]
        VTF = S.new([nk], BF16)
        VCF = S.new([NCTX], BF16)
        RINV = [S.new([1], F32) for _ in range(3)]
        for c in range(nkc):
            S.memset("dve", V1[:, c, :, 64:65], 1.0)
        for c in range(2):
            S.memset("dve", VC[:, c, :, 64:65], 1.0)
        it = 0
        for hp in range(8):
            self.bg_step()
            slot = self.load_w(self.d_wqkv[o_, hp], 8, 384)
            for (t0, w) in _tiles(nq):
                ps = self.pb()[:, :w]
                for k in range(8):
                    S.mm(ps, slot[:, k, 0:128], H[:, k, qoff + t0:qoff + t0 + w], k == 0, k == 7)
                S.copy("act", QT[:, t0:t0 + w], ps)
            for (t0, w) in _tiles(nk):
                ps = self.pb()[:, :w]
                for k in range(8):
                    S.mm(ps, slot[:, k, 128:256], H[:, k, t0:t0 + w], k == 0, k == 7)
                S.copy("act", KT[:, t0:t0 + w], ps)
            for (t0, w) in _tiles(nk):
                ps = self.pb()[:, :w]
                for k in range(8):
                    S.mm(ps, slot[:, k, 256:384], H[:, k, t0:t0 + w], k == 0, k == 7)
                S.copy("act", VTF[:, t0:t0 + w], ps)
            for c in range(nkc):
                pst = self.pb()[:, 0:64].bitcast(BF16)
                S.transpose(pst, VTF[:, c * 128:(c + 1) * 128], self.IDENT)
                S.copy("dve", V1[:, c, :, 0:64], pst.rearrange("p (a b) -> p a b", a=2))
            ps = self.pb()[:, :NCTX]
            for k in range(8):
                S.mm(ps, slot[:, k, 128:256], HC[:, k, :], k == 0, k == 7)
            S.copy("act", KC, ps)
            ps = self.pb()[:, :NCTX]
            for k in range(8):
                S.mm(ps, slot[:, k, 256:384], HC[:, k, :], k == 0, k == 7)
            S.copy("act", VCF, ps)
            for c in range(2):
                pst = self.pb()[:, 0:64].bitcast(BF16)
                S.transpose(pst, VCF[:, c * 128:(c + 1) * 128], self.IDENT)
                S.copy("dve", VC[:, c, :, 0:64], pst.rearrange("p (a b) -> p a b", a=2))
            if do_ctx_q:
                ps = self.pb()[:, :NCTX]
                for k in range(8):
                    S.mm(ps, slot[:, k, 0:128], HC[:, k, :], k == 0, k == 7)
                S.copy("act", QC, ps)
            for hh in range(2):
                S.dma("pool", E[hh], self.d_btab[o_, hp * 2 + hh][:, tbase * 5:15, :])
                S.act(E[hh], E[hh], AF.Exp)
                S.tt("dve", E[hh], E[hh], self.MTAB[:, tbase * 5:15, :], ALU.mult)
            blocks = [("x", a) for a in range(a0, a1)]
            if do_ctx_q:
                blocks += [("c", 0), ("c", 1)]
            its = [(kind, a, hh) for (kind, a) in blocks for hh in range(2)]
            LA = 2

            def stage1(i):
                kind, a, hh = its[i]
                hsl = slice(hh * 64, hh * 64 + 64)
                pt = PT[i % 3]
                ptf = pt.rearrange("p a b -> p (a b)")
                if kind == "x":
                    typ = min(a, 2)
                    c_s = cs(a) - k0
                    q = QT[hsl, (a - a0) * 128:(a - a0 + 1) * 128]
                    pss = self.pb2(896)
                    for jj in range(5):
                        S.mm(pss[:, jj * 128:(jj + 1) * 128], KT[hsl, (c_s + jj) * 128:(c_s + jj + 1) * 128], q, True, True)
                    for jc in range(2):
                        S.mm(pss[:, (5 + jc) * 128:(6 + jc) * 128], KC[hsl, jc * 128:(jc + 1) * 128], q, True, True)
                    S.act(ptf[:, 0:512], pss[:, 0:512], AF.Exp, scale=0.125)
                    S.act(ptf[:, 512:896], pss[:, 512:896], AF.Exp, scale=0.125)
                    S.tt("dve", pt[:, 0:5, :], pt[:, 0:5, :], E[hh][:, (typ - tbase) * 5:(typ - tbase) * 5 + 5, :], ALU.mult)
                else:
                    q = QC[hsl, a * 128:(a + 1) * 128]
                    pss = self.pb()[:, 0:256]
                    for jc in range(2):
                        S.mm(pss[:, jc * 128:(jc + 1) * 128], KC[hsl, jc * 128:(jc + 1) * 128], q, True, True)
                    S.act(ptf[:, 0:256], pss, AF.Exp, scale=0.125)

            def stage2(i):
                kind, a, hh = its[i]
                pt = PT[i % 3]
                rinv = RINV[i % 3]
                pso = self.pb()[:, 0:65]
                if kind == "x":
                    c_s = cs(a) - k0
                    for jj in range(5):
                        S.mm(pso, pt[:, jj, :], V1[:, c_s + jj, hh, :], jj == 0, False)
                    for jc in range(2):
                        S.mm(pso, pt[:, 5 + jc, :], VC[:, jc, hh, :], False, jc == 1)
                    dst = OTOK[:, a - a0, hh * 64:hh * 64 + 64]
                else:
                    for jc in range(2):
                        S.mm(pso, pt[:, jc, :], VC[:, jc, hh, :], jc == 0, jc == 1)
                    dst = OTOKC[:, a, hh * 64:hh * 64 + 64]
                S.recip(rinv, pso[:, 64:65])
                S.ts("dve", dst, pso[:, 0:64], rinv[:, 0:1], None, ALU.mult)

            for i in range(len(its) + LA):
                if i < len(its):
                    stage1(i)
                if i >= LA:
                    stage2(i - LA)
            for (kind, a) in blocks:
                pst = self.pb()[:, 0:64].bitcast(BF16)
                if kind == "x":
                    S.transpose(pst, OTOK[:, a - a0, :], self.IDENT)
                    S.copy("dve", OT[:, hp, (a - a0) * 128:(a - a0 + 1) * 128], pst)
                else:
                    S.transpose(pst, OTOKC[:, a, :], self.IDENT)
                    S.copy("dve", OTC[:, hp, a * 128:(a + 1) * 128], pst)
        yield
        q0 = a0 * 128
        for hf in range(2):
            slot = self.load_w(self.d_wona[o_, hf], 8, 512)
            for mi in range(4):
                m = hf * 4 + mi
                for (t0, w) in _tiles(nq):
                    ps = self.pb()[:, :w]
                    for k in range(8):
                        S.mm(ps, slot[:, k, mi * 128:(mi + 1) * 128], OT[:, k, t0:t0 + w], k == 0, k == 7)
                    rr = self.X[:, m, q0 + t0:q0 + t0 + w]
                    S.stt("dve", rr, ps, G1[:, m, 0:1], rr, ALU.mult, ALU.add)
                if do_ctx_q:
                    ps = self.pb()[:, :NCTX]
                    for k in range(8):
                        S.mm(ps, slot[:, k, mi * 128:(mi + 1) * 128], OTC[:, k, :], k == 0, k == 7)
                    rr = self.XC[:, m, :]
                    S.stt("dve", rr, ps, G1[:, m, 1:2], rr, ALU.mult, ALU.add)

    def run_pipelined(self, gens):
        next(gens[0])
        for i, g in enumerate(gens):
            next(g)
            if i + 1 < len(gens):
                next(gens[i + 1])
            for _ in g:
                pass

    def write_out(self, final):
        S = self.S
        S.sb_top = self.SCR0
        self.norm_tmp()
        if final:
            ST = [S.new([8, 512], F32) for _ in range(2)]
            for i, (t0, w) in enumerate(_tiles(NOUT)):
                st = ST[i % 2]
                self.norm_h_f32(self.X[:, :, t0:t0 + w], st, w)
                for k in range(8):
                    S.dma("sp", self.d_out[k][:, t0:t0 + w], st[:, k, :w])
        else:
            for k in range(8):
                S.dma("sp", self.d_out[k], self.X[:, k, 0:NOUT])
        S.wait_all_dma("sp")

    def norm_h_f32(self, src, dst, w):
        S = self.S
        ps = self.pb()[:, :w]
        for k in range(8):
            sq = self.SQ[k % 2][:, :w]
            S.act(sq, src[:, k, :], AF.Square)
            S.mm(ps, self.ONES, sq, k == 0, k == 7, inc=True)
        r = self.RSA[:, :w]
        S.ts("dve", r, ps, 1024.0 * EPS, None, ALU.add)
        S.act(r, r, AF.Sqrt)
        S.recip(r, r)
        for k in range(8):
            tmp = self.TMPN[k % 2][:, :w]
            S.tt("dve", tmp, src[:, k, :], r, ALU.mult)
            S.act(dst[:, k, :w], tmp, AF.Identity, scale=self.NFIN[:, k:k + 1])

    def build(self, stop=""):
        self.prologue()
        for l in range(4):
            if stop == "p":
                break
            last = l == 3
            if not last:
                self.bg = self.mod_tasks(l + 1)
            if l % 2 == 0:
                e = l // 2
                self.ab_tables(e)
                nout = AB_OUT[l]
                nin = AB_IN[l]
                calls = []
                for (s_, t_) in _split(nout // 128, 4):
                    c0, c1 = s_ * 128, t_ * 128
                    hs = max(c0 - 128, 0)
                    he = min(c1 + 128, nin)
                    calls.append((l, e, self.X, c0, c1, hs, he, 0, 0 if c0 == 0 else None, None))
                calls.append((l, e, self.XC, 0, NCTX, 0, NCTX, 1, 1, 2))
                nh_max = max(c[6] - c[5] for c in calls)
                self.run_pipelined([self.ab_mixer(*c, nh_max) for c in calls])
            else:
                o_ = l // 2
                sp = _split(NA_PAIRS[l], 3)
                NCH = NA_NCH[l]
                cs = lambda a: min(max(a - 2, 0), NCH - 5)
                npm = max(a1 - a0 for (a0, a1) in sp)
                nkm = max(cs(a1 - 1) + 5 - cs(a0) for (a0, a1) in sp)
                self.run_pipelined([self.na_mixer(l, o_, a0, a1, (not last) and i == len(sp) - 1, npm, nkm)
                                    for i, (a0, a1) in enumerate(sp)])
            self.bg_flush()
            if stop == "m%d" % l:
                break
            n = FFN_N[l]
            h1 = (n // 128 + 1) // 2 * 128
            segs = [(self.X[:, :, h1:n], n - h1, 0)]
            if not last:
                segs.append((self.XC, NCTX, 1))
            self.ffn(l, [[(self.X[:, :, 0:h1], h1, 0)], segs])
            if stop == "f%d" % l:
                break
        self.write_out(final=(stop == ""))
        return self.nc


def _na_tables():
    out = {}
    for rev in (0, 1):
        dr_i = np.zeros((128, 15, 128), np.int64)
        dc_i = np.zeros((128, 15, 128), np.int64)
        msk = np.zeros((128, 15, 128), np.float32)
        kp = np.arange(128)
        qi = np.arange(128)
        kr2, kcl = kp // 64, kp % 64
        qr2, qcl = qi // 64, qi % 64
        for typ, a in ((0, 0), (1, 1), (2, 4)):
            cs = max(a - 2, 0)
            for jj in range(5):
                krl = 2 * (cs + jj) + kr2[:, None]
                qrl = 2 * a + qr2[None, :]
                kc_ = kcl[:, None] + 0 * qrl
                qc_ = qcl[None, :] + 0 * krl
                if rev:
                    r, c, kr, kc = 63 - qrl, 63 - qc_, 63 - krl, 63 - kc_
                else:
                    r, c, kr, kc = qrl, qc_, krl, kc_
                r = r + 0 * kr
                kr = kr + 0 * r
                rs = np.clip(r - 4, 0, 56)
                cst = np.clip(c - 8, 0, 48)
                ok = (kr >= rs) & (kr < rs + 8) & (kc >= cst) & (kc < cst + 16)
                dr = np.where(ok, kr - r + 7, 0)
                dc = np.clip(kc - c, -15, 15) + 15
                dr_i[:, typ * 5 + jj, :] = dr
                dc_i[:, typ * 5 + jj, :] = dc
                msk[:, typ * 5 + jj, :] = ok
        out[rev] = (dr_i, dc_i, msk)
    return out


def _invc(rev):
    t = np.zeros((128, 3, 4, 8), np.float32)
    for g, wd in enumerate((2, 4, 8, 16)):
        half = wd // 2
        for which, L, locs in ((0, 4096, range(8)), (1, 256, range(8)), (2, 256, range(248, 256))):
            for i, tl in enumerate(locs):
                tg = (L - 1 - tl) if rev else tl
                cnt = min(tg + half, L) - max(tg - half, 0)
                t[:, which, g, i] = np.float32(1.0) / np.float32(cnt)
    return t


_CACHE = {}


def kernel(x, c, ctx, c_ctx, w_mod, b_mod, norm_mix, norm_ffn, w_in_ab, ln_v, w_spatial,
           b_spatial, w_pool, pool_scale, w_out_ab, w_qkv, rpb, w_out_na, w_ffn_in,
           w_ffn_out, norm_final):
    f = lambda a: np.ascontiguousarray(np.asarray(a, dtype=np.float32))
    x, c, ctx, c_ctx = f(x), f(c), f(ctx), f(c_ctx)
    w_mod, b_mod, norm_mix, norm_ffn = f(w_mod), f(b_mod), f(norm_mix), f(norm_ffn)
    w_in_ab, ln_v, w_spatial, b_spatial = f(w_in_ab), f(ln_v), f(w_spatial), f(b_spatial)
    w_pool, pool_scale, w_out_ab, w_qkv = f(w_pool), f(pool_scale), f(w_out_ab), f(w_qkv)
    rpb, w_out_na, w_ffn_in, w_ffn_out, norm_final = f(rpb), f(w_out_na), f(w_ffn_in), f(w_ffn_out), f(norm_final)

    stop = DEBUG_STOP
    if "nc" not in _CACHE:
        _CACHE["nc"] = Prog().build(stop)
    nc = _CACHE["nc"]

    kc = lambda w: w.reshape(8, 128, -1)
    shared = {}
    shared["wmod"] = f(w_mod.reshape(4, 8, 128, 12, 512).transpose(0, 3, 2, 1, 4))
    shared["bmod"] = f(np.repeat(b_mod.reshape(4, 48, 128).transpose(2, 0, 1)[..., None], 2, axis=3))
    shared["nmix"] = f(np.repeat(norm_mix.reshape(4, 8, 128).transpose(2, 0, 1)[..., None], 2, axis=3))
    shared["nffn"] = f(np.repeat(norm_ffn.reshape(4, 8, 128).transpose(2, 0, 1)[..., None], 2, axis=3))
    shared["nfin"] = f(norm_final.reshape(8, 128).T)
    shared["ident"] = np.eye(128, dtype=np.float32)
    shared["winab"] = f(w_in_ab.reshape(2, 8, 128, 3, 512).transpose(0, 3, 2, 1, 4))
    shared["woutab"] = f(w_out_ab.reshape(2, 8, 128, 2, 512).transpose(0, 3, 2, 1, 4))
    shared["lnvB"] = f(np.broadcast_to(ln_v[:, None, :], (2, 128, 512)))
    shared["wpool"] = f(w_pool.transpose(0, 2, 1, 3))
    shared["pscale"] = f(pool_scale.reshape(2, 4, 128).transpose(2, 0, 1))
    shared["wqkv"] = f(w_qkv.reshape(2, 8, 128, 3, 8, 128).transpose(0, 4, 2, 1, 3, 5).reshape(2, 8, 128, 8, 384))
    shared["wona"] = f(w_out_na.reshape(2, 8, 128, 2, 512).transpose(0, 3, 2, 1, 4))
    shared["wfin"] = f(w_ffn_in.reshape(4, 8, 128, 2, 22, 128).transpose(0, 4, 2, 1, 3, 5).reshape(4, 22, 128, 8, 256))
    shared["wfout"] = f(w_ffn_out.reshape(4, 22, 128, 1024))
    nat = _na_tables()
    per_rev = {}
    for rev in (0, 1):
        d = {}
        ws = w_spatial[:, :, ::-1, ::-1] if rev else w_spatial
        d["wsT"] = f(ws.transpose(0, 3, 1, 2))
        bs = b_spatial[:, :, ::-1] if rev else b_spatial
        d["bsB"] = f(np.broadcast_to(bs[:, None, :, :], (2, 128, 4, 128)))
        d["invc"] = _invc(rev)
        d["alpha"] = np.full((128, 1), 0.0 if rev else 1.0, np.float32)
        dr_i, dc_i, msk = nat[rev]
        d["btab"] = f(rpb[:, :, dr_i, dc_i])
        d["mtab"] = msk
        per_rev[rev] = d
    in_maps = []
    for core in range(8):
        b, rev = core // 2, core % 2
        m = dict(shared)
        m.update(per_rev[rev])
        if rev:
            xs = x[b, ::-1][:NTOK]
            cs_ = ctx[b, ::-1]
        else:
            xs = x[b, :NTOK]
            cs_ = ctx[b]
        m["xT"] = f(xs.T.reshape(8, 128, NTOK))
        m["cT"] = f(cs_.T.reshape(8, 128, NCTX))
        m["cc"] = f(np.stack([c[b], c_ctx], axis=1).reshape(8, 128, 2).transpose(1, 0, 2))
        in_maps.append(m)
    res = run_bass_kernel_spmd(nc, in_maps, core_ids=list(range(8)))
    out = np.zeros((4, 4096, D), np.float32)
    for core in range(8):
        b, rev = core // 2, core % 2
        o = np.asarray(res.results[core]["out"], dtype=np.float32).reshape(D, NOUT).T
        if rev:
            out[b, 2048:] = o[::-1]
        else:
            out[b, :2048] = o
    return out
```

```python
import contextlib
import numpy as np
import concourse.bass as bass
import concourse.mybir as mybir

F32 = mybir.dt.float32
BF16 = mybir.dt.bfloat16
AF = mybir.ActivationFunctionType
ALU = mybir.AluOpType
AX = mybir.AxisListType

NLANES = 24
NHW = 10
GRAN = 64
_DTSIZE = {F32: 4, BF16: 2}


class Sched:
    def __init__(self, nc, sb_bytes, ps_bytes=16384):
        self.nc = nc
        self.es = contextlib.ExitStack()
        self.eng = {"pe": nc.tensor, "act": nc.scalar, "dve": nc.vector,
                    "pool": nc.gpsimd, "sp": nc.sync}
        names = ["pe", "act", "dve", "pool"] + [f"ln{i}" for i in range(NLANES)]
        self.names = names
        self.idx = {n: i for i, n in enumerate(names)}
        self.NE = len(names)
        self.sems = [self.es.enter_context(nc.semaphore("s_" + n)) for n in names]
        self.cnt = np.zeros(self.NE, np.int64)
        self.seen = {e: np.zeros(self.NE, np.int64) for e in self.eng}
        self.snap = {}
        self.seen["pe"][self.idx["pe"]] = 1 << 60
        self.lane_rr = 0
        self.lane_rr_sw = 0
        self.sb = self.es.enter_context(nc.sbuf_tensor("arena_sb", [128, sb_bytes // 4], F32))
        self.ps = self.es.enter_context(nc.psum_tensor("arena_ps", [128, ps_bytes // 4], F32))
        self.trk = {
            "arena_sb": (np.zeros((sb_bytes // GRAN + 1, self.NE), np.int64),
                         np.zeros((sb_bytes // GRAN + 1, self.NE), np.int64)),
            "arena_ps": (np.zeros((ps_bytes // GRAN + 1, self.NE), np.int64),
                         np.zeros((ps_bytes // GRAN + 1, self.NE), np.int64)),
        }
        self.sb_top = 0
        self.sb_bytes = sb_bytes
        self.blk_cache = {}
        self.n_wait = 0
        self.n_ins = 0

    def alloc(self, nbytes, align=64):
        off = (self.sb_top + align - 1) // align * align
        assert off + nbytes <= self.sb_bytes, f"SBUF overflow {off + nbytes} > {self.sb_bytes}"
        self.sb_top = off + nbytes
        self.sb_max = max(getattr(self, 'sb_max', 0), self.sb_top)
        return off

    def view(self, off, shape, dt):
        n = int(np.prod(shape))
        nb = n * _DTSIZE[dt]
        assert off % 4 == 0 and nb % 4 == 0
        v = self.sb[:, off // 4:(off + nb) // 4]
        if dt != F32:
            v = v.bitcast(dt)
        if len(shape) == 2:
            v = v.rearrange("p (a b) -> p a b", a=shape[0])
        elif len(shape) == 3:
            v = v.rearrange("p (a b c) -> p a b c", a=shape[0], b=shape[1])
        return v

    def new(self, shape, dt):
        n = int(np.prod(shape)) * _DTSIZE[dt]
        n = (n + 3) // 4 * 4
        off = self.alloc(n)
        return self.view(off, shape, dt)

    def psum(self, bank, cols=512, dt=F32, col0=0):
        v = self.ps[:, bank * 512 + col0: bank * 512 + col0 + cols]
        return v

    def _blocks(self, ap):
        name = ap.tensor.name
        if name not in self.trk:
            return None, None
        key = (name, ap.offset, ap.ap, ap.dtype)
        r = self.blk_cache.get(key)
        if r is None:
            es = _DTSIZE[ap.dtype]
            dims = ap.ap
            pstride = dims[0][0]
            off = (ap.offset % pstride) * es
            inner = [(s * es, c) for s, c in dims[1:]]
            starts = np.array([off], np.int64)
            run = es
            if inner:
                s_last, c_last = inner[-1]
                if s_last == es:
                    run = es * c_last
                    inner = inner[:-1]
                for s, c in inner:
                    starts = (starts[:, None] + (np.arange(c, dtype=np.int64) * s)[None, :]).ravel()
            lo = starts // GRAN
            hi = (starts + run - 1) // GRAN
            if len(starts) == 1:
                r = np.arange(lo[0], hi[0] + 1)
            else:
                r = np.unique(np.concatenate([np.arange(a, b + 1) for a, b in zip(lo, hi)]))
            self.blk_cache[key] = r
        return self.trk[name], r

    def _need(self, reads, writes):
        need = np.zeros(self.NE, np.int64)
        for ap in reads:
            t, b = self._blocks(ap)
            if t is not None:
                np.maximum(need, t[0][b].max(0), out=need)
        for ap in writes:
            t, b = self._blocks(ap)
            if t is not None:
                np.maximum(need, t[0][b].max(0), out=need)
                np.maximum(need, t[1][b].max(0), out=need)
        return need

    def _do_waits(self, e, need):
        seen = self.seen[e]
        eng = self.eng[e]
        for j in np.argsort(-(need - seen)):
            if need[j] > seen[j]:
                eng.wait_ge(self.sems[j], int(need[j]))
                self.n_wait += 1
                seen[j] = need[j]
                sn = self.snap.get((int(j), int(need[j])))
                if sn is not None:
                    np.maximum(seen, sn, out=seen)

    def _record(self, ei, val, reads, writes):
        for ap in reads:
            t, b = self._blocks(ap)
            if t is not None:
                t[1][b, ei] = val
        for ap in writes:
            t, b = self._blocks(ap)
            if t is not None:
                t[0][b, ei] = val

    def op(self, e, fn, reads, writes, inc=True):
        ei = self.idx[e]
        need = self._need(reads, writes)
        self._do_waits(e, need)
        ins = fn()
        self.n_ins += 1
        val = int(self.cnt[ei]) + 1
        self._record(ei, val, reads, writes)
        if inc:
            ins.then_inc(self.sems[ei], 1)
            self.cnt[ei] = val
            sn = self.seen[e].copy()
            sn[ei] = val
            self.snap[(ei, val)] = sn
        return ins

    def dma(self, q, out, in_):
        if q == "pool":
            lane = 4 + NHW + self.lane_rr_sw
            self.lane_rr_sw = (self.lane_rr_sw + 1) % (NLANES - NHW)
        else:
            lane = 4 + self.lane_rr
            self.lane_rr = (self.lane_rr + 1) % NHW
        need = self._need([in_], [out])
        need[lane] = max(need[lane], self.cnt[lane])
        self._do_waits(q, need)
        val = int(self.cnt[lane]) + 16
        ins = self.eng[q].dma_start(out=out, in_=in_)
        ins.then_inc(self.sems[lane], 16)
        self.n_ins += 1
        self.cnt[lane] = val
        self._record(lane, val, [in_], [out])
        self.snap[(lane, val)] = self.seen[q].copy()
        return (lane, val)

    def wait_all_dma(self, e="sp"):
        need = np.zeros(self.NE, np.int64)
        need[4:] = self.cnt[4:]
        self._do_waits(e, need)

    def mm(self, out, lhsT, rhs, start, stop, inc=None):
        return self.op("pe", lambda: self.nc.tensor.matmul(out, lhsT, rhs, start=start, stop=stop),
                       [lhsT, rhs], [out], inc=(stop if inc is None else inc))

    def transpose(self, out, in_, ident):
        return self.op("pe", lambda: self.nc.tensor.transpose(out, in_, ident), [in_, ident], [out])

    def act(self, out, in_, func, bias=None, scale=None, accum_out=None, e="act"):
        kw = {}
        reads = [in_]
        writes = [out]
        if bias is not None:
            kw["bias"] = bias
            if not isinstance(bias, (int, float)):
                reads.append(bias)
        if scale is not None:
            kw["scale"] = scale
            if not isinstance(scale, (int, float)):
                reads.append(scale)
        if accum_out is not None:
            kw["accum_out"] = accum_out
            writes.append(accum_out)
        return self.op("act", lambda: self.nc.scalar.activation(out, in_, func, **kw), reads, writes)

    def tt(self, e, out, in0, in1, op):
        return self.op(e, lambda: self.eng[e].tensor_tensor(out, in0, in1, op), [in0, in1], [out])

    def ts(self, e, out, in0, s1, s2, op0, op1=None, accum_out=None):
        reads = [in0] + [s for s in (s1, s2) if s is not None and not isinstance(s, (int, float))]
        writes = [out] + ([accum_out] if accum_out is not None else [])
        kw = {}
        if op1 is not None:
            kw["op1"] = op1
        if accum_out is not None:
            kw["accum_out"] = accum_out
        return self.op(e, lambda: self.eng[e].tensor_scalar(out, in0, s1, s2, op0, **kw), reads, writes)

    def stt(self, e, out, in0, scalar, in1, op0, op1):
        reads = [in0, in1] + ([] if isinstance(scalar, (int, float)) else [scalar])
        return self.op(e, lambda: self.eng[e].scalar_tensor_tensor(out, in0, scalar, in1, op0, op1),
                       reads, [out])

    def copy(self, e, out, in_):
        if e == "act":
            return self.op(e, lambda: self.nc.scalar.copy(out, in_), [in_], [out])
        return self.op(e, lambda: self.eng[e].tensor_copy(out, in_), [in_], [out])

    def memset(self, e, out, val):
        return self.op(e, lambda: self.eng[e].memset(out, val), [], [out])

    def recip(self, out, in_):
        return self.op("dve", lambda: self.nc.vector.reciprocal(out, in_), [in_], [out])


import os
from concourse.bass_utils import run_bass_kernel_spmd

D = 1024
NTOK = 2816
NOUT = 2048
NCTX = 256
EPS = 1e-6
DFF = 2816
SB_BYTES = 212800
NRING = 3
DEBUG_STOP = os.environ.get("MK_STOP", "")

AB_OUT = {0: 2688, 2: 2304}
AB_IN = {0: 2816, 2: 2432}
NA_PAIRS = {1: 19, 3: 16}
NA_NCH = {1: 21, 3: 18}
FFN_N = {0: 2688, 1: 2432, 2: 2304, 3: 2048}


def _split(n, parts):
    base, rem = divmod(n, parts)
    out = []
    s = 0
    for i in range(parts):
        e = s + base + (1 if i < rem else 0)
        out.append((s, e))
        s = e
    return out


def _tiles(n, w=512):
    return [(t, min(w, n - t)) for t in range(0, n, w)]


class Prog:
    def __init__(self):
        nc = bass.Bass("TRN2", target_bir_lowering=False)
        self.nc = nc
        dt = lambda name, shape, kind="ExternalInput": nc.dram_tensor(name, list(shape), F32, kind=kind).ap()
        self.d_x = dt("xT", [8, 128, NTOK])
        self.d_c = dt("cT", [8, 128, NCTX])
        self.d_cc = dt("cc", [128, 8, 2])
        self.d_wmod = dt("wmod", [4, 12, 128, 8, 512])
        self.d_bmod = dt("bmod", [128, 4, 48, 2])
        self.d_nmix = dt("nmix", [128, 4, 8, 2])
        self.d_nffn = dt("nffn", [128, 4, 8, 2])
        self.d_nfin = dt("nfin", [128, 8])
        self.d_ident = dt("ident", [128, 128])
        self.d_winab = dt("winab", [2, 3, 128, 8, 512])
        self.d_woutab = dt("woutab", [2, 2, 128, 8, 512])
        self.d_wsT = dt("wsT", [2, 128, 4, 128])
        self.d_bsB = dt("bsB", [2, 128, 4, 128])
        self.d_lnvB = dt("lnvB", [2, 128, 512])
        self.d_wpool = dt("wpool", [2, 128, 4, 128])
        self.d_pscale = dt("pscale", [128, 2, 4])
        self.d_invc = dt("invc", [128, 3, 4, 8])
        self.d_alpha = dt("alpha", [128, 1])
        self.d_wqkv = dt("wqkv", [2, 8, 128, 8, 384])
        self.d_wona = dt("wona", [2, 2, 128, 8, 512])
        self.d_btab = dt("btab", [2, 16, 128, 15, 128])
        self.d_mtab = dt("mtab", [128, 15, 128])
        self.d_wfin = dt("wfin", [4, 22, 128, 8, 256])
        self.d_wfout = dt("wfout", [4, 22, 128, 1024])
        self.d_out = dt("out", [8, 128, NOUT], kind="ExternalOutput")

        S = Sched(nc, SB_BYTES)
        self.S = S
        self.pbc = 0
        self.ringc = 0
        self.bg = []
        self.deferred = []
        self.X = S.new([8, NTOK], F32)
        self.XC = S.new([8, NCTX], F32)
        self.RING = [S.new([4096], BF16) for _ in range(NRING)]
        self.MOD = S.new([4, 48, 2], F32)
        self.A1 = S.new([4, 8, 2], F32)
        self.A2 = S.new([4, 8, 2], F32)
        self.ONES = S.new([128], BF16)
        self.IDENT = S.new([128], BF16)
        self.NFIN = S.new([8], F32)
        self.ALPHA = S.new([1], F32)
        self.INVC = S.new([3, 4, 8], F32)
        self.PSC = S.new([2, 4], F32)
        self.MTAB = S.new([15, 128], BF16)
        self.WST = S.new([4, 128], BF16)
        self.BSB = S.new([512], F32)
        self.LNVB = S.new([512], F32)
        self.WPOOL = S.new([4, 128], BF16)
        self.HPREV = S.new([8, 256], BF16)
        self.SCR0 = S.sb_top

    def pb(self, cols=512):
        b = self.pbc % 7
        self.pbc += 1
        return self.S.ps[:, b * 512: b * 512 + cols]

    def pb2(self, cols):
        while (self.pbc % 7) not in (0, 2, 4):
            self.pbc += 1
        b = self.pbc % 7
        self.pbc += 2
        return self.S.ps[:, b * 512: b * 512 + cols]

    def bg_step(self, n=1):
        for _ in range(n):
            if self.bg:
                self.bg[0]()
                self.bg.pop(0)

    def bg_flush(self):
        while self.bg:
            self.bg_step()

    def ring(self, bg=False):
        if bg:
            return self.RING[NRING - 1]
        nr = NRING - 1 if self.bg else NRING
        r = self.RING[self.ringc % nr]
        self.ringc += 1
        return r

    def load_w(self, src, k, n, bg=False):
        slot = self.ring(bg)[:, 0:k * n].rearrange("p (k n) -> p k n", k=k)
        self.S.dma("pool", slot, src)
        return slot

    def norm_tmp(self, nmax=512):
        S = self.S
        self.SQ = [S.new([512], BF16) for _ in range(2)]
        self.RSA = S.new([max(nmax, 512)], F32)
        self.TMPN = [S.new([512], F32) for _ in range(2)]

    def defer_step(self, n=1):
        for _ in range(n):
            if self.deferred:
                self.deferred.pop(0)()

    def defer_flush(self):
        while self.deferred:
            self.deferred.pop(0)()

    def norm_multi(self, jobs, defer=False):
        S = self.S
        offs = []
        o = 0
        for (A, B, j, src, dst, n) in jobs:
            offs.append(o)
            o += n
        ntot = o
        for (A, B, j, src, dst, n), o in zip(jobs, offs):
            for (t0, w) in _tiles(n):
                ps = self.pb()[:, :w]
                for k in range(8):
                    sq = self.SQ[k % 2][:, :w]
                    S.act(sq, src[:, k, t0:t0 + w], AF.Square)
                    S.mm(ps, self.ONES, sq, k == 0, k == 7, inc=True)
                S.ts("dve", self.RSA[:, o + t0:o + t0 + w], ps, 1024.0 * EPS, None, ALU.add)
        r = self.RSA[:, 0:ntot]
        S.act(r, r, AF.Sqrt)
        S.recip(r, r)
        TMPN, RSA = self.TMPN, self.RSA

        def unit(A, B, j, src, dst, o, t0, w, k):
            def f():
                tmp = TMPN[k % 2][:, :w]
                S.tt("dve", tmp, src[:, k, t0:t0 + w], RSA[:, o + t0:o + t0 + w], ALU.mult)
                S.act(dst[:, k, t0:t0 + w], tmp, AF.Identity, bias=B[:, k, j:j + 1], scale=A[:, k, j:j + 1])
            return f

        for (A, B, j, src, dst, n), o in zip(jobs, offs):
            for (t0, w) in _tiles(n):
                for k in range(8):
                    u = unit(A, B, j, src, dst, o, t0, w, k)
                    if defer:
                        self.deferred.append(u)
                    else:
                        u()

    def norm_h(self, A, B, j, src, dst, n, defer=False):
        self.norm_multi([(A, B, j, src, dst, n)], defer=defer)

    def prologue(self):
        S = self.S
        nc = self.nc
        S.sb_top = self.SCR0
        for k in range(8):
            S.dma("sp", self.X[:, k, :], self.d_x[k])
            S.dma("sp", self.XC[:, k, :], self.d_c[k])
        ccs = S.new([8, 2], F32)
        csil = S.new([8, 2], BF16)
        bmod = S.new([4, 48, 2], F32)
        nmix = S.new([4, 8, 2], F32)
        nffn = S.new([4, 8, 2], F32)
        onesf = S.new([128], F32)
        S.dma("sp", ccs, self.d_cc)
        S.dma("sp", bmod, self.d_bmod)
        S.dma("sp", nmix, self.d_nmix)
        S.dma("sp", nffn, self.d_nffn)
        S.dma("sp", self.NFIN, self.d_nfin)
        S.dma("sp", self.ALPHA, self.d_alpha)
        S.dma("sp", self.INVC, self.d_invc)
        S.dma("sp", self.PSC, self.d_pscale)
        S.dma("pool", self.IDENT, self.d_ident)
        S.dma("pool", self.MTAB, self.d_mtab)
        S.memset("dve", onesf, 1.0)
        S.copy("dve", self.ONES, onesf)
        S.act(csil, ccs, AF.Silu)
        self.p_csil, self.p_bmod, self.p_nmix, self.p_nffn = csil, bmod, nmix, nffn
        self.SCR0 = S.sb_top
        for t in self.mod_tasks(0, bg=False):
            t()
        S.ts("dve", self.NFIN, self.NFIN, 32.0, None, ALU.mult)

    def mod_tasks(self, l, bg=True):
        S = self.S
        ps = S.ps[:, 7 * 512:7 * 512 + 96]
        csil, bmod = self.p_csil, self.p_bmod

        def piece(n):
            def f():
                slot = self.load_w(self.d_wmod[l, n], 8, 512, bg=bg)
                for mi in range(4):
                    m = n * 4 + mi
                    for k in range(8):
                        S.mm(ps[:, m * 2:(m + 1) * 2], slot[:, k, mi * 128:(mi + 1) * 128], csil[:, k, :], k == 0, k == 7)
            return f

        def fin():
            S.tt("dve", self.MOD[:, l].rearrange("p a b -> p (a b)"), ps, bmod[:, l].rearrange("p a b -> p (a b)"), ALU.add)
            for (Adst, gain, c0) in ((self.A1, self.p_nmix, 8), (self.A2, self.p_nffn, 32)):
                S.stt("dve", Adst[:, l], self.MOD[:, l, c0:c0 + 8, :], 1.0, gain[:, l], ALU.add, ALU.mult)
                S.ts("dve", Adst[:, l], Adst[:, l], 32.0, None, ALU.mult)
        return [piece(n) for n in range(12)] + [fin]

    def ffn(self, l, seglists):
        S = self.S
        S.sb_top = self.SCR0
        nmax = max(sum(sg[1] for sg in segs) for segs in seglists)
        self.norm_tmp(nmax)
        H = S.new([8, nmax], BF16)
        HID = S.new([8, nmax], BF16)
        SA = [S.new([512], F32) for _ in range(2)]
        B2 = self.MOD[:, l, 24:32, :]
        G2 = self.MOD[:, l, 40:48, :]

        def prep(segs, defer=False):
            off = 0
            tl = []
            jobs = []
            for (res, n, j) in segs:
                jobs.append((self.A2[:, l], B2, j, res, H[:, :, off:off + n], n))
                for (t0, w) in _tiles(n):
                    tl.append((off + t0, w, res, t0, j))
                off += n
            self.norm_multi(jobs, defer=defer)
            return tl

        it = 0
        blocks = ((0, 8), (8, 15), (15, 22))
        tl_next = prep(seglists[0])
        for si in range(len(seglists)):
            tl = tl_next
            for bi, (j0, j1) in enumerate(blocks):
                for jh in range(j0, j1):
                    slot = self.load_w(self.d_wfin[l, jh], 8, 256)
                    jl = jh - j0
                    for (c0, w, res, t0, j) in tl:
                        pa = self.pb()[:, :w]
                        pg = self.pb()[:, :w]
                        for k in range(8):
                            S.mm(pa, slot[:, k, 0:128], H[:, k, c0:c0 + w], k == 0, k == 7)
                        for k in range(8):
                            S.mm(pg, slot[:, k, 128:256], H[:, k, c0:c0 + w], k == 0, k == 7)
                        sa = SA[it % 2][:, :w]
                        it += 1
                        S.act(sa, pa, AF.Silu)
                        S.tt("dve", HID[:, jl, c0:c0 + w], sa, pg, ALU.mult)
                if bi == len(blocks) - 1 and si + 1 < len(seglists):
                    tl_next = prep(seglists[si + 1], defer=True)
                nb = j1 - j0
                wo = []
                for q0 in range(0, nb, 4):
                    qn = min(4, nb - q0)
                    slot = self.ring()[:, 0:qn * 1024].rearrange("p (k n) -> p k n", k=qn)
                    S.dma("pool", slot, self.d_wfout[l, j0 + q0:j0 + q0 + qn].rearrange("j p n -> p j n"))
                    for qi in range(qn):
                        wo.append(slot[:, qi, :])
                dstep = -(-len(self.deferred) // (8 * len(tl)))
                for m in range(8):
                    for (c0, w, res, t0, j) in tl:
                        py = self.pb()[:, :w]
                        for jl in range(nb):
                            S.mm(py, wo[jl][:, m * 128:(m + 1) * 128], HID[:, jl, c0:c0 + w], jl == 0, jl == nb - 1)
                        rr = res[:, m, t0:t0 + w]
                        S.stt("dve", rr, py, G2[:, m, j:j + 1], rr, ALU.mult, ALU.add)
                        self.defer_step(dstep)
                self.defer_flush()

    def ab_tables(self, e):
        S = self.S
        S.dma("pool", self.WST, self.d_wsT[e])
        S.dma("sp", self.BSB.rearrange("p (a b) -> p a b", a=4), self.d_bsB[e])
        S.dma("sp", self.LNVB, self.d_lnvB[e])
        S.dma("pool", self.WPOOL, self.d_wpool[e])

    def ab_mixer(self, l, e, Xb, c0, c1, hs, he, j, fix_head, fix_tail, nh_max):
        S = self.S
        nc = self.nc
        S.sb_top = self.SCR0
        nh = he - hs
        n = c1 - c0
        o = c0 - hs
        H = S.new([8, nh_max], BF16)[:, :, 0:nh]
        B1 = self.MOD[:, l, 0:8, :]
        G1 = self.MOD[:, l, 16:24, :]
        self.norm_tmp(nh_max)
        if o > 0:
            S.copy("act", H[:, :, 0:o], self.HPREV[:, :, 256 - o:256])
        self.norm_h(self.A1[:, l], B1, j, Xb[:, :, c0:he], H[:, :, o:nh], nh - o, defer=self.hoisting)
        if j == 0:
            sv = lambda: S.copy("act", self.HPREV[:, :, 128:256], H[:, :, o + n - 128:o + n])
            if self.hoisting:
                self.deferred.append(sv)
            else:
                sv()
        top = S.sb_top
        yield
        S.sb_top = top
        YA = S.new([4, n], BF16)
        YB = S.new([4, n], BF16)
        PB = [S.new([nh + 16], F32) for _ in range(2)]
        Aa = S.new([nh + 16], F32)
        Ab = S.new([nh + 16], F32)
        DF = S.new([n], BF16)
        VT = [S.new([512], F32) for _ in range(2)]
        CEN = [S.new([512], F32) for _ in range(2)]
        VH = [S.new([512], BF16) for _ in range(2)]
        SQJ = S.new([512], F32)
        SM = [S.new([4], F32) for _ in range(2)]
        T8 = S.new([8], F32)
        slot_p = self.load_w(self.d_winab[e, 2], 8, 512)
        slot_u = self.load_w(self.d_winab[e, 0], 8, 512)

        def u_proj(m):
            for (t0, w) in _tiles(n):
                ps = self.pb()[:, :w]
                for k in range(8):
                    S.mm(ps, slot_u[:, k, m * 128:(m + 1) * 128], H[:, k, o + t0:o + t0 + w], k == 0, k == 7)
                S.act(YA[:, m, t0:t0 + w], ps, AF.Gelu_apprx_tanh)

        def p_proj(g):
            P = PB[g % 2]
            S.memset("dve", P[:, 0:8], 0.0)
            S.memset("dve", P[:, 8 + nh:16 + nh], 0.0)
            for (t0, w) in _tiles(nh):
                ps = self.pb()[:, :w]
                for k in range(8):
                    S.mm(ps, slot_p[:, k, g * 128:(g + 1) * 128], H[:, k, t0:t0 + w], k == 0, k == 7)
                S.copy("act", P[:, 8 + t0:8 + t0 + w], ps)

        def pooling(g):
            P = PB[g % 2]
            wd = 2 ** (g + 1)
            half = wd // 2
            cur = P
            length = nh + 16
            step = 1
            bufs = [Aa, Ab]
            bi = 0
            while step < wd:
                nxt = bufs[bi]
                bi ^= 1
                L2 = length - step
                S.tt("dve", nxt[:, 0:L2], cur[:, 0:L2], cur[:, step:step + L2], ALU.add)
                cur = nxt
                length = L2
                step *= 2
            i0 = o + 8
            Dd = bufs[bi][:, 0:n]
            fw_ = cur[:, i0 - half:i0 - half + n]
            rv_ = cur[:, i0 - half + 1:i0 - half + 1 + n]
            S.tt("dve", Dd, fw_, rv_, ALU.subtract)
            S.stt("dve", Dd, Dd, self.ALPHA[:, 0:1], rv_, ALU.mult, ALU.add)
            S.stt("dve", DF, Dd, 1.0 / wd, P[:, i0:i0 + n], ALU.mult, ALU.subtract)
            if fix_head is not None:
                S.tt("dve", T8, Dd[:, 0:8], self.INVC[:, fix_head, g, :], ALU.mult)
                S.tt("dve", DF[:, 0:8], T8, P[:, i0:i0 + 8], ALU.subtract)
            if fix_tail is not None:
                S.tt("dve", T8, Dd[:, n - 8:n], self.INVC[:, fix_tail, g, :], ALU.mult)
                S.tt("dve", DF[:, n - 8:n], T8, P[:, i0 + n - 8:i0 + n], ALU.subtract)

        def yb_proj(g):
            for (t0, w) in _tiles(n):
                ps = self.pb()[:, :w]
                S.mm(ps, self.WPOOL[:, g, :], DF[:, t0:t0 + w], True, True)
                S.act(YB[:, g, t0:t0 + w], ps, AF.Identity, scale=self.PSC[:, e, g:g + 1])

        p_proj(0)
        for g in range(4):
            if j == 0:
                self.bg_step()
            u_proj(g)
            if g < 3:
                p_proj(g + 1)
            pooling(g)
            yb_proj(g)

        slot_v = self.load_w(self.d_winab[e, 1], 8, 512)
        nchunk = n // 128

        def v_a(ci):
            tc = o + ci * 128
            ps = self.pb()
            for k in range(8):
                S.mm(ps, H[:, k, tc:tc + 128], slot_v[:, k, :], k == 0, k == 7)
            S.act(VT[ci % 2], ps, AF.Gelu_apprx_tanh)

        def v_b1(ci):
            vt = VT[ci % 2]
            cen = CEN[ci % 2]
            vh = VH[ci % 2]
            sm = SM[ci % 2]
            S.op("dve", lambda: nc.vector.reduce_sum(sm[:, 0:1], vt, AX.X), [vt], [sm[:, 0:1]])
            S.ts("dve", sm[:, 1:2], sm[:, 0:1], -1.0 / 512.0, None, ALU.mult)
            S.ts("dve", cen, vt, sm[:, 1:2], None, ALU.add)
            S.act(SQJ, cen, AF.Square)
            S.op("dve", lambda: nc.vector.reduce_sum(sm[:, 2:3], SQJ, AX.X), [SQJ], [sm[:, 2:3]])
            S.ts("dve", sm[:, 3:4], sm[:, 2:3], 1.0 / 512.0, EPS, ALU.mult, ALU.add)
            S.act(sm[:, 3:4], sm[:, 3:4], AF.Sqrt)
            S.recip(sm[:, 3:4], sm[:, 3:4])
            S.stt("dve", vh, cen, sm[:, 3:4], self.LNVB, ALU.mult, ALU.mult)

        def v_b2(ci):
            cen = CEN[ci % 2]
            vh = VH[ci % 2]
            psg = self.pb()
            for g in range(4):
                S.mm(psg[:, g * 128:(g + 1) * 128], vh[:, g * 128:(g + 1) * 128], self.WST[:, g, :], True, True)
            S.tt("dve", cen, psg, self.BSB, ALU.add)
            ya = YA[:, :, ci * 128:(ci + 1) * 128]
            S.tt("dve", ya, cen.rearrange("p (a b) -> p a b", a=4), ya, ALU.mult)

        for i in range(nchunk + 2):
            if i < nchunk:
                v_a(i)
            if 1 <= i <= nchunk:
                v_b1(i - 1)
            if i >= 2:
                v_b2(i - 2)
        yield
        for hf in range(2):
            slot = self.load_w(self.d_woutab[e, hf], 8, 512)
            for mi in range(4):
                m = hf * 4 + mi
                for (t0, w) in _tiles(n):
                    ps = self.pb()[:, :w]
                    for k in range(8):
                        rhs = (YA if k < 4 else YB)[:, k % 4, t0:t0 + w]
                        S.mm(ps, slot[:, k, mi * 128:(mi + 1) * 128], rhs, k == 0, k == 7)
                    rr = Xb[:, m, c0 + t0:c0 + t0 + w]
                    S.stt("dve", rr, ps, G1[:, m, j:j + 1], rr, ALU.mult, ALU.add)
                    self.defer_step(-(-self.defer_total // (8 * len(_tiles(n)))))

    def na_mixer(self, l, o_, a0, a1, do_ctx_q, npair_max, nkc_max):
        S = self.S
        nc = self.nc
        S.sb_top = self.SCR0
        NCH = NA_NCH[l]
        cs = lambda a: min(max(a - 2, 0), NCH - 5)
        k0 = cs(a0)
        k1 = cs(a1 - 1) + 5
        nkc = k1 - k0
        nk = nkc * 128
        npair = a1 - a0
        nq = npair * 128
        qoff = (a0 - k0) * 128
        H = S.new([8, nkc_max * 128], BF16)[:, :, 0:nk]
        HC = S.new([8, NCTX], BF16)
        OT = S.new([8, npair_max * 128], BF16)[:, :, 0:nq]
        OTC = S.new([8, NCTX], BF16)
        QT = S.new([npair_max * 128], BF16)
        KT = S.new([nkc_max * 128], BF16)
        V1 = S.new([nkc_max, 2, 65], BF16)
        QC = S.new([NCTX], BF16)
        KC = S.new([NCTX], BF16)
        VC = S.new([2, 2, 65], BF16)
        B1 = self.MOD[:, l, 0:8, :]
        G1 = self.MOD[:, l, 16:24, :]
        mark = S.sb_top
        self.norm_tmp(nkc_max * 128 + NCTX)
        lh = (a0 - k0) * 128
        if lh > 0:
            S.copy("act", H[:, :, 0:lh], self.HPREV[:, :, 256 - lh:256])
        self.norm_multi([(self.A1[:, l], B1, 0, self.X[:, :, a0 * 128:k1 * 128], H[:, :, lh:nk], nk - lh),
                         (self.A1[:, l], B1, 1, self.XC, HC, NCTX)], defer=self.hoisting)
        sv = lambda: S.copy("act", self.HPREV, H[:, :, (a1 - 2 - k0) * 128:(a1 - k0) * 128])
        if self.hoisting:
            self.deferred.append(sv)
        else:
            sv()
        yield
        S.sb_top = mark
        OTOK = S.new([npair, 128], BF16)
        OTOKC = S.new([2, 128], BF16)
        PT = [S.new([7, 128], BF16) for _ in range(3)]
        tbase = 0 if a0 < 2 else 2
        ntyp = 3 - tbase
        E = [S.new([5 * ntyp, 128], BF16) for _ in range(2)]
        VTF = S.new([nk], BF16)
        VCF = S.new([NCTX], BF16)
        RINV = [S.new([1], F32) for _ in range(3)]
        for c in range(nkc):
            S.memset("dve", V1[:, c, :, 64:65], 1.0)
        for c in range(2):
            S.memset("dve", VC[:, c, :, 64:65], 1.0)
        it = 0
        for hp in range(8):
            self.bg_step()
            slot = self.load_w(self.d_wqkv[o_, hp], 8, 384)
            for (t0, w) in _tiles(nq):
                ps = self.pb()[:, :w]
                for k in range(8):
                    S.mm(ps, slot[:, k, 0:128], H[:, k, qoff + t0:qoff + t0 + w], k == 0, k == 7)
                S.copy("act", QT[:, t0:t0 + w], ps)
            for (t0, w) in _tiles(nk):
                ps = self.pb()[:, :w]
                for k in range(8):
                    S.mm(ps, slot[:, k, 128:256], H[:, k, t0:t0 + w], k == 0, k == 7)
                S.copy("act", KT[:, t0:t0 + w], ps)
            for (t0, w) in _tiles(nk):
                ps = self.pb()[:, :w]
                for k in range(8):
                    S.mm(ps, slot[:, k, 256:384], H[:, k, t0:t0 + w], k == 0, k == 7)
                S.copy("act", VTF[:, t0:t0 + w], ps)
            for c in range(nkc):
                pst = self.pb()[:, 0:64].bitcast(BF16)
                S.transpose(pst, VTF[:, c * 128:(c + 1) * 128], self.IDENT)
                S.copy("dve", V1[:, c, :, 0:64], pst.rearrange("p (a b) -> p a b", a=2))
            ps = self.pb()[:, :NCTX]
            for k in range(8):
                S.mm(ps, slot[:, k, 128:256], HC[:, k, :], k == 0, k == 7)
            S.copy("act", KC, ps)
            ps = self.pb()[:, :NCTX]
            for k in range(8):
                S.mm(ps, slot[:, k, 256:384], HC[:, k, :], k == 0, k == 7)
            S.copy("act", VCF, ps)
            for c in range(2):
                pst = self.pb()[:, 0:64].bitcast(BF16)
                S.transpose(pst, VCF[:, c * 128:(c + 1) * 128], self.IDENT)
                S.copy("dve", VC[:, c, :, 0:64], pst.rearrange("p (a b) -> p a b", a=2))
            if do_ctx_q:
                ps = self.pb()[:, :NCTX]
                for k in range(8):
                    S.mm(ps, slot[:, k, 0:128], HC[:, k, :], k == 0, k == 7)
                S.copy("act", QC, ps)
            for hh in range(2):
                S.dma("pool", E[hh], self.d_btab[o_, hp * 2 + hh][:, tbase * 5:15, :])
                S.act(E[hh], E[hh], AF.Exp)
                S.tt("dve", E[hh], E[hh], self.MTAB[:, tbase * 5:15, :], ALU.mult)
            blocks = [("x", a) for a in range(a0, a1)]
            if do_ctx_q:
                blocks += [("c", 0), ("c", 1)]
            its = [(kind, a, hh) for (kind, a) in blocks for hh in range(2)]
            LA = 2

            def stage1(i):
                kind, a, hh = its[i]
                hsl = slice(hh * 64, hh * 64 + 64)
                pt = PT[i % 3]
                ptf = pt.rearrange("p a b -> p (a b)")
                if kind == "x":
                    typ = min(a, 2)
                    c_s = cs(a) - k0
                    q = QT[hsl, (a - a0) * 128:(a - a0 + 1) * 128]
                    pss = self.pb2(896)
                    for jj in range(5):
                        S.mm(pss[:, jj * 128:(jj + 1) * 128], KT[hsl, (c_s + jj) * 128:(c_s + jj + 1) * 128], q, True, True)
                    for jc in range(2):
                        S.mm(pss[:, (5 + jc) * 128:(6 + jc) * 128], KC[hsl, jc * 128:(jc + 1) * 128], q, True, True)
                    S.act(ptf[:, 0:512], pss[:, 0:512], AF.Exp, scale=0.125)
                    S.act(ptf[:, 512:896], pss[:, 512:896], AF.Exp, scale=0.125)
                    S.tt("dve", pt[:, 0:5, :], pt[:, 0:5, :], E[hh][:, (typ - tbase) * 5:(typ - tbase) * 5 + 5, :], ALU.mult)
                else:
                    q = QC[hsl, a * 128:(a + 1) * 128]
                    pss = self.pb()[:, 0:256]
                    for jc in range(2):
                        S.mm(pss[:, jc * 128:(jc + 1) * 128], KC[hsl, jc * 128:(jc + 1) * 128], q, True, True)
                    S.act(ptf[:, 0:256], pss, AF.Exp, scale=0.125)

            def stage2(i):
                kind, a, hh = its[i]
                pt = PT[i % 3]
                rinv = RINV[i % 3]
                pso = self.pb()[:, 0:65]
                if kind == "x":
                    c_s = cs(a) - k0
                    for jj in range(5):
                        S.mm(pso, pt[:, jj, :], V1[:, c_s + jj, hh, :], jj == 0, False)
                    for jc in range(2):
                        S.mm(pso, pt[:, 5 + jc, :], VC[:, jc, hh, :], False, jc == 1)
                    dst = OTOK[:, a - a0, hh * 64:hh * 64 + 64]
                else:
                    for jc in range(2):
                        S.mm(pso, pt[:, jc, :], VC[:, jc, hh, :], jc == 0, jc == 1)
                    dst = OTOKC[:, a, hh * 64:hh * 64 + 64]
                S.recip(rinv, pso[:, 64:65])
                S.ts("dve", dst, pso[:, 0:64], rinv[:, 0:1], None, ALU.mult)

            for i in range(len(its) + LA):
                if i < len(its):
                    stage1(i)
                if i >= LA:
                    stage2(i - LA)
            for (kind, a) in blocks:
                pst = self.pb()[:, 0:64].bitcast(BF16)
                if kind == "x":
                    S.transpose(pst, OTOK[:, a - a0, :], self.IDENT)
                    S.copy("dve", OT[:, hp, (a - a0) * 128:(a - a0 + 1) * 128], pst)
                else:
                    S.transpose(pst, OTOKC[:, a, :], self.IDENT)
                    S.copy("dve", OTC[:, hp, a * 128:(a + 1) * 128], pst)
        yield
        q0 = a0 * 128
        for hf in range(2):
            slot = self.load_w(self.d_wona[o_, hf], 8, 512)
            for mi in range(4):
                m = hf * 4 + mi
                for (t0, w) in _tiles(nq):
                    ps = self.pb()[:, :w]
                    for k in range(8):
                        S.mm(ps, slot[:, k, mi * 128:(mi + 1) * 128], OT[:, k, t0:t0 + w], k == 0, k == 7)
                    rr = self.X[:, m, q0 + t0:q0 + t0 + w]
                    S.stt("dve", rr, ps, G1[:, m, 0:1], rr, ALU.mult, ALU.add)
                    self.defer_step(-(-self.defer_total // (8 * len(_tiles(nq)))))
                if do_ctx_q:
                    ps = self.pb()[:, :NCTX]
                    for k in range(8):
                        S.mm(ps, slot[:, k, mi * 128:(mi + 1) * 128], OTC[:, k, :], k == 0, k == 7)
                    rr = self.XC[:, m, :]
                    S.stt("dve", rr, ps, G1[:, m, 1:2], rr, ALU.mult, ALU.add)

    def run_pipelined(self, gens):
        self.hoisting = False
        self.defer_total = 0
        next(gens[0])
        for i, g in enumerate(gens):
            next(g)
            if i + 1 < len(gens):
                self.hoisting = True
                next(gens[i + 1])
                self.hoisting = False
            self.defer_total = len(self.deferred)
            for _ in g:
                pass
            self.defer_flush()

    def write_out(self, final):
        S = self.S
        S.sb_top = self.SCR0
        self.norm_tmp()
        if final:
            ST = [S.new([8, 512], F32) for _ in range(2)]
            for i, (t0, w) in enumerate(_tiles(NOUT)):
                st = ST[i % 2]
                self.norm_h_f32(self.X[:, :, t0:t0 + w], st, w)
                for k in range(8):
                    S.dma("sp", self.d_out[k][:, t0:t0 + w], st[:, k, :w])
        else:
            for k in range(8):
                S.dma("sp", self.d_out[k], self.X[:, k, 0:NOUT])
        S.wait_all_dma("sp")

    def norm_h_f32(self, src, dst, w):
        S = self.S
        ps = self.pb()[:, :w]
        for k in range(8):
            sq = self.SQ[k % 2][:, :w]
            S.act(sq, src[:, k, :], AF.Square)
            S.mm(ps, self.ONES, sq, k == 0, k == 7, inc=True)
        r = self.RSA[:, :w]
        S.ts("dve", r, ps, 1024.0 * EPS, None, ALU.add)
        S.act(r, r, AF.Sqrt)
        S.recip(r, r)
        for k in range(8):
            tmp = self.TMPN[k % 2][:, :w]
            S.tt("dve", tmp, src[:, k, :], r, ALU.mult)
            S.act(dst[:, k, :w], tmp, AF.Identity, scale=self.NFIN[:, k:k + 1])

    def build(self, stop=""):
        self.prologue()
        for l in range(4):
            if stop == "p":
                break
            last = l == 3
            if not last:
                self.bg = self.mod_tasks(l + 1)
            if l % 2 == 0:
                e = l // 2
                self.ab_tables(e)
                nout = AB_OUT[l]
                nin = AB_IN[l]
                calls = []
                for (s_, t_) in _split(nout // 128, 4):
                    c0, c1 = s_ * 128, t_ * 128
                    hs = max(c0 - 128, 0)
                    he = min(c1 + 128, nin)
                    calls.append((l, e, self.X, c0, c1, hs, he, 0, 0 if c0 == 0 else None, None))
                calls.append((l, e, self.XC, 0, NCTX, 0, NCTX, 1, 1, 2))
                nh_max = max(c[6] - c[5] for c in calls)
                self.run_pipelined([self.ab_mixer(*c, nh_max) for c in calls])
            else:
                o_ = l // 2
                sp = _split(NA_PAIRS[l], 3)
                NCH = NA_NCH[l]
                cs = lambda a: min(max(a - 2, 0), NCH - 5)
                npm = max(a1 - a0 for (a0, a1) in sp)
                nkm = max(cs(a1 - 1) + 5 - cs(a0) for (a0, a1) in sp)
                self.run_pipelined([self.na_mixer(l, o_, a0, a1, (not last) and i == len(sp) - 1, npm, nkm)
                                    for i, (a0, a1) in enumerate(sp)])
            self.bg_flush()
            if stop == "m%d" % l:
                break
            n = FFN_N[l]
            h1 = (n // 128 + 1) // 2 * 128
            segs = [(self.X[:, :, h1:n], n - h1, 0)]
            if not last:
                segs.append((self.XC, NCTX, 1))
            self.ffn(l, [[(self.X[:, :, 0:h1], h1, 0)], segs])
            if stop == "f%d" % l:
                break
        self.write_out(final=(stop == ""))
        return self.nc


def _na_tables():
    out = {}
    for rev in (0, 1):
        dr_i = np.zeros((128, 15, 128), np.int64)
        dc_i = np.zeros((128, 15, 128), np.int64)
        msk = np.zeros((128, 15, 128), np.float32)
        kp = np.arange(128)
        qi = np.arange(128)
        kr2, kcl = kp // 64, kp % 64
        qr2, qcl = qi // 64, qi % 64
        for typ, a in ((0, 0), (1, 1), (2, 4)):
            cs = max(a - 2, 0)
            for jj in range(5):
                krl = 2 * (cs + jj) + kr2[:, None]
                qrl = 2 * a + qr2[None, :]
                kc_ = kcl[:, None] + 0 * qrl
                qc_ = qcl[None, :] + 0 * krl
                if rev:
                    r, c, kr, kc = 63 - qrl, 63 - qc_, 63 - krl, 63 - kc_
                else:
                    r, c, kr, kc = qrl, qc_, krl, kc_
                r = r + 0 * kr
                kr = kr + 0 * r
                rs = np.clip(r - 4, 0, 56)
                cst = np.clip(c - 8, 0, 48)
                ok = (kr >= rs) & (kr < rs + 8) & (kc >= cst) & (kc < cst + 16)
                dr = np.where(ok, kr - r + 7, 0)
                dc = np.clip(kc - c, -15, 15) + 15
                dr_i[:, typ * 5 + jj, :] = dr
                dc_i[:, typ * 5 + jj, :] = dc
                msk[:, typ * 5 + jj, :] = ok
        out[rev] = (dr_i, dc_i, msk)
    return out


def _invc(rev):
    t = np.zeros((128, 3, 4, 8), np.float32)
    for g, wd in enumerate((2, 4, 8, 16)):
        half = wd // 2
        for which, L, locs in ((0, 4096, range(8)), (1, 256, range(8)), (2, 256, range(248, 256))):
            for i, tl in enumerate(locs):
                tg = (L - 1 - tl) if rev else tl
                cnt = min(tg + half, L) - max(tg - half, 0)
                t[:, which, g, i] = np.float32(1.0) / np.float32(cnt)
    return t


_CACHE = {}


def kernel(x, c, ctx, c_ctx, w_mod, b_mod, norm_mix, norm_ffn, w_in_ab, ln_v, w_spatial,
           b_spatial, w_pool, pool_scale, w_out_ab, w_qkv, rpb, w_out_na, w_ffn_in,
           w_ffn_out, norm_final):
    f = lambda a: np.ascontiguousarray(np.asarray(a, dtype=np.float32))
    x, c, ctx, c_ctx = f(x), f(c), f(ctx), f(c_ctx)
    w_mod, b_mod, norm_mix, norm_ffn = f(w_mod), f(b_mod), f(norm_mix), f(norm_ffn)
    w_in_ab, ln_v, w_spatial, b_spatial = f(w_in_ab), f(ln_v), f(w_spatial), f(b_spatial)
    w_pool, pool_scale, w_out_ab, w_qkv = f(w_pool), f(pool_scale), f(w_out_ab), f(w_qkv)
    rpb, w_out_na, w_ffn_in, w_ffn_out, norm_final = f(rpb), f(w_out_na), f(w_ffn_in), f(w_ffn_out), f(norm_final)

    stop = DEBUG_STOP
    if "nc" not in _CACHE:
        _CACHE["nc"] = Prog().build(stop)
    nc = _CACHE["nc"]

    kc = lambda w: w.reshape(8, 128, -1)
    shared = {}
    shared["wmod"] = f(w_mod.reshape(4, 8, 128, 12, 512).transpose(0, 3, 2, 1, 4))
    shared["bmod"] = f(np.repeat(b_mod.reshape(4, 48, 128).transpose(2, 0, 1)[..., None], 2, axis=3))
    shared["nmix"] = f(np.repeat(norm_mix.reshape(4, 8, 128).transpose(2, 0, 1)[..., None], 2, axis=3))
    shared["nffn"] = f(np.repeat(norm_ffn.reshape(4, 8, 128).transpose(2, 0, 1)[..., None], 2, axis=3))
    shared["nfin"] = f(norm_final.reshape(8, 128).T)
    shared["ident"] = np.eye(128, dtype=np.float32)
    shared["winab"] = f(w_in_ab.reshape(2, 8, 128, 3, 512).transpose(0, 3, 2, 1, 4))
    shared["woutab"] = f(w_out_ab.reshape(2, 8, 128, 2, 512).transpose(0, 3, 2, 1, 4))
    shared["lnvB"] = f(np.broadcast_to(ln_v[:, None, :], (2, 128, 512)))
    shared["wpool"] = f(w_pool.transpose(0, 2, 1, 3))
    shared["pscale"] = f(pool_scale.reshape(2, 4, 128).transpose(2, 0, 1))
    shared["wqkv"] = f(w_qkv.reshape(2, 8, 128, 3, 8, 128).transpose(0, 4, 2, 1, 3, 5).reshape(2, 8, 128, 8, 384))
    shared["wona"] = f(w_out_na.reshape(2, 8, 128, 2, 512).transpose(0, 3, 2, 1, 4))
    shared["wfin"] = f(w_ffn_in.reshape(4, 8, 128, 2, 22, 128).transpose(0, 4, 2, 1, 3, 5).reshape(4, 22, 128, 8, 256))
    shared["wfout"] = f(w_ffn_out.reshape(4, 22, 128, 1024))
    nat = _na_tables()
    per_rev = {}
    for rev in (0, 1):
        d = {}
        ws = w_spatial[:, :, ::-1, ::-1] if rev else w_spatial
        d["wsT"] = f(ws.transpose(0, 3, 1, 2))
        bs = b_spatial[:, :, ::-1] if rev else b_spatial
        d["bsB"] = f(np.broadcast_to(bs[:, None, :, :], (2, 128, 4, 128)))
        d["invc"] = _invc(rev)
        d["alpha"] = np.full((128, 1), 0.0 if rev else 1.0, np.float32)
        dr_i, dc_i, msk = nat[rev]
        d["btab"] = f(rpb[:, :, dr_i, dc_i])
        d["mtab"] = msk
        per_rev[rev] = d
    in_maps = []
    for core in range(8):
        b, rev = core // 2, core % 2
        m = dict(shared)
        m.update(per_rev[rev])
        if rev:
            xs = x[b, ::-1][:NTOK]
            cs_ = ctx[b, ::-1]
        else:
            xs = x[b, :NTOK]
            cs_ = ctx[b]
        m["xT"] = f(xs.T.reshape(8, 128, NTOK))
        m["cT"] = f(cs_.T.reshape(8, 128, NCTX))
        m["cc"] = f(np.stack([c[b], c_ctx], axis=1).reshape(8, 128, 2).transpose(1, 0, 2))
        in_maps.append(m)
    res = run_bass_kernel_spmd(nc, in_maps, core_ids=list(range(8)))
    out = np.zeros((4, 4096, D), np.float32)
    for core in range(8):
        b, rev = core // 2, core % 2
        o = np.asarray(res.results[core]["out"], dtype=np.float32).reshape(D, NOUT).T
        if rev:
            out[b, 2048:] = o[::-1]
        else:
            out[b, :2048] = o
    return out
```
